# Optimizing a Trainium2 kernel written in Bass

```python
import jax, jax.numpy as jnp
from jax import lax
import numpy as np

D_MODEL = 2048
BATCH = 4
SEQ = 4096
DEPTH = 2

EXPAND = 2
D_INNER = EXPAND * D_MODEL
CONV_K = 4
EPS = 1e-6
LRU_WIDTH = D_INNER // 2
LRU_BLOCKS = 16
LRU_BLOCK = LRU_WIDTH // LRU_BLOCKS
LRU_C = 8.0
MLSTM_WIDTH = D_INNER - LRU_WIDTH
MLSTM_HEADS = 4
MLSTM_DV = MLSTM_WIDTH // MLSTM_HEADS
MLSTM_DQK = MLSTM_DV // 2
MLSTM_CHUNK = 64
MLSTM_GATE_IN = MLSTM_HEADS * (2 * MLSTM_DQK + MLSTM_DV)
EVEN_IN = LRU_WIDTH + MLSTM_WIDTH + D_INNER
SSD_HEADDIM = 64
SSD_HEADS = D_INNER // SSD_HEADDIM
SSD_GROUPS = 8
SSD_HPG = SSD_HEADS // SSD_GROUPS
SSD_STATE = 128
SSD_CHUNK = 128
SSD_CONV_DIM = D_INNER + 2 * SSD_GROUPS * SSD_STATE
SSD_IN = D_INNER + SSD_CONV_DIM + SSD_HEADS
N_EVEN = (DEPTH + 1) // 2
N_ODD = DEPTH // 2

kernel_name = "hybrid_rglru_mlstm_ssd_trunk"


def rmsnorm(x, w):
    xf = x.astype(jnp.float32)
    y = xf * lax.rsqrt(jnp.mean(xf * xf, axis=-1, keepdims=True) + EPS)
    return (y * w.astype(jnp.float32)).astype(x.dtype)


def causal_conv(x, w, b):
    s = x.shape[1]
    xp = jnp.pad(x, ((0, 0), (CONV_K - 1, 0), (0, 0)))
    y = b + w[0] * xp[:, 0:s]
    for tap in range(1, CONV_K):
        y = y + w[tap] * xp[:, tap:tap + s]
    return y


def rglru(x, w_a, b_a, w_x, b_x, lam):
    bn, s, _ = x.shape
    xb = x.reshape(bn, s, LRU_BLOCKS, LRU_BLOCK)
    r = jax.nn.sigmoid((jnp.einsum('bsni,nij->bsnj', xb, w_a).reshape(bn, s, LRU_WIDTH) + b_a).astype(jnp.float32))
    i = jax.nn.sigmoid((jnp.einsum('bsni,nij->bsnj', xb, w_x).reshape(bn, s, LRU_WIDTH) + b_x).astype(jnp.float32))
    log_a = -LRU_C * r * jax.nn.softplus(-lam.astype(jnp.float32))
    a = jnp.exp(log_a)
    u = jnp.sqrt(-jnp.expm1(2.0 * log_a)) * (i * x.astype(jnp.float32))

    def combine(left, right):
        a1, b1 = left
        a2, b2 = right
        return a1 * a2, a2 * b1 + b2

    _, h = lax.associative_scan(combine, (a, u), axis=1)
    return h


def mlstm_chunkwise(q, k, v, ig, lf):
    bn, nh, s, _ = q.shape
    nc = s // MLSTM_CHUNK

    def chunks(t):
        return jnp.moveaxis(t.reshape(bn, nh, nc, MLSTM_CHUNK, *t.shape[3:]), 2, 0)

    causal = jnp.tril(jnp.ones((MLSTM_CHUNK, MLSTM_CHUNK), dtype=bool))

    def step(carry, inp):
        c_mat, n_vec, m = carry
        qc, kc, vc, ic, fc = inp
        g = jnp.cumsum(fc, axis=-1)
        dmat = jnp.where(causal, g[..., :, None] - g[..., None, :] + ic[..., None, :], -jnp.inf)
        m_inter = g + m[..., None]
        m_j = jnp.maximum(m_inter, jnp.max(dmat, axis=-1))
        sc = jnp.einsum('bhld,bhsd->bhls', qc, kc) * jnp.exp(dmat - m_j[..., None])
        inter = jnp.exp(m_inter - m_j)
        num = jnp.einsum('bhls,bhsv->bhlv', sc, vc) + inter[..., None] * jnp.einsum('bhvd,bhld->bhlv', c_mat, qc)
        den = jnp.sum(sc, axis=-1) + inter * jnp.einsum('bhd,bhld->bhl', n_vec, qc)
        h = num / jnp.maximum(jnp.abs(den), jnp.exp(-m_j))[..., None]
        g_tot = g[..., -1]
        w = g_tot[..., None] - g + ic
        m_new = jnp.maximum(g_tot + m, jnp.max(w, axis=-1))
        decay = jnp.exp(g_tot + m - m_new)
        ws = jnp.exp(w - m_new[..., None])
        c_mat = decay[..., None, None] * c_mat + jnp.einsum('bhs,bhsv,bhsd->bhvd', ws, vc, kc)
        n_vec = decay[..., None] * n_vec + jnp.einsum('bhs,bhsd->bhd', ws, kc)
        return (c_mat, n_vec, m_new), h

    init = (jnp.zeros((bn, nh, MLSTM_DV, MLSTM_DQK), jnp.float32),
            jnp.zeros((bn, nh, MLSTM_DQK), jnp.float32),
            jnp.zeros((bn, nh), jnp.float32))
    _, hs = lax.scan(step, init, (chunks(q), chunks(k), chunks(v), chunks(ig), chunks(lf)))
    return jnp.moveaxis(hs, 0, 2).reshape(bn, nh, s, MLSTM_DV)


def lru_mlstm_layer(x, w_in, conv_l_w, conv_l_b, lru_wa, lru_ba, lru_wx, lru_bx, lru_lam,
                    conv_m_w, conv_m_b, w_q, w_k, w_v, w_o, w_if, b_if, m_norm, w_out):
    bn, s, _ = x.shape
    u = x @ w_in
    x_l, x_m, z = jnp.split(u, [LRU_WIDTH, LRU_WIDTH + MLSTM_WIDTH], axis=-1)
    y_l = rglru(causal_conv(x_l, conv_l_w, conv_l_b), lru_wa, lru_ba, lru_wx, lru_bx, lru_lam)
    x_mc = jax.nn.silu(causal_conv(x_m, conv_m_w, conv_m_b))
    xm_h = x_m.reshape(bn, s, MLSTM_HEADS, MLSTM_DV)
    xmc_h = x_mc.reshape(bn, s, MLSTM_HEADS, MLSTM_DV)
    q = jnp.einsum('bshi,hij->bshj', xmc_h, w_q)
    k = jnp.einsum('bshi,hij->bshj', xmc_h, w_k) * (MLSTM_DQK ** -0.5)
    v = jnp.einsum('bshi,hij->bshj', xm_h, w_v)
    o = jax.nn.sigmoid(jnp.einsum('bshi,hij->bshj', xm_h, w_o).astype(jnp.float32))
    gin = jnp.concatenate([q.reshape(bn, s, -1), k.reshape(bn, s, -1), v.reshape(bn, s, -1)], axis=-1)
    gates = (gin @ w_if + b_if).astype(jnp.float32)
    ig = jnp.transpose(gates[..., :MLSTM_HEADS], (0, 2, 1))
    lf = jnp.transpose(jax.nn.log_sigmoid(gates[..., MLSTM_HEADS:]), (0, 2, 1))
    to_bhs = lambda t: jnp.transpose(t, (0, 2, 1, 3)).astype(jnp.float32)
    h = mlstm_chunkwise(to_bhs(q), to_bhs(k), to_bhs(v), ig, lf)
    h = rmsnorm(jnp.transpose(h, (0, 2, 1, 3)), m_norm)
    y_m = (o * h).reshape(bn, s, MLSTM_WIDTH)
    y = jnp.concatenate([y_l, y_m], axis=-1).astype(x.dtype) * jax.nn.silu(z)
    return y @ w_out


def ssd_scan(xh, dt, a, bm, cm):
    bn, s = xh.shape[:2]
    nc = s // SSD_CHUNK
    ch = lambda t: t.reshape(bn, nc, SSD_CHUNK, *t.shape[2:])
    xdt = ch(xh * dt[..., None])
    da_cs = jnp.cumsum(ch(dt * a), axis=2)
    bm, cm = ch(bm), ch(cm)
    causal = jnp.tril(jnp.ones((SSD_CHUNK, SSD_CHUNK), dtype=bool))
    da_t = jnp.moveaxis(da_cs, 2, -1)
    decay = jnp.exp(jnp.where(causal, da_t[..., :, None] - da_t[..., None, :], -jnp.inf))
    cb = jnp.einsum('bclgn,bcsgn->bcgls', cm, bm)
    y_diag = jnp.einsum('bcgls,bcgels,bcsgep->bclgep', cb, decay, xdt)
    decay_states = jnp.exp(da_cs[:, :, -1:] - da_cs)
    states = jnp.einsum('bclgn,bclge,bclgep->bcgepn', bm, decay_states, xdt)
    chunk_decay = jnp.exp(da_cs[:, :, -1])

    def step(st, inp):
        new_st, dc = inp
        return dc[..., None, None] * st + new_st, st

    init = jnp.zeros((bn, SSD_GROUPS, SSD_HPG, SSD_HEADDIM, SSD_STATE), jnp.float32)
    _, s_in = lax.scan(step, init, (jnp.moveaxis(states, 1, 0), jnp.moveaxis(chunk_decay, 1, 0)))
    s_in = jnp.moveaxis(s_in, 0, 1)
    y_off = jnp.einsum('bclgn,bcgepn,bclge->bclgep', cm, s_in, jnp.exp(da_cs))
    return (y_diag + y_off).reshape(bn, s, SSD_GROUPS, SSD_HPG, SSD_HEADDIM)


def ssd_layer(x, w_in, conv_w, conv_b, dt_bias, a_log, d_skip, gnorm, w_out):
    bn, s, _ = x.shape
    u = x @ w_in
    z, xbc, dt = jnp.split(u, [D_INNER, D_INNER + SSD_CONV_DIM], axis=-1)
    xbc = jax.nn.silu(causal_conv(xbc, conv_w, conv_b))
    xs, bm, cm = jnp.split(xbc, [D_INNER, D_INNER + SSD_GROUPS * SSD_STATE], axis=-1)
    dt = jax.nn.softplus(dt.astype(jnp.float32) + dt_bias.astype(jnp.float32))
    a = -jnp.exp(a_log.astype(jnp.float32)).reshape(SSD_GROUPS, SSD_HPG)
    xh = xs.astype(jnp.float32).reshape(bn, s, SSD_GROUPS, SSD_HPG, SSD_HEADDIM)
    y = ssd_scan(xh, dt.reshape(bn, s, SSD_GROUPS, SSD_HPG), a,
                 bm.astype(jnp.float32).reshape(bn, s, SSD_GROUPS, SSD_STATE),
                 cm.astype(jnp.float32).reshape(bn, s, SSD_GROUPS, SSD_STATE))
    y = y + d_skip.astype(jnp.float32).reshape(SSD_GROUPS, SSD_HPG, 1) * xh
    y = y.reshape(bn, s, D_INNER).astype(x.dtype) * jax.nn.silu(z)
    y = rmsnorm(y.reshape(bn, s, SSD_GROUPS, D_INNER // SSD_GROUPS),
                gnorm.reshape(SSD_GROUPS, D_INNER // SSD_GROUPS)).reshape(bn, s, D_INNER)
    return y @ w_out


def setup_inputs(seed: int = 0) -> dict:
    key = jax.random.key(seed)
    ks = iter(jax.random.split(key, 40))
    nrm = lambda shape, scale: jax.random.normal(next(ks), shape, jnp.float32) * scale
    uni = lambda shape, lo, hi: jax.random.uniform(next(ks), shape, jnp.float32, lo, hi)
    ne, no, h = N_EVEN, N_ODD, MLSTM_HEADS
    x = jax.random.normal(next(ks), (BATCH, SEQ, D_MODEL), jnp.float32)
    a0 = uni((ne, LRU_WIDTH), 0.9, 0.999) ** (1.0 / LRU_C)
    dt0 = jnp.exp(uni((no, SSD_HEADS), float(np.log(1e-3)), float(np.log(1e-1))))
    return {
        "x": x,
        "e_norm": 1.0 + nrm((ne, D_MODEL), 0.05),
        "e_w_in": nrm((ne, D_MODEL, EVEN_IN), D_MODEL ** -0.5),
        "e_conv_l_w": nrm((ne, CONV_K, LRU_WIDTH), 0.5),
        "e_conv_l_b": nrm((ne, LRU_WIDTH), 0.02),
        "e_lru_wa": nrm((ne, LRU_BLOCKS, LRU_BLOCK, LRU_BLOCK), LRU_BLOCK ** -0.5),
        "e_lru_ba": nrm((ne, LRU_WIDTH), 0.02),
        "e_lru_wx": nrm((ne, LRU_BLOCKS, LRU_BLOCK, LRU_BLOCK), LRU_BLOCK ** -0.5),
        "e_lru_bx": nrm((ne, LRU_WIDTH), 0.02),
        "e_lru_lam": jnp.log(a0) - jnp.log1p(-a0),
        "e_conv_m_w": nrm((ne, CONV_K, MLSTM_WIDTH), 0.5),
        "e_conv_m_b": nrm((ne, MLSTM_WIDTH), 0.02),
        "e_w_q": nrm((ne, h, MLSTM_DV, MLSTM_DQK), MLSTM_DV ** -0.5),
        "e_w_k": nrm((ne, h, MLSTM_DV, MLSTM_DQK), MLSTM_DV ** -0.5),
        "e_w_v": nrm((ne, h, MLSTM_DV, MLSTM_DV), MLSTM_DV ** -0.5),
        "e_w_o": nrm((ne, h, MLSTM_DV, MLSTM_DV), MLSTM_DV ** -0.5),
        "e_w_if": nrm((ne, MLSTM_GATE_IN, 2 * h), MLSTM_GATE_IN ** -0.5),
        "e_b_if": jnp.concatenate([nrm((ne, h), 0.1), uni((ne, h), 3.0, 6.0)], axis=-1),
        "e_m_norm": 1.0 + nrm((ne, h, MLSTM_DV), 0.05),
        "e_w_out": nrm((ne, D_INNER, D_MODEL), D_INNER ** -0.5),
        "o_norm": 1.0 + nrm((no, D_MODEL), 0.05),
        "o_w_in": nrm((no, D_MODEL, SSD_IN), D_MODEL ** -0.5),
        "o_conv_w": nrm((no, CONV_K, SSD_CONV_DIM), 0.5),
        "o_conv_b": nrm((no, SSD_CONV_DIM), 0.02),
        "o_dt_bias": dt0 + jnp.log(-jnp.expm1(-dt0)),
        "o_a_log": jnp.log(uni((no, SSD_HEADS), 1.0, 16.0)),
        "o_d_skip": 1.0 + nrm((no, SSD_HEADS), 0.1),
        "o_gnorm": 1.0 + nrm((no, D_INNER), 0.05),
        "o_w_out": nrm((no, D_INNER, D_MODEL), D_INNER ** -0.5),
        "final_norm": 1.0 + nrm((D_MODEL,), 0.05),
    }


def reference(x, e_norm, e_w_in, e_conv_l_w, e_conv_l_b, e_lru_wa, e_lru_ba, e_lru_wx, e_lru_bx,
              e_lru_lam, e_conv_m_w, e_conv_m_b, e_w_q, e_w_k, e_w_v, e_w_o, e_w_if, e_b_if,
              e_m_norm, e_w_out, o_norm, o_w_in, o_conv_w, o_conv_b, o_dt_bias, o_a_log,
              o_d_skip, o_gnorm, o_w_out, final_norm):
    for layer in range(DEPTH):
        j = layer // 2
        if layer % 2 == 0:
            x = x + lru_mlstm_layer(
                rmsnorm(x, e_norm[j]), e_w_in[j], e_conv_l_w[j], e_conv_l_b[j], e_lru_wa[j],
                e_lru_ba[j], e_lru_wx[j], e_lru_bx[j], e_lru_lam[j], e_conv_m_w[j], e_conv_m_b[j],
                e_w_q[j], e_w_k[j], e_w_v[j], e_w_o[j], e_w_if[j], e_b_if[j], e_m_norm[j],
                e_w_out[j]).astype(x.dtype)
        else:
            x = x + ssd_layer(
                rmsnorm(x, o_norm[j]), o_w_in[j], o_conv_w[j], o_conv_b[j], o_dt_bias[j],
                o_a_log[j], o_d_skip[j], o_gnorm[j], o_w_out[j]).astype(x.dtype)
    return rmsnorm(x, final_norm)
```

```python
import contextlib
import numpy as np
import concourse.bass as bass
import concourse.mybir as mybir

F32 = mybir.dt.float32
BF16 = mybir.dt.bfloat16
AF = mybir.ActivationFunctionType
ALU = mybir.AluOpType
AX = mybir.AxisListType

NDMA_SEM = 6


class Buf:
    def __init__(self, name, t):
        self.name = name
        self.t = t
        self.st = {}

    def __getitem__(self, idx):
        return self.t[idx]


class BufView(Buf):
    def __init__(self, base, ap):
        self.name = base.name
        self.t = ap
        self.st = base.st
        if getattr(base, "excl", False):
            self.excl = True


class K:
    def __init__(self):
        self.nc = bass.Bass("TRN2", target_bir_lowering=False)
        self.es = contextlib.ExitStack()
        nc = self.nc
        self.eng = {"pe": nc.tensor, "act": nc.scalar, "dve": nc.vector, "pool": nc.gpsimd, "sp": nc.sync}
        self.csem = {e: self.es.enter_context(nc.semaphore("s_" + e)) for e in ["pe", "act", "dve", "pool"]}
        self.ccnt = {e: 0 for e in self.csem}
        self.dsem = {q: [self.es.enter_context(nc.semaphore("d_%s%d" % (q, i))) for i in range(NDMA_SEM)]
                     for q in ["sp", "pool"]}
        self.dcnt = {q: 0 for q in self.dsem}
        self.dtok = {q: [None] * NDMA_SEM for q in self.dsem}
        self.seen = {e: {} for e in self.eng}
        self.pe_pending = []
        self.nbuf = 0
        self.ninst = 0

    def sb(self, name, shape, dt=F32):
        t = getattr(self, "les", self.es).enter_context(self.nc.sbuf_tensor(name, list(shape), dt))
        return Buf(name, t)

    def ps(self, name, shape, dt=F32):
        t = getattr(self, "les", self.es).enter_context(self.nc.psum_tensor(name, list(shape), dt))
        b = Buf(name, t)
        b.excl = True
        return b

    def dram(self, name, shape, dt=F32, kind="ExternalInput"):
        t = self.nc.dram_tensor(name, list(shape), dt, kind=kind)
        b = Buf(name, t.ap())
        return b

    def _wait(self, e, tok):
        if tok is None:
            return
        sem, val, semname = tok
        if self.seen[e].get(semname, 0) >= val:
            return
        self.eng[e].wait_ge(sem, val)
        self.seen[e][semname] = val

    def _deps(self, e, reads, writes):
        toks = []
        for (b, k) in reads:
            s = b.st.get(k)
            if s and s[0] is not None:
                toks.append(s[0])
            if s and getattr(b, "excl", False):
                toks.extend(t for (eng, t) in s[1].items() if eng != e)
        for (b, k) in writes:
            s = b.st.get(k)
            if s:
                if s[0] is not None:
                    toks.append(s[0])
                toks.extend(s[1].values())
        return toks

    def _commit(self, e, tok, reads, writes):
        for (b, k) in reads:
            s = b.st.setdefault(k, [None, {}])
            s[1][e] = tok
        for (b, k) in writes:
            b.st[k] = [tok, {}]

    def op(self, e, fn, reads=(), writes=(), signal=True):
        if getattr(self, "cut", None) is not None:
            if self.cut <= 0:
                return None
            self.cut -= 1
        reads = [r if isinstance(r, tuple) else (r, 0) for r in reads]
        writes = [w if isinstance(w, tuple) else (w, 0) for w in writes]
        for tok in self._deps(e, reads, writes):
            if e == "pe" and tok[2] == "c_pe":
                continue
            self._wait(e, tok)
        ins = fn()
        self.ninst += 1
        if e == "pe" and not signal:
            self.pe_pending.append((reads, writes))
            return ins
        self.ccnt[e] += 1
        tok = (self.csem[e], self.ccnt[e], "c_" + e)
        ins.then_inc(self.csem[e], 1)
        if e == "pe":
            for (r, w) in self.pe_pending:
                self._commit(e, tok, r, w)
            self.pe_pending = []
        self._commit(e, tok, reads, writes)
        return ins

    def dma(self, q, out_ap, in_ap, reads=(), writes=()):
        reads = [r if isinstance(r, tuple) else (r, 0) for r in reads]
        writes = [w if isinstance(w, tuple) else (w, 0) for w in writes]
        if getattr(self, "cut", None) is not None and self.cut <= 0:
            return None
        for tok in self._deps(q, reads, writes):
            self._wait(q, tok)
        i = self.dcnt[q] % NDMA_SEM
        self._wait(q, self.dtok[q][i])
        n = self.dcnt[q] // NDMA_SEM + 1
        self.dcnt[q] += 1
        sem = self.dsem[q][i]
        ins = self.eng[q].dma_start(out=out_ap, in_=in_ap)
        ins.then_inc(sem, 16)
        self.ninst += 1
        tok = (sem, 16 * n, "d_%s%d" % (q, i))
        self.dtok[q][i] = tok
        self._commit(q, tok, reads, writes)
        return ins

    def finish(self, out_bufs):
        for b in out_bufs:
            for k, s in b.st.items():
                self._wait("sp", s[0])

    def barrier(self):
        toks = [(self.csem[e], self.ccnt[e], "c_" + e) for e in self.csem if self.ccnt[e] > 0]
        for q in self.dtok:
            toks += [t for t in self.dtok[q] if t is not None]
        for e in self.eng:
            for t in toks:
                self._wait(e, t)


class PV:
    def __init__(self, buf, ap):
        self.buf, self.ap = buf, ap


class TS:
    pass


def pipe_gen(gens, depth=2, skew=3):
    it = iter(gens)
    active = []
    since = skew
    done = False
    while True:
        if not done and len(active) < depth and since >= skew:
            try:
                active.append(next(it))
                since = 0
            except StopIteration:
                done = True
        if not active:
            if done:
                break
            since = skew
            continue
        for g in list(active):
            try:
                next(g)
            except StopIteration:
                active.remove(g)
        since += 1
        yield


def run_pipe(gens, depth=2, skew=3):
    for _ in pipe_gen(gens, depth, skew):
        pass


def mix(ga, gb, nb):
    da = db = False
    while not (da and db):
        if not da:
            try:
                next(ga)
            except StopIteration:
                da = True
        for _ in range(nb if not da else 1000000):
            if db:
                break
            try:
                next(gb)
            except StopIteration:
                db = True

D = 2048
NCH = 16
EVEN_IN = 8192
EPS = 1e-6
LRU_SKEW = 3
SSD_SKEW = 1
MIN_SKEW = 2
SIN_SKEW = 2
ML_STEPS = 8


def host_cols_l0(p):
    def col(v):
        return np.ascontiguousarray(v.reshape(-1, 128).T)
    def col4(w):
        return np.ascontiguousarray(w.reshape(4, -1, 128).transpose(2, 1, 0).reshape(128, -1))
    cols = np.concatenate([
        col(p["e_norm"][0]),
        col4(p["e_conv_l_w"][0]),
        col(p["e_conv_l_b"][0]),
        col(p["e_lru_ba"][0]),
        col(p["e_lru_bx"][0]),
        col(p["e_lru_lam"][0]),
        col4(p["e_conv_m_w"][0]),
        col(p["e_conv_m_b"][0]),
    ], axis=1).astype(np.float32)
    return cols


def consts_host():
    ident = np.eye(128, dtype=np.float32)
    mask01 = np.triu(np.ones((128, 128), np.float32))
    negmask = np.where(mask01 > 0, 0.0, -30000.0).astype(np.float32)
    return np.concatenate([ident, mask01, negmask, np.ascontiguousarray(negmask.T)], axis=1)


class L0:
    def __init__(self, k, T, xin, xout):
        self.k = k
        self.T = T
        self.TT = T // 128
        self.xin, self.xout = xin, xout
        nc = k.nc
        self.w_in = k.dram("e_w_in", [D, EVEN_IN])
        self.w_out = k.dram("e_w_out", [4096, D])
        self.wa = k.dram("e_lru_wa", [16, 128, 128])
        self.wx = k.dram("e_lru_wx", [16, 128, 128])
        self.wq = k.dram("e_w_q", [4, 512, 256])
        self.wk = k.dram("e_w_k", [4, 512, 256])
        self.wv = k.dram("e_w_v", [4, 512, 512])
        self.wo = k.dram("e_w_o", [4, 512, 512])
        self.wif = k.dram("e_w_if", [4096, 8])
        self.cols_d = k.dram("e_cols", [128, 224])
        self.bif_d = k.dram("e_bif", [4, 2])
        self.mnorm_d = k.dram("e_mncol", [128, 16])
        self.const_d = k.dram("consts", [128, 512])
        self.NSLOT = 72
        self.scr = k.dram("wscr0", [self.NSLOT, 128, 4096], BF16, kind="Internal")

    def alloc(self):
        k, T, TT = self.k, self.T, self.TT
        s = self
        s.cols = k.sb("cols", [128, 224])
        s.bif = k.sb("bif", [4, 2])
        s.mncol = k.sb("mncol", [128, 16])
        s.cst = k.sb("cst", [128, 512])
        s.cstb = k.sb("cstb", [128, 512], BF16)
        s.ones = k.sb("ones", [128, 512])
        s.onesb = k.sb("onesb", [128, 8], BF16)
        s.kcol = k.sb("kcol", [128, 32])
        s.colsh = k.sb("colsh", [128, 224])
        s.mhalf = k.sb("mhalf", [128, 2])
        s.phalf = k.sb("phalf", [128, T])
        s.wab = k.sb("wab", [128, 16, 128], BF16)
        s.wxb = k.sb("wxb", [128, 16, 128], BF16)
        s.wifb = k.sb("wifb", [128, 32, 8], BF16)
        s.tail_l = k.sb("tail_l", [128, 16, 3])
        s.tail_m = k.sb("tail_m", [128, 16, 3])
        s.hlast = k.sb("hlast", [128, 16])
        s.CT = k.sb("CT", [128, 4, 2, 512])
        s.CTb = k.sb("CTb", [128, 4, 2, 512], BF16)
        s.nst = k.sb("nst", [128, 4, 2])
        s.nstb = k.sb("nstb", [128, 4, 2], BF16)
        s.Gl = k.sb("Gl", [4, 1])
        s.Ml = k.sb("Ml", [4, 1])
        s.xt = [k.sb("xt%d" % i, [128, D]) for i in range(2)]
        s.st4 = k.sb("st4", [128, 8])
        s.xnT = k.sb("xnT", [128, NCH, T], BF16)
        s.yT = k.sb("yT", [128, 16, T], BF16)
        s.SW = 256
        s.slab = [k.sb("slab%d" % i, [128, 16, 256], BF16) for i in range(4)]
        s.nslab = 0
        s.xe = [k.sb("xe%d" % i, [128, T + 3]) for i in range(2)]
        s.tsets = []
        for i in range(2):
            ts = TS()
            ts.xc = k.sb("xc%d" % i, [128, T])
            ts.xcb = k.sb("xcb%d" % i, [128, T], BF16)
            for nm in ["rr", "ii", "aa", "a2", "hh", "sz1", "th"]:
                setattr(ts, nm, k.sb("%s%d" % (nm, i), [128, T]))
            s.tsets.append(ts)
        s.xmT = k.sb("xmT", [128, 16, T], BF16)
        s.xmcT = k.sb("xmcT", [128, 16, T], BF16)
        s.szm = k.sb("szm", [128, 16, T], BF16)
        s.qT = k.sb("qT", [128, 4, 2, T], BF16)
        s.kT = k.sb("kT", [128, 4, 2, T], BF16)
        s.vTt = [k.sb("vTt%d" % i, [128, T], BF16) for i in range(2)]
        s.vtok = k.sb("vtok", [128, TT, 512], BF16)
        s.otok = k.sb("otok", [128, TT, 512], BF16)
        s.wsl = [k.sb("wsl%d" % i, [128, 4, 512], BF16) for i in range(2)]
        s.g_ig = k.sb("g_ig", [4, T])
        s.g_t1 = k.sb("g_t1", [4, T])
        s.g_t2 = k.sb("g_t2", [4, T])
        s.g_lf = k.sb("g_lf", [4, T])
        s.g_G = k.sb("g_G", [4, T])
        s.g_a = k.sb("g_a", [4, T])
        s.g_cm = k.sb("g_cm", [4, 8])
        s.g_Mn = k.sb("g_Mn", [4, 8])
        s.g_Mp = k.sb("g_Mp", [4, 8])
        s.g_nMp = k.sb("g_nMp", [4, 8])
        s.g_nMn = k.sb("g_nMn", [4, 8])
        s.g_dd = k.sb("g_dd", [4, 8])
        s.g_rows = k.sb("g_rows", [4, 4, T])
        s.g_cols = k.sb("g_cols", [128, TT, 4, 4])
        s.scT = [k.sb("scT%d" % i, [128, 128], BF16) for i in range(2)]
        s.kws = [k.sb("kws%d" % i, [128, 256], BF16) for i in range(2)]
        s.hb = [k.sb("hb%d" % i, [128, 512]) for i in range(2)]
        s.hb2 = [k.sb("hb2%d" % i, [128, 512]) for i in range(2)]
        s.ytok = [k.sb("ytok%d" % i, [128, 512], BF16) for i in range(2)]
        s.sm = [k.sb("sm%d" % i, [128, 8]) for i in range(2)]
        s.psA = [k.ps("psA%d" % i, [128, 512]) for i in range(4)]
        s.npsA = 0
        s.npsB = 0
        s.psT = [k.ps("psT%d" % i, [128, 1024], BF16) for i in range(2)]
        s.npsT = 0
        s.psS = k.ps("psS", [128, 512])
        s.psG = k.ps("psG", [128, 512])

    def rstd(self, buf, ap, n):
        k, s = self.k, self
        nc = k.nc
        P = ap.shape[0]
        k.op("dve", lambda: nc.vector.tensor_scalar(out=ap, in0=ap, scalar1=1.0 / n, scalar2=EPS, op0=ALU.mult, op1=ALU.add), reads=[buf], writes=[buf])
        k.op("act", lambda: nc.scalar.activation(out=ap, in_=ap, func=AF.Ln), reads=[buf], writes=[buf])
        k.op("act", lambda: nc.scalar.activation(out=ap, in_=ap, func=AF.Exp, scale=-0.5), reads=[buf], writes=[buf])

    def nextA(self):
        b = self.psA[self.npsA % len(self.psA)]
        self.npsA += 1
        return b

    def nextB(self):
        b = self.psA[2 + self.npsB % 2]
        self.npsB += 1
        return b

    def nextT(self):
        i = self.npsT % 2
        self.npsT += 1
        return i

    def _cached_load(self, dst, src_ap):
        k, s = self.k, self
        if not hasattr(s, "wcache"):
            s.wcache = {}
            s.nq = 0
        shp = tuple(src_ap.shape)
        key = (src_ap.name, src_ap.offset, shp)
        dview = dst.t[:, 0:shp[1], 0:shp[2]]
        if key not in s.wcache:
            slot = len(s.wcache)
            assert slot < s.NSLOT, slot
            s.wcache[key] = slot
            k.dma("pool", dview, src_ap, writes=[dst])
            sv = s.scr.t[slot][:, 0:shp[1] * shp[2]].rearrange("p (c f) -> p c f", f=shp[2])
            k.dma("sp", sv, dview, reads=[dst], writes=[(s.scr, slot)])
        else:
            slot = s.wcache[key]
            sv = s.scr.t[slot][:, 0:shp[1] * shp[2]].rearrange("p (c f) -> p c f", f=shp[2])
            k.dma("sp", dview, sv, reads=[(s.scr, slot)], writes=[dst])

    def load_slab(self, src_ap, pool=None):
        if pool is None:
            sl = self.slab[self.nslab % len(self.slab)]
            self.nslab += 1
        else:
            idx, cnt = pool
            sl = self.slab[idx[cnt[0] % len(idx)]]
            cnt[0] += 1
        self._cached_load(sl, src_ap)
        return sl

    def setup(self):
        k, s = self.k, self
        nc = k.nc
        k.dma("sp", s.cols[:, :], s.cols_d[:, :], writes=[s.cols])
        k.dma("sp", s.bif[:, :], s.bif_d[:, :], writes=[s.bif])
        k.dma("sp", s.mncol[:, :], s.mnorm_d[:, :], writes=[s.mncol])
        k.dma("sp", s.cst[:, :], s.const_d[:, :], writes=[s.cst])
        k.dma("pool", s.cstb[:, :], s.const_d[:, :], writes=[s.cstb])
        k.dma("pool", s.wab[:, :, :], s.wa.t.rearrange("n i j -> i n j"), writes=[s.wab])
        k.dma("pool", s.wxb[:, :, :], s.wx.t.rearrange("n i j -> i n j"), writes=[s.wxb])
        k.dma("pool", s.wifb[:, :, :], s.wif.t.rearrange("(c p) g -> p c g", p=128), writes=[s.wifb])
        k.op("dve", lambda: nc.vector.memset(s.ones[:, :], 1.0), writes=[s.ones])
        k.op("dve", lambda: nc.vector.memset(s.onesb[:, :], 1.0), writes=[s.onesb])
        k.op("dve", lambda: nc.vector.memset(s.phalf[:, :], 0.5), writes=[s.phalf])
        k.op("dve", lambda: nc.vector.memset(s.mhalf[:, 0:1], -0.5), writes=[s.mhalf])
        k.op("dve", lambda: nc.vector.memset(s.mhalf[:, 1:2], 0.5), reads=[s.mhalf], writes=[s.mhalf])
        k.op("dve", lambda: nc.vector.tensor_scalar(out=s.colsh[:, :], in0=s.cols[:, :], scalar1=0.5, scalar2=None, op0=ALU.mult), reads=[s.cols], writes=[s.colsh])
        k.op("dve", lambda: nc.vector.tensor_scalar(out=s.mncol[:, :], in0=s.mncol[:, :], scalar1=0.5, scalar2=None, op0=ALU.mult), reads=[s.mncol], writes=[s.mncol])
        for b in [s.tail_l, s.tail_m, s.hlast, s.CT, s.CTb, s.nst, s.nstb, s.Gl, s.Ml]:
            k.op("dve", (lambda b=b: nc.vector.memset(b.t[:], 0.0)), writes=[b])
        lam = s.cols[:, 128:144]
        k.op("act", lambda: nc.scalar.activation(out=s.kcol[:, 0:16], in_=lam, func=AF.Exp, scale=-1.0),
             reads=[s.cols], writes=[s.kcol])
        k.op("act", lambda: nc.scalar.activation(out=s.kcol[:, 0:16], in_=s.kcol[:, 0:16], func=AF.Ln, bias=1.0),
             reads=[s.kcol], writes=[s.kcol])
        k.op("dve", lambda: nc.vector.tensor_scalar(out=s.kcol[:, 0:16], in0=s.kcol[:, 0:16], scalar1=-4.0, scalar2=None, op0=ALU.mult),
             reads=[s.kcol], writes=[s.kcol])

    def rmsnorm_g(self, b, stage=None):
        k, s, T, TT = self.k, self, self.T, self.TT
        nc = k.nc
        if not hasattr(s, "xnb_all"):
            s.xnb_all = [k.sb(s.__class__.__name__ + "xnba%d" % i, [128, D], BF16) for i in range(TT)]
        for tt in range(TT):
            r0 = b * T + tt * 128
            if stage is None:
                src, keys = s.xt[tt][:, :], [s.xt[tt]]
            else:
                src, keys = stage[tt]
            k.dma("sp", src, s.xin[r0:r0 + 128, :], reads=[(s.xin, r0)], writes=keys)
            yield
            k.op("act", lambda src=src, tt=tt: nc.scalar.activation(out=s.xnb_all[tt][:, :], in_=src, func=AF.Square, accum_out=s.st4[:, tt:tt + 1]),
                 reads=keys, writes=[s.st4, s.xnb_all[tt]])
            yield
            s.rstd(s.st4, s.st4[:, tt:tt + 1], D)
            yield
            k.op("act", lambda src=src, tt=tt: nc.scalar.activation(out=s.xnb_all[tt][:, :], in_=src, func=AF.Copy, scale=s.st4[:, tt:tt + 1]),
                 reads=keys + [s.st4], writes=[s.xnb_all[tt]])
            yield
        ident = s.cstb[:, 0:128]
        for c in range(NCH):
            i = s.nextT()
            for tt in range(TT):
                k.op("pe", lambda c=c, tt=tt, i=i: nc.tensor.transpose(out=s.psT[i][:, tt * 128:(tt + 1) * 128], in_=s.xnb_all[tt][:, c * 128:(c + 1) * 128], identity=ident),
                     reads=[s.xnb_all[tt], s.cstb], writes=[s.psT[i]], signal=(tt == TT - 1))
            k.op("dve", lambda c=c, i=i: nc.vector.tensor_scalar(out=s.xnT[:, c, :], in0=s.psT[i][:, 0:T], scalar1=s.cols[:, c:c + 1], scalar2=None, op0=ALU.mult),
                 reads=[s.psT[i], s.cols], writes=[(s.xnT, c)])
            yield

    def rmsnorm_T(self, b):
        for _ in self.rmsnorm_g(b):
            pass

    def load_resid(self, b):
        k, s, T, TT = self.k, self, self.T, self.TT
        for tt in range(TT):
            r0 = b * T + tt * 128
            k.dma("sp", s.xt[tt][:, :], s.xin[r0:r0 + 128, :], reads=[(s.xin, r0)], writes=[s.xt[tt]])

    def inproj_tile(self, sl, j, ps):
        k, s, T = self.k, self, self.T
        nc = k.nc
        for c in range(NCH):
            k.op("pe", lambda c=c: nc.tensor.matmul(ps[:, 0:T], lhsT=sl[:, c, j * 128:(j + 1) * 128], rhs=s.xnT[:, c, :], start=(c == 0), stop=(c == NCH - 1)),
                 reads=[sl, (s.xnT, c)], writes=[ps], signal=(c == NCH - 1))

    def conv_g(self, ps, xe, tail, ci, wbase, bbase, out, cols=None, offload=False, xe_act=False):
        k, s, T = self.k, self, self.T
        nc = k.nc
        cols = s.cols if cols is None else cols
        w = lambda tap: cols[:, wbase + ci * 4 + tap: wbase + ci * 4 + tap + 1]
        k.op("dve", lambda: nc.vector.tensor_copy(out=xe[:, 0:3], in_=tail[:, ci, :]), reads=[(tail, ci)], writes=[xe])
        if offload or xe_act:
            k.op("act", lambda: nc.scalar.copy(out=xe[:, 3:T + 3], in_=ps.ap), reads=[ps.buf, xe], writes=[xe])
        else:
            k.op("dve", lambda: nc.vector.tensor_copy(out=xe[:, 3:T + 3], in_=ps.ap), reads=[ps.buf, xe], writes=[xe])
        k.op("act", lambda: nc.scalar.activation(out=out[:, :], in_=ps.ap, func=AF.Identity, scale=w(3), bias=cols[:, bbase + ci: bbase + ci + 1]),
             reads=[ps.buf, cols], writes=[out])
        yield
        for tap in range(3):
            k.op("dve", lambda tap=tap: nc.vector.scalar_tensor_tensor(out=out[:, :], in0=xe[:, tap:tap + T], scalar=w(tap), in1=out[:, :], op0=ALU.mult, op1=ALU.add),
                 reads=[xe, cols, out], writes=[out])
        if offload:
            k.op("pool", lambda: nc.gpsimd.tensor_copy(out=tail[:, ci, :], in_=xe[:, T:T + 3]), reads=[xe], writes=[(tail, ci)])
        else:
            k.op("dve", lambda: nc.vector.tensor_copy(out=tail[:, ci, :], in_=xe[:, T:T + 3]), reads=[xe], writes=[(tail, ci)])
        yield

    def inproj_view(self, sl, j, ps, half):
        k, s, T = self.k, self, self.T
        nc = k.nc
        v = PV(ps, ps[:, half * T:(half + 1) * T])
        for c in range(NCH):
            k.op("pe", lambda c=c: nc.tensor.matmul(v.ap, lhsT=sl[:, c, j * 128:(j + 1) * 128], rhs=s.xnT[:, c, :], start=(c == 0), stop=(c == NCH - 1)),
                 reads=[sl, (s.xnT, c)], writes=[ps], signal=(c == NCH - 1))
        return v

    def lru_tile(self, ci, slx, slz, jx, ts):
        k, s, T = self.k, self, self.T
        nc = k.nc
        V, A, G = nc.vector, nc.scalar, nc.gpsimd
        p1 = s.psA[ci % 2]
        vx = s.inproj_view(slx, jx, p1, 0)
        vz = s.inproj_view(slz, jx, p1, 1)
        yield
        xe = s.xe[ci % 2]
        yield from s.conv_g(vx, xe, s.tail_l, ci, 16, 80, ts.xc)
        k.op("pool", lambda: G.tensor_copy(out=ts.xcb[:, :], in_=ts.xc[:, :]), reads=[ts.xc], writes=[ts.xcb])
        k.op("act", lambda: A.activation(out=ts.sz1[:, :], in_=vz.ap, func=AF.Tanh, scale=0.5), reads=[p1], writes=[ts.sz1])
        k.op("dve", lambda: V.scalar_tensor_tensor(out=ts.sz1[:, :], in0=ts.sz1[:, :], scalar=1.0, in1=vz.ap, op0=ALU.add, op1=ALU.mult), reads=[ts.sz1, p1], writes=[ts.sz1])
        yield
        p2 = p1
        k.op("pe", lambda: nc.tensor.matmul(p2[:, 0:T], lhsT=s.wab[:, ci, :], rhs=ts.xcb[:, :], start=True, stop=True),
             reads=[s.wab, ts.xcb], writes=[p2], signal=False)
        k.op("pe", lambda: nc.tensor.matmul(p2[:, T:2 * T], lhsT=s.wxb[:, ci, :], rhs=ts.xcb[:, :], start=True, stop=True),
             reads=[s.wxb, ts.xcb], writes=[p2])
        yield
        k.op("act", lambda: A.activation(out=ts.rr[:, :], in_=p2[:, 0:T], func=AF.Tanh, scale=0.5, bias=s.colsh[:, 96 + ci:97 + ci]),
             reads=[p2, s.colsh], writes=[ts.rr])
        k.op("act", lambda: A.activation(out=ts.ii[:, :], in_=p2[:, T:2 * T], func=AF.Tanh, scale=0.5, bias=s.colsh[:, 112 + ci:113 + ci]),
             reads=[p2, s.colsh], writes=[ts.ii])
        k.op("act", lambda: A.activation(out=ts.aa[:, :], in_=ts.rr[:, :], func=AF.Exp, scale=s.kcol[:, ci:ci + 1], bias=s.kcol[:, ci:ci + 1]),
             reads=[ts.rr, s.kcol], writes=[ts.aa])
        yield
        k.op("pool", lambda: G.tensor_tensor(out=ts.a2[:, :], in0=ts.aa[:, :], in1=ts.aa[:, :], op=ALU.mult), reads=[ts.aa], writes=[ts.a2])
        k.op("act", lambda: A.activation(out=ts.a2[:, :], in_=ts.a2[:, :], func=AF.Sqrt, scale=-1.0, bias=1.0), reads=[ts.a2], writes=[ts.a2])
        k.op("dve", lambda: V.scalar_tensor_tensor(out=ts.ii[:, :], in0=ts.ii[:, :], scalar=1.0, in1=ts.xc[:, :], op0=ALU.add, op1=ALU.mult),
             reads=[ts.ii, ts.xc], writes=[ts.ii])
        yield
        k.op("dve", lambda: V.scalar_tensor_tensor(out=ts.ii[:, :], in0=ts.ii[:, :], scalar=0.5, in1=ts.a2[:, :], op0=ALU.mult, op1=ALU.mult),
             reads=[ts.ii, ts.a2], writes=[ts.ii])
        k.op("dve", lambda: V.tensor_tensor_scan(out=ts.hh[:, :], data0=ts.aa[:, :], data1=ts.ii[:, :], initial=s.hlast[:, ci:ci + 1], op0=ALU.mult, op1=ALU.add),
             reads=[ts.aa, ts.ii, s.hlast], writes=[ts.hh])
        k.op("dve", lambda: V.tensor_copy(out=s.hlast[:, ci:ci + 1], in_=ts.hh[:, T - 1:T]),
             reads=[ts.hh], writes=[s.hlast])
        k.op("dve", lambda: V.scalar_tensor_tensor(out=s.yT[:, ci, :], in0=ts.hh[:, :], scalar=0.5, in1=ts.sz1[:, :], op0=ALU.mult, op1=ALU.mult),
             reads=[ts.hh, ts.sz1], writes=[(s.yT, ci)])
        yield

    def lru_tiles(self, b):
        s = self
        wv = s.w_in.t.rearrange("(c p) f -> p c f", p=128)
        SW = s.SW
        nt = SW // 128
        for g in range(2048 // SW):
            slx = s.load_slab(wv[:, :, g * SW:(g + 1) * SW])
            slz = s.load_slab(wv[:, :, 4096 + g * SW: 4096 + (g + 1) * SW])
            for j in range(nt):
                ci = g * nt + j
                yield s.lru_tile(ci, slx, slz, j, s.tsets[ci % 2])

    def lru(self, b):
        run_pipe(self.lru_tiles(b), depth=2, skew=LRU_SKEW)

    def lru_mlstm(self, b):
        s = self

        def ml():
            yield from s.mlstm_gates(b)
            yield from s.mlstm_core(b)
        mix(pipe_gen(s.lru_tiles(b), depth=2, skew=LRU_SKEW), ml(), ML_STEPS)

    def outproj_g(self, b, half, store=True, src=None, banks=None, slabpool=None):
        k, s, T, TT = self.k, self, self.T, self.TT
        nc = k.nc
        wv = s.w_out.t.rearrange("(c p) d -> p c d", p=128)
        yT = s.yT if src is None else src
        nb = 0
        for db in range(8):
            sl = s.load_slab(wv[:, half * 16:(half + 1) * 16, db * 256:(db + 1) * 256], pool=slabpool)
            for tt in range(TT):
                xt = s.xt[tt]
                if banks is None:
                    ps = s.nextA()
                else:
                    ps = banks[nb % len(banks)]
                    nb += 1
                for c in range(16):
                    k.op("pe", lambda sl=sl, c=c, tt=tt, ps=ps: nc.tensor.matmul(ps[:, 0:256], lhsT=yT[:, c, tt * 128:(tt + 1) * 128], rhs=sl[:, c, 0:256], start=(c == 0), stop=(c == 15)),
                         reads=[sl, (yT, c)], writes=[ps], signal=(c == 15))
                    if c % 4 == 3:
                        yield
                k.op("dve", lambda xt=xt, ps=ps, db=db: nc.vector.tensor_tensor(out=xt[:, db * 256:(db + 1) * 256], in0=xt[:, db * 256:(db + 1) * 256], in1=ps[:, 0:256], op=ALU.add),
                     reads=[ps, xt], writes=[xt])
        if half == 1 and store:
            for tt in range(TT):
                r0 = b * T + tt * 128
                k.dma("pool", s.xout[r0:r0 + 128, :], s.xt[tt][:, :], reads=[s.xt[tt]], writes=[(s.xout, r0)])

    def outproj(self, b, half, store=True, src=None):
        for _ in self.outproj_g(b, half, store=store, src=src):
            pass

    def conv_tile(self, ci, sl, j, ts, tail, wbase, bbase, dst, raw_dst=None, xe_act=False):
        k, s, T = self.k, self, self.T
        nc = k.nc
        ps = s.nextA()
        v = s.inproj_view(sl, j, ps, 0)
        yield
        if raw_dst is not None:
            k.op("act", lambda: nc.scalar.copy(out=raw_dst[:, ci, :], in_=v.ap), reads=[ps], writes=[(raw_dst, ci)])
        yield from s.conv_g(v, s.xe[ci % 2], tail, ci, wbase, bbase, ts.xc, cols=s.colsh, xe_act=xe_act)
        k.op("act", lambda: nc.scalar.activation(out=ts.th[:, :], in_=ts.xc[:, :], func=AF.Tanh), reads=[ts.xc], writes=[ts.th])
        yield
        k.op("dve", lambda: nc.vector.scalar_tensor_tensor(out=dst[:, ci, :], in0=ts.th[:, :], scalar=1.0, in1=ts.xc[:, :], op0=ALU.add, op1=ALU.mult),
             reads=[ts.th, ts.xc], writes=[(dst, ci)])
        yield

    def silu_tile(self, ci, sl, j, dst, ts):
        k, s, T = self.k, self, self.T
        nc = k.nc
        ps = s.nextA()
        v = s.inproj_view(sl, j, ps, 0)
        yield
        k.op("act", lambda: nc.scalar.activation(out=ts.th[:, :], in_=v.ap, func=AF.Tanh, scale=0.5), reads=[ps], writes=[ts.th])
        yield
        k.op("dve", lambda: nc.vector.scalar_tensor_tensor(out=dst[:, ci, :], in0=ts.th[:, :], scalar=1.0, in1=v.ap, op0=ALU.add, op1=ALU.mult),
             reads=[ts.th, ps], writes=[(dst, ci)])
        yield

    def mlstm_in(self, b):
        s = self
        wv = s.w_in.t.rearrange("(c p) f -> p c f", p=128)
        SW = s.SW
        nt = SW // 128

        def tiles():
            for g in range(2048 // SW):
                sl = s.load_slab(wv[:, :, 2048 + g * SW: 2048 + (g + 1) * SW])
                for j in range(nt):
                    ci = g * nt + j
                    yield s.conv_tile(ci, sl, j, s.tsets[ci % 2], s.tail_m, 144, 208, s.xmcT, raw_dst=s.xmT)
            for g in range(2048 // SW):
                sl = s.load_slab(wv[:, :, 6144 + g * SW: 6144 + (g + 1) * SW])
                for j in range(nt):
                    ci = g * nt + j
                    yield s.silu_tile(ci, sl, j, s.szm, s.tsets[ci % 2])
        run_pipe(tiles(), depth=2, skew=MIN_SKEW)

    def load_w(self, src_ap):
        s = self
        if not hasattr(s, "nwsl"):
            s.nwsl = 0
        w = s.wsl[s.nwsl % len(s.wsl)]
        s.nwsl += 1
        s._cached_load(w, src_ap)
        return w

    def mlstm_gates(self, b):
        k, s, T, TT = self.k, self, self.T, self.TT
        nc = k.nc
        for h in range(4):
            for (wd, dst, scl) in ((s.wq, s.qT, 1.0), (s.wk, s.kT, 1.0 / 16.0)):
                w = s.load_w(wd.t[h].rearrange("(c p) j -> p c j", p=128))
                for jt in range(2):
                    ps = s.nextB()
                    for ic in range(4):
                        yield
                        k.op("pe", lambda w=w, ic=ic, jt=jt, ps=ps, h=h: nc.tensor.matmul(ps[:, 0:T], lhsT=w[:, ic, jt * 128:(jt + 1) * 128], rhs=s.xmcT[:, 4 * h + ic, :], start=(ic == 0), stop=(ic == 3)),
                             reads=[w, (s.xmcT, 4 * h + ic)], writes=[ps], signal=(ic == 3))
                    yield
                    k.op("act", lambda dst=dst, h=h, jt=jt, ps=ps, scl=scl: nc.scalar.mul(out=dst[:, h, jt, :], in_=ps[:, 0:T], mul=scl),
                         reads=[ps], writes=[(dst, (h, jt))])
        n = 0
        def gmm(c, rhs_ap, rbuf, last):
            nonlocal n
            first = (n == 0)
            k.op("pe", lambda: nc.tensor.matmul(s.psG[0:4, 0:T], lhsT=s.wifb[:, c, 0:4], rhs=rhs_ap, start=first, stop=last),
                 reads=[s.wifb, rbuf], writes=[s.psG], signal=False)
            k.op("pe", lambda: nc.tensor.matmul(s.psS[0:4, 0:T], lhsT=s.wifb[:, c, 4:8], rhs=rhs_ap, start=first, stop=last),
                 reads=[s.wifb, rbuf], writes=[s.psS], signal=last)
            n += 1
        for h in range(4):
            for jt in range(2):
                yield
                gmm(h * 2 + jt, s.qT[:, h, jt, :], (s.qT, (h, jt)), False)
                yield
                gmm(8 + h * 2 + jt, s.kT[:, h, jt, :], (s.kT, (h, jt)), False)
        for h in range(4):
            w = s.load_w(s.wv.t[h].rearrange("(c p) j -> p c j", p=128))
            for jt in range(4):
                ps = s.nextB()
                for ic in range(4):
                    yield
                    k.op("pe", lambda w=w, ic=ic, jt=jt, ps=ps, h=h: nc.tensor.matmul(ps[:, 0:T], lhsT=w[:, ic, jt * 128:(jt + 1) * 128], rhs=s.xmT[:, 4 * h + ic, :], start=(ic == 0), stop=(ic == 3)),
                         reads=[w, (s.xmT, 4 * h + ic)], writes=[ps], signal=(ic == 3))
                vt = s.vTt[(h * 4 + jt) % 2]
                yield
                k.op("act", lambda vt=vt, ps=ps: nc.scalar.copy(out=vt[:, :], in_=ps[:, 0:T]), reads=[ps], writes=[vt])
                yield
                gmm(16 + h * 4 + jt, vt[:, :], vt, (h == 3 and jt == 3))
        V, A = nc.vector, nc.scalar
        yield
        k.op("act", lambda: A.activation(out=s.g_ig[:, :], in_=s.psG[0:4, 0:T], func=AF.Identity, bias=s.bif[:, 0:1]), reads=[s.psG, s.bif], writes=[s.g_ig])
        yield
        k.op("act", lambda: A.activation(out=s.g_t1[:, :], in_=s.psS[0:4, 0:T], func=AF.Identity, bias=s.bif[:, 1:2]), reads=[s.psS, s.bif], writes=[s.g_t1])
        yield
        k.op("dve", lambda: V.tensor_scalar(out=s.g_t2[:, :], in0=s.g_t1[:, :], scalar1=-1.0, scalar2=None, op0=ALU.mult), reads=[s.g_t1], writes=[s.g_t2])
        yield
        k.op("dve", lambda: V.tensor_tensor(out=s.g_t2[:, :], in0=s.g_t2[:, :], in1=s.g_t1[:, :], op=ALU.min), reads=[s.g_t1, s.g_t2], writes=[s.g_t2])
        yield
        k.op("act", lambda: A.activation(out=s.g_t2[:, :], in_=s.g_t2[:, :], func=AF.Exp), reads=[s.g_t2], writes=[s.g_t2])
        yield
        k.op("act", lambda: A.activation(out=s.g_t2[:, :], in_=s.g_t2[:, :], func=AF.Ln, bias=1.0), reads=[s.g_t2], writes=[s.g_t2])
        yield
        k.op("dve", lambda: V.tensor_scalar(out=s.g_lf[:, :], in0=s.g_t1[:, :], scalar1=0.0, scalar2=None, op0=ALU.min), reads=[s.g_t1], writes=[s.g_lf])
        yield
        k.op("dve", lambda: V.tensor_tensor(out=s.g_lf[:, :], in0=s.g_lf[:, :], in1=s.g_t2[:, :], op=ALU.subtract), reads=[s.g_lf, s.g_t2], writes=[s.g_lf])
        yield
        k.op("dve", lambda: V.tensor_tensor_scan(out=s.g_G[:, :], data0=s.ones[0:4, 0:T], data1=s.g_lf[:, :], initial=s.Gl[:, 0:1], op0=ALU.mult, op1=ALU.add),
             reads=[s.ones, s.g_lf, s.Gl], writes=[s.g_G])
        yield
        k.op("dve", lambda: V.tensor_copy(out=s.Gl[:, 0:1], in_=s.g_G[:, T - 1:T]), reads=[s.g_G], writes=[s.Gl])
        yield
        k.op("dve", lambda: V.tensor_tensor(out=s.g_a[:, :], in0=s.g_ig[:, :], in1=s.g_G[:, :], op=ALU.subtract), reads=[s.g_ig, s.g_G], writes=[s.g_a])
        yield
        k.op("dve", lambda: V.tensor_reduce(out=s.g_cm[:, 0:TT], in_=s.g_a[:, :].rearrange("p (c l) -> p c l", l=128), axis=AX.X, op=ALU.max), reads=[s.g_a], writes=[s.g_cm])
        yield
        k.op("dve", lambda: V.tensor_tensor_scan(out=s.g_Mn[:, 0:TT], data0=s.g_cm[:, 0:TT], data1=s.g_cm[:, 0:TT], initial=s.Ml[:, 0:1], op0=ALU.max, op1=ALU.max),
             reads=[s.g_cm, s.Ml], writes=[s.g_Mn])
        yield
        k.op("dve", lambda: V.tensor_copy(out=s.g_Mp[:, 0:1], in_=s.Ml[:, 0:1]), reads=[s.Ml], writes=[s.g_Mp])
        if TT > 1:
            yield
            k.op("dve", lambda: V.tensor_copy(out=s.g_Mp[:, 1:TT], in_=s.g_Mn[:, 0:TT - 1]), reads=[s.g_Mn, s.g_Mp], writes=[s.g_Mp])
        yield
        k.op("dve", lambda: V.tensor_copy(out=s.Ml[:, 0:1], in_=s.g_Mn[:, TT - 1:TT]), reads=[s.g_Mn, s.g_Mp], writes=[s.Ml])
        yield
        k.op("dve", lambda: V.tensor_scalar(out=s.g_nMp[:, 0:TT], in0=s.g_Mp[:, 0:TT], scalar1=-1.0, scalar2=None, op0=ALU.mult), reads=[s.g_Mp], writes=[s.g_nMp])
        yield
        k.op("dve", lambda: V.tensor_scalar(out=s.g_nMn[:, 0:TT], in0=s.g_Mn[:, 0:TT], scalar1=-1.0, scalar2=None, op0=ALU.mult), reads=[s.g_Mn], writes=[s.g_nMn])
        yield
        k.op("dve", lambda: V.tensor_tensor(out=s.g_dd[:, 0:TT], in0=s.g_Mp[:, 0:TT], in1=s.g_Mn[:, 0:TT], op=ALU.subtract), reads=[s.g_Mp, s.g_Mn], writes=[s.g_dd])
        for c in range(TT):
            sl_ = slice(c * 128, (c + 1) * 128)
            yield
            k.op("act", lambda c=c, sl_=sl_: A.activation(out=s.g_rows[:, 0, sl_], in_=s.g_a[:, sl_], func=AF.Exp, bias=s.g_nMp[:, c:c + 1]), reads=[s.g_a, s.g_nMp], writes=[s.g_rows])
            yield
            k.op("act", lambda c=c, sl_=sl_: A.activation(out=s.g_rows[:, 1, sl_], in_=s.g_a[:, sl_], func=AF.Exp, bias=s.g_nMn[:, c:c + 1]), reads=[s.g_a, s.g_nMn], writes=[s.g_rows])
            yield
            k.op("act", lambda c=c, sl_=sl_: A.activation(out=s.g_rows[:, 2, sl_], in_=s.g_G[:, sl_], func=AF.Exp, scale=-1.0, bias=s.g_nMp[:, c:c + 1]), reads=[s.g_G, s.g_nMp], writes=[s.g_rows])
            yield
            k.op("act", lambda c=c, sl_=sl_: A.activation(out=s.g_rows[:, 3, sl_], in_=s.g_a[:, sl_], func=AF.Exp, scale=0.0, bias=s.g_dd[:, c:c + 1]), reads=[s.g_a, s.g_dd], writes=[s.g_rows])
        for c in range(TT):
            for q in range(4):
                o = (c * 4 + q) * 4
                last = (c == TT - 1 and q == 3)
                yield
                k.op("pe", lambda c=c, q=q, o=o: nc.tensor.transpose(out=s.psG[:, o:o + 4], in_=s.g_rows[:, q, c * 128:(c + 1) * 128], identity=s.cst[0:4, 0:4]),
                     reads=[s.g_rows, s.cst], writes=[s.psG], signal=last)
        yield
        k.op("dve", lambda: V.tensor_copy(out=s.g_cols[:, :, :, :].rearrange("p c q h -> p (c q h)"), in_=s.psG[:, 0:TT * 16]), reads=[s.psG], writes=[s.g_cols])

    def mlstm_core(self, b):
        k, s, T, TT = self.k, self, self.T, self.TT
        nc = k.nc
        V, A, P = nc.vector, nc.scalar, nc.tensor
        identb = s.cstb[:, 0:128]
        mask01 = s.cstb[:, 128:256]
        it = 0
        for h in range(4):
            wv_ = s.load_w(s.wv.t[h].rearrange("(c p) j -> p c j", p=128))
            wo_ = s.load_w(s.wo.t[h].rearrange("(c p) j -> p c j", p=128))
            for tt in range(TT):
                for (w, dst, fn) in ((wv_, s.vtok, AF.Copy), (wo_, s.otok, AF.Sigmoid)):
                    ps = s.nextB()
                    for ic in range(4):
                        yield
                        k.op("pe", lambda w=w, ic=ic, tt=tt, ps=ps, h=h: P.matmul(ps[:, 0:512], lhsT=s.xmT[:, 4 * h + ic, tt * 128:(tt + 1) * 128], rhs=w[:, ic, 0:512], start=(ic == 0), stop=(ic == 3)),
                             reads=[w, (s.xmT, 4 * h + ic)], writes=[ps], signal=(ic == 3))
                    yield
                    k.op("act", lambda dst=dst, tt=tt, ps=ps, fn=fn: A.activation(out=dst[:, tt, :], in_=ps[:, 0:512], func=fn), reads=[ps], writes=[(dst, tt)])
            for c in range(TT):
                cs = slice(c * 128, (c + 1) * 128)
                col = lambda q: s.g_cols[:, c, q, h:h + 1]
                scT, kws, hb, hb2, ytok, sm = s.scT[it % 2], s.kws[it % 2], s.hb[it % 2], s.hb2[it % 2], s.ytok[it % 2], s.sm[it % 2]
                it += 1
                for dt in range(2):
                    yield
                    k.op("pe", lambda dt=dt: P.matmul(s.psS[:, 0:128], lhsT=s.kT[:, h, dt, cs], rhs=s.qT[:, h, dt, cs], start=(dt == 0), stop=(dt == 1)),
                         reads=[(s.kT, (h, dt)), (s.qT, (h, dt))], writes=[s.psS], signal=(dt == 1))
                yield
                k.op("dve", lambda: V.scalar_tensor_tensor(out=scT[:, :], in0=s.psS[:, 0:128], scalar=col(0), in1=mask01, op0=ALU.mult, op1=ALU.mult),
                     reads=[s.psS, s.g_cols, s.cstb], writes=[scT])
                psN = s.nextB()
                yield
                k.op("pe", lambda: P.matmul(psN[:, 0:512], lhsT=scT[:, :], rhs=s.vtok[:, c, :], start=True, stop=False), reads=[scT, (s.vtok, c)], writes=[psN], signal=False)
                for dt in range(2):
                    yield
                    k.op("pe", lambda dt=dt: P.matmul(psN[:, 0:512], lhsT=s.qT[:, h, dt, cs], rhs=s.CTb[:, h, dt, :], start=False, stop=(dt == 1)),
                         reads=[(s.qT, (h, dt)), (s.CTb, h)], writes=[psN], signal=(dt == 1))
                yield
                k.op("pe", lambda: P.matmul(s.psG[:, 0:1], lhsT=scT[:, :], rhs=s.onesb[:, 0:1], start=True, stop=False), reads=[scT, s.onesb], writes=[s.psG], signal=False)
                for dt in range(2):
                    yield
                    k.op("pe", lambda dt=dt: P.matmul(s.psG[:, 0:1], lhsT=s.qT[:, h, dt, cs], rhs=s.nstb[:, h, dt:dt + 1], start=False, stop=(dt == 1)),
                         reads=[(s.qT, (h, dt)), (s.nstb, h)], writes=[s.psG], signal=(dt == 1))
                yield
                k.op("dve", lambda: V.tensor_scalar(out=sm[:, 3:4], in0=s.psG[:, 0:1], scalar1=-1.0, scalar2=None, op0=ALU.mult), reads=[s.psG], writes=[sm])
                yield
                k.op("dve", lambda: V.tensor_tensor(out=sm[:, 0:1], in0=s.psG[:, 0:1], in1=sm[:, 3:4], op=ALU.max), reads=[s.psG, sm], writes=[sm])
                yield
                k.op("dve", lambda: V.tensor_scalar(out=sm[:, 0:1], in0=sm[:, 0:1], scalar1=col(2), scalar2=None, op0=ALU.max), reads=[sm, s.g_cols], writes=[sm])
                yield
                k.op("dve", lambda: V.reciprocal(out=sm[:, 1:2], in_=sm[:, 0:1]), reads=[sm], writes=[sm])
                yield
                k.op("act", lambda: A.activation(out=hb[:, :], in_=psN[:, 0:512], func=AF.Copy, scale=sm[:, 1:2]), reads=[psN, sm], writes=[hb])
                yield
                k.op("act", lambda: A.activation(out=hb2[:, :], in_=hb[:, :], func=AF.Square, accum_out=sm[:, 2:3]), reads=[hb], writes=[hb2, sm])
                yield
                s.rstd(sm, sm[:, 2:3], 512)
                yield
                k.op("dve", lambda: V.scalar_tensor_tensor(out=ytok[:, :], in0=hb[:, :], scalar=sm[:, 2:3], in1=s.otok[:, c, :], op0=ALU.mult, op1=ALU.mult),
                     reads=[hb, sm, (s.otok, c)], writes=[ytok])
                i = s.nextT()
                for vt in range(4):
                    yield
                    k.op("pe", lambda vt=vt: P.transpose(out=s.psT[i][:, vt * 128:(vt + 1) * 128], in_=ytok[:, vt * 128:(vt + 1) * 128], identity=identb),
                         reads=[ytok, s.cstb], writes=[s.psT[i]], signal=(vt == 3))
                for vt in range(4):
                    ft = 4 * h + vt
                    yield
                    k.op("dve", lambda vt=vt, ft=ft: V.scalar_tensor_tensor(out=s.xmcT[:, ft, cs], in0=s.psT[i][:, vt * 128:(vt + 1) * 128], scalar=s.mncol[:, ft:ft + 1], in1=s.szm[:, ft, cs], op0=ALU.mult, op1=ALU.mult),
                         reads=[s.psT[i], s.mncol, (s.szm, ft)], writes=[(s.xmcT, ft)])
                i2 = s.nextT()
                for dt in range(2):
                    yield
                    k.op("pe", lambda dt=dt: P.transpose(out=s.psT[i2][:, dt * 128:(dt + 1) * 128], in_=s.kT[:, h, dt, cs], identity=identb),
                         reads=[(s.kT, (h, dt)), s.cstb], writes=[s.psT[i2]], signal=(dt == 1))
                yield
                k.op("dve", lambda: V.tensor_scalar(out=kws[:, :], in0=s.psT[i2][:, 0:256], scalar1=col(1), scalar2=None, op0=ALU.mult), reads=[s.psT[i2], s.g_cols], writes=[kws])
                for dt in range(2):
                    psC = s.nextB()
                    yield
                    k.op("pe", lambda dt=dt, psC=psC: P.matmul(psC[:, 0:512], lhsT=kws[:, dt * 128:(dt + 1) * 128], rhs=s.vtok[:, c, :], start=True, stop=True), reads=[kws, (s.vtok, c)], writes=[psC])
                    yield
                    k.op("dve", lambda dt=dt, psC=psC: V.scalar_tensor_tensor(out=s.CT[:, h, dt, :], in0=s.CT[:, h, dt, :], scalar=col(3), in1=psC[:, 0:512], op0=ALU.mult, op1=ALU.add),
                         reads=[(s.CT, h), s.g_cols, psC], writes=[(s.CT, h)])
                    yield
                    k.op("act", lambda dt=dt: A.copy(out=s.CTb[:, h, dt, :], in_=s.CT[:, h, dt, :]), reads=[(s.CT, h)], writes=[(s.CTb, h)])
                    yield
                    k.op("pe", lambda dt=dt: P.matmul(s.psG[:, 8 + dt:9 + dt], lhsT=kws[:, dt * 128:(dt + 1) * 128], rhs=s.onesb[:, 0:1], start=True, stop=True), reads=[kws, s.onesb], writes=[s.psG])
                    yield
                    k.op("dve", lambda dt=dt: V.scalar_tensor_tensor(out=s.nst[:, h, dt:dt + 1], in0=s.nst[:, h, dt:dt + 1], scalar=col(3), in1=s.psG[:, 8 + dt:9 + dt], op0=ALU.mult, op1=ALU.add),
                         reads=[(s.nst, h), s.g_cols, s.psG], writes=[(s.nst, h)])
                    yield
                    k.op("act", lambda dt=dt: A.copy(out=s.nstb[:, h, dt:dt + 1], in_=s.nst[:, h, dt:dt + 1]), reads=[(s.nst, h)], writes=[(s.nstb, h)])


SSD_IN = 10304
XE_ACT = True


def host_cols_l1(p):
    def col(v):
        return np.ascontiguousarray(v.reshape(-1, 128).T)
    def col4(w):
        return np.ascontiguousarray(w.reshape(4, -1, 128).transpose(2, 1, 0).reshape(128, -1))
    cols = np.concatenate([col(p["o_norm"][0]), col4(p["o_conv_w"][0]), col(p["o_conv_b"][0]), col(p["o_gnorm"][0])], axis=1).astype(np.float32)
    rep = lambda v: np.broadcast_to(v.reshape(1, -1), (128, v.size))
    reps = np.ascontiguousarray(np.concatenate([rep(p["o_dt_bias"][0]), rep(p["o_a_log"][0]), rep(p["o_d_skip"][0])], axis=1)).astype(np.float32)
    return cols, reps


class L1(L0):
    def __init__(self, k, T, xin, xout, out):
        self.k, self.T, self.TT = k, T, T // 128
        self.xin, self.xout, self.out = xin, xout, out
        self.w_in = k.dram("o_w_in", [D, SSD_IN])
        self.w_out = k.dram("o_w_out", [4096, D])
        self.cols_d = k.dram("o_cols", [128, 288])
        self.reps_d = k.dram("o_reps", [128, 192])
        self.frep_d = k.dram("final_rep", [128, D])
        self.const_d = k.dram("consts1", [128, 512])
        self.NSLOT = 64
        self.scr = k.dram("wscr1", [self.NSLOT, 128, 4096], BF16, kind="Internal")

    def alloc(self):
        k, T, TT, s = self.k, self.T, self.TT, self
        s.cols = k.sb("cols1", [128, 288])
        s.colsh = k.sb("colsh1", [128, 288])
        s.mhalf = k.sb("mhalf1", [128, 2])
        s.SW = 256
        s.reps = k.sb("reps1", [128, 192])
        s.arep = k.sb("arep", [128, 64])
        s.frep = k.sb("frep", [128, D])
        s.cst = k.sb("cst1", [128, 512])
        s.cstb = k.sb("cstb1", [128, 128], BF16)
        s.ones = k.sb("ones1", [128, 128])
        s.tail = k.sb("tail1", [128, 48, 3])
        s.S = k.sb("S", [128, 8, 512])
        s.Sb = k.sb("Sb", [128, 8, 512], BF16)
        s.xt = [k.sb("xt1%d" % i, [128, D]) for i in range(2)]
        s.st4 = k.sb("st41", [128, 8])
        s.xnT = k.sb("xnT1", [128, NCH, T], BF16)
        s.yT = k.sb("yT1", [128, 16, T], BF16)
        s.slab = [k.sb("slab1%d" % i, [128, 16, 256], BF16) for i in range(5)]
        s.nslab = 0
        s.xe = [k.sb("xe1%d" % i, [128, T + 3]) for i in range(2)]
        s.tsets = []
        for i in range(2):
            ts = TS()
            ts.xc = k.sb("xc1%d" % i, [128, T])
            ts.th = k.sb("th1%d" % i, [128, T])
            s.tsets.append(ts)
        s.xbcT = k.sb("xbcT", [128, 48, T], BF16)
        s.xtok = k.sb("xtok", [128, TT, 8, 512], BF16)
        s.btok = k.sb("btok", [128, TT, 8, 128], BF16)
        s.dt = k.sb("dt", [128, TT, 64])
        s.t1 = k.sb("t1", [128, 64])
        s.t2 = k.sb("t2", [128, 64])
        s.dA = k.sb("dA", [128, TT, 64])
        s.Acs = k.sb("Acs", [128, TT, 64])
        s.bcol = k.sb("bcol", [128, TT, 64])
        s.dcol = k.sb("dcol", [128, TT, 64])
        s.wcol = k.sb("wcol", [128, TT, 64])
        s.crep = k.sb("crep", [128, TT, 64])
        s.wsets = []
        for i in range(2):
            W = TS()
            W.Z = k.sb("Zb%d" % i, [128, 8, 128])
            W.Lp = k.sb("Lpb%d" % i, [128, 8, 128])
            W.MT = k.sb("MTb%d" % i, [128, 8, 128], BF16)
            W.cbS = k.sb("cbS%d" % i, [128, 128])
            W.ysb = k.sb("ysb%d" % i, [128, 512])
            W.y2 = k.sb("y2%d" % i, [128, 512])
            W.szg = k.sb("szg%d" % i, [128, 512], BF16)
            W.xw = k.sb("xw%d" % i, [128, 512], BF16)
            W.ytok = k.sb("ytokL%d" % i, [128, 512], BF16)
            W.sm = k.sb("smL%d" % i, [128, 8])
            s.wsets.append(W)
        s.psA = [k.ps("qsA%d" % i, [128, 512]) for i in range(2)]
        s.npsA = 0
        s.psT = [k.ps("qsT%d" % i, [128, 1024], BF16) for i in range(2)]
        s.npsT = 0
        s.stage = [(s.xbcT.t[:, 16 * i:16 * i + 16, :].rearrange("p c t -> p (c t)").bitcast(F32), [(s.xbcT, ci) for ci in range(16 * i, 16 * i + 16)]) for i in range(2)]
        s.ipool = ([0, 1, 2], [0])
        s.opool = ([3, 4], [0])
        for i in range(2):
            W = s.wsets[i]
            W.w = s.psA[i]
            W.T = s.psT[i]
            W.L = k.ps("qsL%d" % i, [128, 512])
            W.Y = k.ps("qsY%d" % i, [128, 512])

    def setup(self):
        k, s = self.k, self
        nc = k.nc
        V, A = nc.vector, nc.scalar
        k.dma("sp", s.cols[:, :], s.cols_d[:, :], writes=[s.cols])
        k.dma("sp", s.reps[:, :], s.reps_d[:, :], writes=[s.reps])
        k.dma("sp", s.frep[:, :], s.frep_d[:, :], writes=[s.frep])
        k.dma("sp", s.cst[:, :], s.const_d[:, :], writes=[s.cst])
        k.dma("pool", s.cstb[:, :], s.const_d[:, 0:128], writes=[s.cstb])
        k.op("dve", lambda: V.memset(s.ones[:, :], 1.0), writes=[s.ones])
        k.op("dve", lambda: V.memset(s.mhalf[:, 0:1], -0.5), writes=[s.mhalf])
        k.op("dve", lambda: V.memset(s.mhalf[:, 1:2], 0.5), reads=[s.mhalf], writes=[s.mhalf])
        k.op("dve", lambda: V.tensor_scalar(out=s.colsh[:, :], in0=s.cols[:, :], scalar1=0.5, scalar2=None, op0=ALU.mult), reads=[s.cols], writes=[s.colsh])
        for b in [s.tail, s.S, s.Sb]:
            k.op("dve", (lambda b=b: V.memset(b.t[:], 0.0)), writes=[b])
        k.op("act", lambda: A.activation(out=s.arep[:, :], in_=s.reps[:, 64:128], func=AF.Exp), reads=[s.reps], writes=[s.arep])
        k.op("dve", lambda: V.tensor_scalar(out=s.arep[:, :], in0=s.arep[:, :], scalar1=-1.0, scalar2=None, op0=ALU.mult), reads=[s.arep], writes=[s.arep])

    def ssd_in(self, b):
        for _ in self.ssd_in_g(b):
            pass

    def ssd_in_g(self, b, slabpool=None):
        k, s, T, TT = self.k, self, self.T, self.TT
        nc = k.nc
        V, A, P = nc.vector, nc.scalar, nc.tensor
        wv = s.w_in.t.rearrange("(c p) f -> p c f", p=128)
        def tiles():
            for g in range(24):
                sl = s.load_slab(wv[:, :, 4096 + g * 256: 4096 + (g + 1) * 256], pool=slabpool)
                for j in range(2):
                    ci = g * 2 + j
                    yield s.conv_tile(ci, sl, j, s.tsets[ci % 2], s.tail, 16, 208, s.xbcT, xe_act=XE_ACT)
        yield from pipe_gen(tiles(), depth=2, skew=SIN_SKEW)
        sl = s.load_slab(wv[:, :, 10240:10304], pool=slabpool)
        tri = s.cst[:, 128:256]
        for tt in range(TT):
            ps = s.nextA()
            for c in range(NCH):
                k.op("pe", lambda c=c, tt=tt, ps=ps: P.matmul(ps[:, 0:64], lhsT=s.xnT[:, c, tt * 128:(tt + 1) * 128], rhs=sl[:, c, 0:64], start=(c == 0), stop=(c == NCH - 1)),
                     reads=[sl, (s.xnT, c)], writes=[ps], signal=(c == NCH - 1))
            k.op("dve", lambda ps=ps: V.tensor_tensor(out=s.t1[:, :], in0=ps[:, 0:64], in1=s.reps[:, 0:64], op=ALU.add), reads=[ps, s.reps], writes=[s.t1])
            k.op("dve", lambda: V.tensor_scalar(out=s.t2[:, :], in0=s.t1[:, :], scalar1=-1.0, scalar2=None, op0=ALU.mult), reads=[s.t1], writes=[s.t2])
            k.op("dve", lambda: V.tensor_tensor(out=s.t2[:, :], in0=s.t2[:, :], in1=s.t1[:, :], op=ALU.min), reads=[s.t1, s.t2], writes=[s.t2])
            k.op("act", lambda: A.activation(out=s.t2[:, :], in_=s.t2[:, :], func=AF.Exp), reads=[s.t2], writes=[s.t2])
            k.op("act", lambda: A.activation(out=s.t2[:, :], in_=s.t2[:, :], func=AF.Ln, bias=1.0), reads=[s.t2], writes=[s.t2])
            k.op("dve", lambda: V.tensor_scalar(out=s.t1[:, :], in0=s.t1[:, :], scalar1=0.0, scalar2=None, op0=ALU.max), reads=[s.t1], writes=[s.t1])
            k.op("dve", lambda tt=tt: V.tensor_tensor(out=s.dt[:, tt, :], in0=s.t1[:, :], in1=s.t2[:, :], op=ALU.add), reads=[s.t1, s.t2], writes=[s.dt])
            k.op("dve", lambda tt=tt: V.tensor_tensor(out=s.dA[:, tt, :], in0=s.dt[:, tt, :], in1=s.arep[:, :], op=ALU.mult), reads=[s.dt, s.arep], writes=[s.dA])
            psc = s.nextA()
            k.op("pe", lambda tt=tt, psc=psc: P.matmul(psc[:, 0:64], lhsT=tri, rhs=s.dA[:, tt, :], start=True, stop=True), reads=[s.cst, s.dA], writes=[psc])
            k.op("act", lambda tt=tt, psc=psc: A.copy(out=s.Acs[:, tt, :], in_=psc[:, 0:64]), reads=[psc], writes=[s.Acs])
            pse = s.nextA()
            k.op("pe", lambda tt=tt, pse=pse: P.matmul(pse[:, 0:64], lhsT=s.ones[:, :], rhs=s.dA[:, tt, :], start=True, stop=True), reads=[s.ones, s.dA], writes=[pse])
            k.op("act", lambda tt=tt, pse=pse: A.activation(out=s.crep[:, tt, :], in_=pse[:, 0:64], func=AF.Exp), reads=[pse], writes=[s.crep])
            k.op("act", lambda tt=tt: A.activation(out=s.t1[:, :], in_=s.dt[:, tt, :], func=AF.Ln), reads=[s.dt], writes=[s.t1])
            k.op("dve", lambda tt=tt: V.tensor_tensor(out=s.bcol[:, tt, :], in0=s.t1[:, :], in1=s.Acs[:, tt, :], op=ALU.subtract), reads=[s.t1, s.Acs], writes=[s.bcol])
            k.op("act", lambda tt=tt: A.activation(out=s.dcol[:, tt, :], in_=s.Acs[:, tt, :], func=AF.Exp), reads=[s.Acs], writes=[s.dcol])
            k.op("dve", lambda tt=tt, pse=pse: V.tensor_tensor(out=s.t2[:, :], in0=pse[:, 0:64], in1=s.bcol[:, tt, :], op=ALU.add), reads=[pse, s.bcol], writes=[s.t2])
            k.op("act", lambda tt=tt: A.activation(out=s.wcol[:, tt, :], in_=s.t2[:, :], func=AF.Exp), reads=[s.t2], writes=[s.wcol])
        yield
        identb = s.cstb[:, 0:128]
        for tt in range(TT):
            yield
            for g in range(8):
                i = s.nextT()
                for j in range(4):
                    k.op("pe", lambda tt=tt, g=g, j=j, i=i: P.transpose(out=s.psT[i][:, j * 128:(j + 1) * 128], in_=s.xbcT[:, 4 * g + j, tt * 128:(tt + 1) * 128], identity=identb),
                         reads=[(s.xbcT, 4 * g + j), s.cstb], writes=[s.psT[i]], signal=(j == 3))
                k.op("dve", lambda tt=tt, g=g, i=i: V.tensor_copy(out=s.xtok[:, tt, g, :], in_=s.psT[i][:, 0:512]), reads=[s.psT[i]], writes=[(s.xtok, (tt, g))])
            for g2 in range(2):
                i = s.nextT()
                for j in range(4):
                    k.op("pe", lambda tt=tt, g2=g2, j=j, i=i: P.transpose(out=s.psT[i][:, j * 128:(j + 1) * 128], in_=s.xbcT[:, 32 + 4 * g2 + j, tt * 128:(tt + 1) * 128], identity=identb),
                         reads=[(s.xbcT, 32 + 4 * g2 + j), s.cstb], writes=[s.psT[i]], signal=(j == 3))
                k.op("dve", lambda tt=tt, g2=g2, i=i: V.tensor_copy(out=s.btok[:, tt, 4 * g2:4 * g2 + 4, :].rearrange("p a n -> p (a n)"), in_=s.psT[i][:, 0:512]), reads=[s.psT[i]], writes=[s.btok])

    def ssd_stream(self, g, gl, zs, W):
        k, s, T, TT = self.k, self, self.T, self.TT
        nc = k.nc
        V, A, P, G = nc.vector, nc.scalar, nc.tensor, nc.gpsimd
        identb = s.cstb[:, 0:128]
        ident = s.cst[:, 0:128]
        negmaskT = s.cst[:, 384:512]
        bc3 = lambda ap, shape, ax: ap.unsqueeze(ax).to_broadcast(shape)
        v3 = lambda ap: ap.rearrange("p (e j) -> p e j", j=64)
        g8 = slice(g * 8, g * 8 + 8)
        for c in range(TT):
            cs = slice(c * 128, (c + 1) * 128)
            xg = s.xtok[:, c, g, :]
            k.op("pe", lambda: P.matmul(W.w[:, 0:128], lhsT=s.xbcT[:, 32 + g, cs], rhs=s.xbcT[:, 40 + g, cs], start=True, stop=True),
                 reads=[(s.xbcT, 32 + g), (s.xbcT, 40 + g)], writes=[W.w])
            k.op("dve", lambda: V.tensor_tensor(out=W.Z[:, :, :], in0=bc3(negmaskT, [128, 8, 128], 1), in1=bc3(s.Acs[:, c, g8], [128, 8, 128], 2), op=ALU.add),
                 reads=[s.cst, s.Acs], writes=[W.Z])
            yield
            k.op("dve", lambda: V.tensor_copy(out=W.cbS[:, :], in_=W.w[:, 0:128]), reads=[W.w], writes=[W.cbS])
            for hf in range(2):
                for e4 in range(4):
                    e = hf * 4 + e4
                    r = slice(e4 * 128, (e4 + 1) * 128)
                    k.op("pe", lambda e=e, r=r: P.transpose(out=W.L[:, r], in_=W.Z[:, e, :], identity=ident), reads=[W.Z, s.cst], writes=[W.L], signal=(e4 == 3))
                yield
                for e4 in range(4):
                    e = hf * 4 + e4
                    hh = g * 8 + e
                    r = slice(e4 * 128, (e4 + 1) * 128)
                    k.op("act", lambda e=e, r=r, hh=hh: A.activation(out=W.Lp[:, e, :], in_=W.L[:, r], func=AF.Exp, bias=s.bcol[:, c, hh:hh + 1]), reads=[W.L, s.bcol], writes=[(W.Lp, hf)])
                yield
                k.op("dve", lambda hf=hf: V.tensor_tensor(out=W.MT[:, hf * 4:(hf + 1) * 4, :], in0=W.Lp[:, hf * 4:(hf + 1) * 4, :], in1=bc3(W.cbS[:, :], [128, 4, 128], 1), op=ALU.mult),
                     reads=[(W.Lp, hf), W.cbS], writes=[(W.MT, hf)])
                yield
                for e4 in range(4):
                    e = hf * 4 + e4
                    k.op("pe", lambda e=e, hf=hf: P.matmul(W.Y[:, e * 64:(e + 1) * 64], lhsT=W.MT[:, e, :], rhs=s.xtok[:, c, g, e * 64:(e + 1) * 64], start=True, stop=True),
                         reads=[(W.MT, hf), (s.xtok, (c, g))], writes=[W.Y], signal=(e == 7))
            for hz in range(2):
                for kk in range(NCH):
                    k.op("pe", lambda kk=kk, hz=hz: P.matmul(W.w[:, hz * 256:(hz + 1) * 256], lhsT=s.xnT[:, kk, cs], rhs=zs[hz][:, kk, 0:256], start=(kk == 0), stop=(kk == NCH - 1)),
                         reads=[zs[hz], (s.xnT, kk)], writes=[W.w], signal=(kk == NCH - 1 and hz == 1))
            k.op("pool", lambda: G.tensor_tensor(out=v3(W.y2[:, :]), in0=v3(xg), in1=bc3(s.reps[:, 128 + g * 8:136 + g * 8], [128, 8, 64], 2), op=ALU.mult),
                 reads=[(s.xtok, (c, g)), s.reps], writes=[W.y2])
            yield
            k.op("act", lambda: A.activation(out=W.szg[:, :], in_=W.w[:, 0:512], func=AF.Tanh, scale=0.5), reads=[W.w], writes=[W.szg])
            yield
            k.op("dve", lambda: V.scalar_tensor_tensor(out=W.szg[:, :], in0=W.szg[:, :], scalar=1.0, in1=W.w[:, 0:512], op0=ALU.add, op1=ALU.mult), reads=[W.szg, W.w], writes=[W.szg])
            k.op("pe", lambda: P.matmul(W.w[:, 0:512], lhsT=s.xbcT[:, 40 + g, cs], rhs=s.Sb[:, g, :], start=True, stop=True), reads=[(s.xbcT, 40 + g), (s.Sb, g)], writes=[W.w])
            yield
            k.op("dve", lambda: V.tensor_tensor(out=v3(W.ysb[:, :]), in0=v3(W.w[:, 0:512]), in1=bc3(s.dcol[:, c, g8], [128, 8, 64], 2), op=ALU.mult),
                 reads=[W.w, s.dcol], writes=[W.ysb])
            k.op("dve", lambda: V.tensor_tensor(out=W.ysb[:, :], in0=W.ysb[:, :], in1=W.y2[:, :], op=ALU.add), reads=[W.ysb, W.y2], writes=[W.ysb])
            yield
            k.op("dve", lambda: V.tensor_tensor(out=W.ysb[:, :], in0=W.ysb[:, :], in1=W.Y[:, 0:512], op=ALU.add), reads=[W.ysb, W.Y], writes=[W.ysb])
            k.op("dve", lambda: V.scalar_tensor_tensor(out=W.y2[:, :], in0=W.ysb[:, :], scalar=0.5, in1=W.szg[:, :], op0=ALU.mult, op1=ALU.mult), reads=[W.ysb, W.szg], writes=[W.y2])
            k.op("pool", lambda: G.tensor_tensor(out=v3(W.xw[:, :]), in0=v3(xg), in1=bc3(s.wcol[:, c, g8], [128, 8, 64], 2), op=ALU.mult),
                 reads=[(s.xtok, (c, g)), s.wcol], writes=[W.xw])
            yield
            k.op("act", lambda: A.activation(out=W.ysb[:, :], in_=W.y2[:, :], func=AF.Square, accum_out=W.sm[:, 0:1]), reads=[W.y2], writes=[W.ysb, W.sm])
            k.op("pe", lambda: P.matmul(W.w[:, 0:512], lhsT=s.btok[:, c, g, :], rhs=W.xw[:, :], start=True, stop=True), reads=[s.btok, W.xw], writes=[W.w])
            k.op("pool", lambda: G.tensor_tensor(out=v3(s.S[:, g, :]), in0=v3(s.S[:, g, :]), in1=bc3(s.crep[:, c, g8], [128, 8, 64], 2), op=ALU.mult),
                 reads=[(s.S, g), s.crep], writes=[(s.S, g)])
            yield
            s.rstd(W.sm, W.sm[:, 0:1], 512)
            yield
            k.op("dve", lambda: V.tensor_scalar(out=W.ytok[:, :], in0=W.y2[:, :], scalar1=W.sm[:, 0:1], scalar2=None, op0=ALU.mult), reads=[W.y2, W.sm], writes=[W.ytok])
            k.op("dve", lambda: V.tensor_tensor(out=s.S[:, g, :], in0=s.S[:, g, :], in1=W.w[:, 0:512], op=ALU.add), reads=[(s.S, g), W.w], writes=[(s.S, g)])
            yield
            for vt in range(4):
                k.op("pe", lambda vt=vt: P.transpose(out=W.T[:, vt * 128:(vt + 1) * 128], in_=W.ytok[:, vt * 128:(vt + 1) * 128], identity=identb),
                     reads=[W.ytok, s.cstb], writes=[W.T], signal=(vt == 3))
            k.op("act", lambda: A.copy(out=s.Sb[:, g, :], in_=s.S[:, g, :]), reads=[(s.S, g)], writes=[(s.Sb, g)])
            yield
            for vt in range(4):
                ft = 4 * gl + vt
                gcol = 256 + 4 * g + vt
                k.op("dve", lambda vt=vt, ft=ft, gcol=gcol: V.tensor_scalar(out=s.yT[:, ft, cs], in0=W.T[:, vt * 128:(vt + 1) * 128], scalar1=s.cols[:, gcol:gcol + 1], scalar2=None, op0=ALU.mult),
                     reads=[W.T, s.cols], writes=[(s.yT, ft)])
            yield

    def ssd_core(self, b, half):
        s = self
        wv = s.w_in.t.rearrange("(c p) f -> p c f", p=128)
        for pair in range(2):
            gens = []
            for i in range(2):
                gl = pair * 2 + i
                g = half * 4 + gl
                zs = [s.load_slab(wv[:, :, g * 512 + hz * 256: g * 512 + (hz + 1) * 256]) for hz in range(2)]
                gens.append(s.ssd_stream(g, gl, zs, s.wsets[i]))
            run_pipe(gens, depth=2, skew=SSD_SKEW)

    def final(self, b):
        k, s, T, TT = self.k, self, self.T, self.TT
        nc = k.nc
        V, A = nc.vector, nc.scalar
        for tt in range(TT):
            xt = s.xt[tt]
            r0 = b * T + tt * 128
            k.op("act", lambda xt=xt, tt=tt: A.activation(out=s.xnb_all[tt][:, :], in_=xt[:, :], func=AF.Square, accum_out=s.st4[:, tt:tt + 1]), reads=[xt], writes=[s.st4, s.xnb_all[tt]])
            s.rstd(s.st4, s.st4[:, tt:tt + 1], D)
            k.op("dve", lambda xt=xt, tt=tt: V.scalar_tensor_tensor(out=xt[:, :], in0=xt[:, :], scalar=s.st4[:, tt:tt + 1], in1=s.frep[:, :], op0=ALU.mult, op1=ALU.mult), reads=[xt, s.st4, s.frep], writes=[xt])
            k.dma("pool", s.out[r0:r0 + 128, :], xt[:, :], reads=[xt], writes=[(s.out, r0)])


from concourse.bass_utils import run_bass_kernel_spmd

T_BLK = 256
NMIX = 1
L1_CUT = None


def build(S):
    k = K()
    x = k.dram("x", [S, D])
    x1 = k.dram("x1_scratch", [S, D], kind="Internal")
    x2 = k.dram("x2_scratch", [S, D], kind="Internal")
    out = k.dram("out", [S, D], kind="ExternalOutput")
    nb = S // T_BLK
    k.les = contextlib.ExitStack()
    A0 = L0(k, T_BLK, x, x1)
    A0.alloc()
    A0.setup()
    for b in range(nb):
        A0.rmsnorm_T(b)
        A0.mlstm_in(b)
        A0.lru_mlstm(b)
        A0.outproj(b, 0)
        A0.outproj(b, 1, src=A0.xmcT)
    k.barrier()
    k.les.close()
    k.les = contextlib.ExitStack()
    A1 = L1(k, T_BLK, x1, x2, out)
    A1.alloc()
    A1.setup()
    k.cut = L1_CUT
    def chain(*gs):
        for g in gs:
            yield from g
    obanks = [A1.wsets[0].L, A1.wsets[1].L, A1.wsets[0].Y, A1.wsets[1].Y]
    for _ in A1.rmsnorm_g(0, A1.stage):
        pass
    A1.load_resid(0)
    A1.ssd_in(0)
    for b in range(nb):
        A1.ssd_core(b, 0)
        if b > 0:
            A1.load_resid(b)
        A1.outproj(b, 0)
        A1.ssd_core(b, 1)
        if b + 1 < nb:
            mix(A1.outproj_g(b, 1, store=False, banks=obanks, slabpool=A1.opool),
                chain(A1.rmsnorm_g(b + 1, A1.stage), A1.ssd_in_g(b + 1, slabpool=A1.ipool)), NMIX)
        else:
            A1.outproj(b, 1, store=False)
        A1.final(b)
    k.cut = None
    k.finish([out])
    k.barrier()
    k.les.close()
    return k


def make_maps(p, xs):
    cols1, reps1 = host_cols_l1(p)
    base = {"e_w_in": p["e_w_in"][0], "e_w_out": p["e_w_out"][0], "e_lru_wa": p["e_lru_wa"][0], "e_lru_wx": p["e_lru_wx"][0],
            "e_w_q": p["e_w_q"][0], "e_w_k": p["e_w_k"][0], "e_w_v": p["e_w_v"][0], "e_w_o": p["e_w_o"][0], "e_w_if": p["e_w_if"][0],
            "e_cols": host_cols_l0(p), "e_bif": np.ascontiguousarray(p["e_b_if"][0].reshape(2, 4).T),
            "e_mncol": np.ascontiguousarray(p["e_m_norm"][0].reshape(16, 128).T), "consts": consts_host(), "consts1": consts_host(),
            "o_w_in": p["o_w_in"][0], "o_w_out": p["o_w_out"][0], "o_cols": cols1, "o_reps": reps1,
            "final_rep": np.ascontiguousarray(np.broadcast_to(p["final_norm"].reshape(1, D), (128, D)))}
    base = {kk: np.ascontiguousarray(np.asarray(v, dtype=np.float32)) for kk, v in base.items()}
    return [dict(base, x=np.ascontiguousarray(x_)) for x_ in xs]


def kernel(**inputs):
    p = {kk: np.asarray(v) for kk, v in inputs.items()}
    x = p["x"]
    B, S, _ = x.shape
    k = build(S)
    maps = make_maps(p, [x[c % B] for c in range(8)])
    res = run_bass_kernel_spmd(k.nc, maps, core_ids=list(range(8)))
    return np.stack([res.results[b]["out"] for b in range(B)], axis=0).astype(np.float32)
```

```python
import contextlib
import numpy as np
import concourse.bass as bass
import concourse.mybir as mybir

F32 = mybir.dt.float32
BF16 = mybir.dt.bfloat16
AF = mybir.ActivationFunctionType
ALU = mybir.AluOpType
AX = mybir.AxisListType

NDMA_SEM = 6


class Buf:
    def __init__(self, name, t):
        self.name = name
        self.t = t
        self.st = {}

    def __getitem__(self, idx):
        return self.t[idx]


class BufView(Buf):
    def __init__(self, base, ap):
        self.name = base.name
        self.t = ap
        self.st = base.st
        if getattr(base, "excl", False):
            self.excl = True


class K:
    def __init__(self):
        self.nc = bass.Bass("TRN2", target_bir_lowering=False)
        self.es = contextlib.ExitStack()
        nc = self.nc
        self.eng = {"pe": nc.tensor, "act": nc.scalar, "dve": nc.vector, "pool": nc.gpsimd, "sp": nc.sync}
        self.csem = {e: self.es.enter_context(nc.semaphore("s_" + e)) for e in ["pe", "act", "dve", "pool"]}
        self.ccnt = {e: 0 for e in self.csem}
        self.dsem = {q: [self.es.enter_context(nc.semaphore("d_%s%d" % (q, i))) for i in range(NDMA_SEM)]
                     for q in ["sp", "pool"]}
        self.dcnt = {q: 0 for q in self.dsem}
        self.dtok = {q: [None] * NDMA_SEM for q in self.dsem}
        self.seen = {e: {} for e in self.eng}
        self.pe_pending = []
        self.nbuf = 0
        self.ninst = 0

    def sb(self, name, shape, dt=F32):
        t = getattr(self, "les", self.es).enter_context(self.nc.sbuf_tensor(name, list(shape), dt))
        return Buf(name, t)

    def ps(self, name, shape, dt=F32):
        t = getattr(self, "les", self.es).enter_context(self.nc.psum_tensor(name, list(shape), dt))
        b = Buf(name, t)
        b.excl = True
        return b

    def dram(self, name, shape, dt=F32, kind="ExternalInput"):
        t = self.nc.dram_tensor(name, list(shape), dt, kind=kind)
        b = Buf(name, t.ap())
        return b

    def _wait(self, e, tok):
        if tok is None:
            return
        sem, val, semname = tok
        if self.seen[e].get(semname, 0) >= val:
            return
        self.eng[e].wait_ge(sem, val)
        self.seen[e][semname] = val

    def _deps(self, e, reads, writes):
        toks = []
        for (b, k) in reads:
            s = b.st.get(k)
            if s and s[0] is not None:
                toks.append(s[0])
            if s and getattr(b, "excl", False):
                toks.extend(t for (eng, t) in s[1].items() if eng != e)
        for (b, k) in writes:
            s = b.st.get(k)
            if s:
                if s[0] is not None:
                    toks.append(s[0])
                toks.extend(s[1].values())
        return toks

    def _commit(self, e, tok, reads, writes):
        for (b, k) in reads:
            s = b.st.setdefault(k, [None, {}])
            s[1][e] = tok
        for (b, k) in writes:
            b.st[k] = [tok, {}]

    def op(self, e, fn, reads=(), writes=(), signal=True):
        if getattr(self, "cut", None) is not None:
            if self.cut <= 0:
                return None
            self.cut -= 1
        reads = [r if isinstance(r, tuple) else (r, 0) for r in reads]
        writes = [w if isinstance(w, tuple) else (w, 0) for w in writes]
        for tok in self._deps(e, reads, writes):
            if e == "pe" and tok[2] == "c_pe":
                continue
            self._wait(e, tok)
        ins = fn()
        self.ninst += 1
        if e == "pe" and not signal:
            self.pe_pending.append((reads, writes))
            return ins
        self.ccnt[e] += 1
        tok = (self.csem[e], self.ccnt[e], "c_" + e)
        ins.then_inc(self.csem[e], 1)
        if e == "pe":
            for (r, w) in self.pe_pending:
                self._commit(e, tok, r, w)
            self.pe_pending = []
        self._commit(e, tok, reads, writes)
        return ins

    def dma(self, q, out_ap, in_ap, reads=(), writes=()):
        reads = [r if isinstance(r, tuple) else (r, 0) for r in reads]
        writes = [w if isinstance(w, tuple) else (w, 0) for w in writes]
        if getattr(self, "cut", None) is not None and self.cut <= 0:
            return None
        for tok in self._deps(q, reads, writes):
            self._wait(q, tok)
        i = self.dcnt[q] % NDMA_SEM
        self._wait(q, self.dtok[q][i])
        n = self.dcnt[q] // NDMA_SEM + 1
        self.dcnt[q] += 1
        sem = self.dsem[q][i]
        ins = self.eng[q].dma_start(out=out_ap, in_=in_ap)
        ins.then_inc(sem, 16)
        self.ninst += 1
        tok = (sem, 16 * n, "d_%s%d" % (q, i))
        self.dtok[q][i] = tok
        self._commit(q, tok, reads, writes)
        return ins

    def finish(self, out_bufs):
        for b in out_bufs:
            for k, s in b.st.items():
                self._wait("sp", s[0])

    def barrier(self):
        toks = [(self.csem[e], self.ccnt[e], "c_" + e) for e in self.csem if self.ccnt[e] > 0]
        for q in self.dtok:
            toks += [t for t in self.dtok[q] if t is not None]
        for e in self.eng:
            for t in toks:
                self._wait(e, t)


class PV:
    def __init__(self, buf, ap):
        self.buf, self.ap = buf, ap


class TS:
    pass


def pipe_gen(gens, depth=2, skew=3):
    it = iter(gens)
    active = []
    since = skew
    done = False
    while True:
        if not done and len(active) < depth and since >= skew:
            try:
                active.append(next(it))
                since = 0
            except StopIteration:
                done = True
        if not active:
            if done:
                break
            since = skew
            continue
        for g in list(active):
            try:
                next(g)
            except StopIteration:
                active.remove(g)
        since += 1
        yield


def run_pipe(gens, depth=2, skew=3):
    for _ in pipe_gen(gens, depth, skew):
        pass


def mix(ga, gb, nb):
    da = db = False
    while not (da and db):
        if not da:
            try:
                next(ga)
            except StopIteration:
                da = True
        for _ in range(nb if not da else 1000000):
            if db:
                break
            try:
                next(gb)
            except StopIteration:
                db = True

D = 2048
NCH = 16
EVEN_IN = 8192
EPS = 1e-6
LRU_SKEW = 3
SSD_SKEW = 1
MIN_SKEW = 2
SIN_SKEW = 2
ML_STEPS = 6


def host_cols_l0(p):
    def col(v):
        return np.ascontiguousarray(v.reshape(-1, 128).T)
    def col4(w):
        return np.ascontiguousarray(w.reshape(4, -1, 128).transpose(2, 1, 0).reshape(128, -1))
    cols = np.concatenate([
        col(p["e_norm"][0]),
        col4(p["e_conv_l_w"][0]),
        col(p["e_conv_l_b"][0]),
        col(p["e_lru_ba"][0]),
        col(p["e_lru_bx"][0]),
        col(p["e_lru_lam"][0]),
        col4(p["e_conv_m_w"][0]),
        col(p["e_conv_m_b"][0]),
    ], axis=1).astype(np.float32)
    return cols


def consts_host():
    ident = np.eye(128, dtype=np.float32)
    mask01 = np.triu(np.ones((128, 128), np.float32))
    negmask = np.where(mask01 > 0, 0.0, -30000.0).astype(np.float32)
    return np.concatenate([ident, mask01, negmask, np.ascontiguousarray(negmask.T)], axis=1)


class L0:
    def __init__(self, k, T, xin, xout):
        self.k = k
        self.T = T
        self.TT = T // 128
        self.xin, self.xout = xin, xout
        nc = k.nc
        self.w_in = k.dram("e_w_in", [D, EVEN_IN])
        self.w_out = k.dram("e_w_out", [4096, D])
        self.wa = k.dram("e_lru_wa", [16, 128, 128])
        self.wx = k.dram("e_lru_wx", [16, 128, 128])
        self.wq = k.dram("e_w_q", [4, 512, 256])
        self.wk = k.dram("e_w_k", [4, 512, 256])
        self.wv = k.dram("e_w_v", [4, 512, 512])
        self.wo = k.dram("e_w_o", [4, 512, 512])
        self.wif = k.dram("e_w_if", [4096, 8])
        self.cols_d = k.dram("e_cols", [128, 224])
        self.bif_d = k.dram("e_bif", [4, 2])
        self.mnorm_d = k.dram("e_mncol", [128, 16])
        self.const_d = k.dram("consts", [128, 512])
        self.NSLOT = 72
        self.scr = k.dram("wscr0", [self.NSLOT, 128, 4096], BF16, kind="Internal")

    def alloc(self):
        k, T, TT = self.k, self.T, self.TT
        s = self
        s.cols = k.sb("cols", [128, 224])
        s.bif = k.sb("bif", [4, 2])
        s.mncol = k.sb("mncol", [128, 16])
        s.cst = k.sb("cst", [128, 512])
        s.cstb = k.sb("cstb", [128, 512], BF16)
        s.ones = k.sb("ones", [128, 512])
        s.onesb = k.sb("onesb", [128, 8], BF16)
        s.kcol = k.sb("kcol", [128, 32])
        s.colsh = k.sb("colsh", [128, 224])
        s.mhalf = k.sb("mhalf", [128, 2])
        s.phalf = k.sb("phalf", [128, T])
        s.wab = k.sb("wab", [128, 16, 128], BF16)
        s.wxb = k.sb("wxb", [128, 16, 128], BF16)
        s.wifb = k.sb("wifb", [128, 32, 8], BF16)
        s.tail_l = k.sb("tail_l", [128, 16, 3])
        s.tail_m = k.sb("tail_m", [128, 16, 3])
        s.hlast = k.sb("hlast", [128, 16])
        s.CT = k.sb("CT", [128, 4, 2, 512])
        s.CTb = k.sb("CTb", [128, 4, 2, 512], BF16)
        s.nst = k.sb("nst", [128, 4, 2])
        s.nstb = k.sb("nstb", [128, 4, 2], BF16)
        s.Gl = k.sb("Gl", [4, 1])
        s.Ml = k.sb("Ml", [4, 1])
        s.xt = [k.sb("xt%d" % i, [128, D]) for i in range(2)]
        s.st4 = k.sb("st4", [128, 8])
        s.xnT = k.sb("xnT", [128, NCH, T], BF16)
        s.yT = k.sb("yT", [128, 16, T], BF16)
        s.SW = 256
        s.slab = [k.sb("slab%d" % i, [128, 16, 256], BF16) for i in range(4)]
        s.nslab = 0
        s.xe = [k.sb("xe%d" % i, [128, T + 3]) for i in range(2)]
        s.tsets = []
        for i in range(2):
            ts = TS()
            ts.xc = k.sb("xc%d" % i, [128, T])
            ts.xcb = k.sb("xcb%d" % i, [128, T], BF16)
            for nm in ["rr", "ii", "aa", "a2", "hh", "sz1", "th"]:
                setattr(ts, nm, k.sb("%s%d" % (nm, i), [128, T]))
            s.tsets.append(ts)
        s.xmT = k.sb("xmT", [128, 16, T], BF16)
        s.xmcT = k.sb("xmcT", [128, 16, T], BF16)
        s.szm = k.sb("szm", [128, 16, T], BF16)
        s.qT = k.sb("qT", [128, 4, 2, T], BF16)
        s.kT = k.sb("kT", [128, 4, 2, T], BF16)
        s.vTt = [k.sb("vTt%d" % i, [128, T], BF16) for i in range(2)]
        s.vtok = k.sb("vtok", [128, TT, 512], BF16)
        s.otok = k.sb("otok", [128, TT, 512], BF16)
        s.wsl = [k.sb("wsl%d" % i, [128, 4, 512], BF16) for i in range(2)]
        s.g_ig = k.sb("g_ig", [4, T])
        s.g_t1 = k.sb("g_t1", [4, T])
        s.g_t2 = k.sb("g_t2", [4, T])
        s.g_lf = k.sb("g_lf", [4, T])
        s.g_G = k.sb("g_G", [4, T])
        s.g_a = k.sb("g_a", [4, T])
        s.g_cm = k.sb("g_cm", [4, 8])
        s.g_Mn = k.sb("g_Mn", [4, 8])
        s.g_Mp = k.sb("g_Mp", [4, 8])
        s.g_nMp = k.sb("g_nMp", [4, 8])
        s.g_nMn = k.sb("g_nMn", [4, 8])
        s.g_dd = k.sb("g_dd", [4, 8])
        s.g_rows = k.sb("g_rows", [4, 4, T])
        s.g_cols = k.sb("g_cols", [128, TT, 4, 4])
        s.scT = [k.sb("scT%d" % i, [128, 128], BF16) for i in range(2)]
        s.kws = [k.sb("kws%d" % i, [128, 256], BF16) for i in range(2)]
        s.hb = [k.sb("hb%d" % i, [128, 512]) for i in range(2)]
        s.hb2 = [k.sb("hb2%d" % i, [128, 512]) for i in range(2)]
        s.ytok = [k.sb("ytok%d" % i, [128, 512], BF16) for i in range(2)]
        s.sm = [k.sb("sm%d" % i, [128, 8]) for i in range(2)]
        s.psA = [k.ps("psA%d" % i, [128, 512]) for i in range(4)]
        s.npsA = 0
        s.npsB = 0
        s.psT = [k.ps("psT%d" % i, [128, 1024], BF16) for i in range(2)]
        s.npsT = 0
        s.psS = k.ps("psS", [128, 512])
        s.psG = k.ps("psG", [128, 512])

    def rstd(self, buf, ap, n):
        k, s = self.k, self
        nc = k.nc
        P = ap.shape[0]
        k.op("dve", lambda: nc.vector.tensor_scalar(out=ap, in0=ap, scalar1=1.0 / n, scalar2=EPS, op0=ALU.mult, op1=ALU.add), reads=[buf], writes=[buf])
        k.op("act", lambda: nc.scalar.activation(out=ap, in_=ap, func=AF.Ln), reads=[buf], writes=[buf])
        k.op("act", lambda: nc.scalar.activation(out=ap, in_=ap, func=AF.Exp, scale=-0.5), reads=[buf], writes=[buf])

    def nextA(self):
        b = self.psA[self.npsA % len(self.psA)]
        self.npsA += 1
        return b

    def nextB(self):
        b = self.psA[2 + self.npsB % 2]
        self.npsB += 1
        return b

    def nextT(self):
        i = self.npsT % 2
        self.npsT += 1
        return i

    def _cached_load(self, dst, src_ap):
        k, s = self.k, self
        if not hasattr(s, "wcache"):
            s.wcache = {}
            s.nq = 0
        shp = tuple(src_ap.shape)
        key = (src_ap.name, src_ap.offset, shp)
        dview = dst.t[:, 0:shp[1], 0:shp[2]]
        if key not in s.wcache:
            slot = len(s.wcache)
            assert slot < s.NSLOT, slot
            s.wcache[key] = slot
            k.dma("pool", dview, src_ap, writes=[dst])
            sv = s.scr.t[slot][:, 0:shp[1] * shp[2]].rearrange("p (c f) -> p c f", f=shp[2])
            k.dma("sp", sv, dview, reads=[dst], writes=[(s.scr, slot)])
        else:
            slot = s.wcache[key]
            sv = s.scr.t[slot][:, 0:shp[1] * shp[2]].rearrange("p (c f) -> p c f", f=shp[2])
            k.dma("sp", dview, sv, reads=[(s.scr, slot)], writes=[dst])

    def load_slab(self, src_ap, pool=None):
        if pool is None:
            sl = self.slab[self.nslab % len(self.slab)]
            self.nslab += 1
        else:
            idx, cnt = pool
            sl = self.slab[idx[cnt[0] % len(idx)]]
            cnt[0] += 1
        self._cached_load(sl, src_ap)
        return sl

    def setup(self):
        k, s = self.k, self
        nc = k.nc
        k.dma("sp", s.cols[:, :], s.cols_d[:, :], writes=[s.cols])
        k.dma("sp", s.bif[:, :], s.bif_d[:, :], writes=[s.bif])
        k.dma("sp", s.mncol[:, :], s.mnorm_d[:, :], writes=[s.mncol])
        k.dma("sp", s.cst[:, :], s.const_d[:, :], writes=[s.cst])
        k.dma("pool", s.cstb[:, :], s.const_d[:, :], writes=[s.cstb])
        k.dma("pool", s.wab[:, :, :], s.wa.t.rearrange("n i j -> i n j"), writes=[s.wab])
        k.dma("pool", s.wxb[:, :, :], s.wx.t.rearrange("n i j -> i n j"), writes=[s.wxb])
        k.dma("pool", s.wifb[:, :, :], s.wif.t.rearrange("(c p) g -> p c g", p=128), writes=[s.wifb])
        k.op("dve", lambda: nc.vector.memset(s.ones[:, :], 1.0), writes=[s.ones])
        k.op("dve", lambda: nc.vector.memset(s.onesb[:, :], 1.0), writes=[s.onesb])
        k.op("dve", lambda: nc.vector.memset(s.phalf[:, :], 0.5), writes=[s.phalf])
        k.op("dve", lambda: nc.vector.memset(s.mhalf[:, 0:1], -0.5), writes=[s.mhalf])
        k.op("dve", lambda: nc.vector.memset(s.mhalf[:, 1:2], 0.5), reads=[s.mhalf], writes=[s.mhalf])
        k.op("dve", lambda: nc.vector.tensor_scalar(out=s.colsh[:, :], in0=s.cols[:, :], scalar1=0.5, scalar2=None, op0=ALU.mult), reads=[s.cols], writes=[s.colsh])
        k.op("dve", lambda: nc.vector.tensor_scalar(out=s.mncol[:, :], in0=s.mncol[:, :], scalar1=0.5, scalar2=None, op0=ALU.mult), reads=[s.mncol], writes=[s.mncol])
        for b in [s.tail_l, s.tail_m, s.hlast, s.CT, s.CTb, s.nst, s.nstb, s.Gl, s.Ml]:
            k.op("dve", (lambda b=b: nc.vector.memset(b.t[:], 0.0)), writes=[b])
        lam = s.cols[:, 128:144]
        k.op("act", lambda: nc.scalar.activation(out=s.kcol[:, 0:16], in_=lam, func=AF.Exp, scale=-1.0),
             reads=[s.cols], writes=[s.kcol])
        k.op("act", lambda: nc.scalar.activation(out=s.kcol[:, 0:16], in_=s.kcol[:, 0:16], func=AF.Ln, bias=1.0),
             reads=[s.kcol], writes=[s.kcol])
        k.op("dve", lambda: nc.vector.tensor_scalar(out=s.kcol[:, 0:16], in0=s.kcol[:, 0:16], scalar1=-4.0, scalar2=None, op0=ALU.mult),
             reads=[s.kcol], writes=[s.kcol])

    def rmsnorm_g(self, b, stage=None):
        k, s, T, TT = self.k, self, self.T, self.TT
        nc = k.nc
        if not hasattr(s, "xnb_all"):
            s.xnb_all = [k.sb(s.__class__.__name__ + "xnba%d" % i, [128, D], BF16) for i in range(TT)]
        for tt in range(TT):
            r0 = b * T + tt * 128
            if stage is None:
                src, keys = s.xt[tt][:, :], [s.xt[tt]]
            else:
                src, keys = stage[tt]
            k.dma("sp", src, s.xin[r0:r0 + 128, :], reads=[(s.xin, r0)], writes=keys)
            yield
            k.op("act", lambda src=src, tt=tt: nc.scalar.activation(out=s.xnb_all[tt][:, :], in_=src, func=AF.Square, accum_out=s.st4[:, tt:tt + 1]),
                 reads=keys, writes=[s.st4, s.xnb_all[tt]])
            yield
            s.rstd(s.st4, s.st4[:, tt:tt + 1], D)
            yield
            k.op("act", lambda src=src, tt=tt: nc.scalar.activation(out=s.xnb_all[tt][:, :], in_=src, func=AF.Copy, scale=s.st4[:, tt:tt + 1]),
                 reads=keys + [s.st4], writes=[s.xnb_all[tt]])
            yield
        ident = s.cstb[:, 0:128]
        for c in range(NCH):
            i = s.nextT()
            for tt in range(TT):
                k.op("pe", lambda c=c, tt=tt, i=i: nc.tensor.transpose(out=s.psT[i][:, tt * 128:(tt + 1) * 128], in_=s.xnb_all[tt][:, c * 128:(c + 1) * 128], identity=ident),
                     reads=[s.xnb_all[tt], s.cstb], writes=[s.psT[i]], signal=(tt == TT - 1))
            k.op("dve", lambda c=c, i=i: nc.vector.tensor_scalar(out=s.xnT[:, c, :], in0=s.psT[i][:, 0:T], scalar1=s.cols[:, c:c + 1], scalar2=None, op0=ALU.mult),
                 reads=[s.psT[i], s.cols], writes=[(s.xnT, c)])
            yield

    def rmsnorm_T(self, b):
        for _ in self.rmsnorm_g(b):
            pass

    def load_resid(self, b):
        k, s, T, TT = self.k, self, self.T, self.TT
        for tt in range(TT):
            r0 = b * T + tt * 128
            k.dma("sp", s.xt[tt][:, :], s.xin[r0:r0 + 128, :], reads=[(s.xin, r0)], writes=[s.xt[tt]])

    def inproj_tile(self, sl, j, ps):
        k, s, T = self.k, self, self.T
        nc = k.nc
        for c in range(NCH):
            k.op("pe", lambda c=c: nc.tensor.matmul(ps[:, 0:T], lhsT=sl[:, c, j * 128:(j + 1) * 128], rhs=s.xnT[:, c, :], start=(c == 0), stop=(c == NCH - 1)),
                 reads=[sl, (s.xnT, c)], writes=[ps], signal=(c == NCH - 1))

    def conv_g(self, ps, xe, tail, ci, wbase, bbase, out, cols=None, offload=False, xe_act=False):
        k, s, T = self.k, self, self.T
        nc = k.nc
        cols = s.cols if cols is None else cols
        w = lambda tap: cols[:, wbase + ci * 4 + tap: wbase + ci * 4 + tap + 1]
        k.op("dve", lambda: nc.vector.tensor_copy(out=xe[:, 0:3], in_=tail[:, ci, :]), reads=[(tail, ci)], writes=[xe])
        if offload or xe_act:
            k.op("act", lambda: nc.scalar.copy(out=xe[:, 3:T + 3], in_=ps.ap), reads=[ps.buf, xe], writes=[xe])
        else:
            k.op("dve", lambda: nc.vector.tensor_copy(out=xe[:, 3:T + 3], in_=ps.ap), reads=[ps.buf, xe], writes=[xe])
        k.op("act", lambda: nc.scalar.activation(out=out[:, :], in_=ps.ap, func=AF.Identity, scale=w(3), bias=cols[:, bbase + ci: bbase + ci + 1]),
             reads=[ps.buf, cols], writes=[out])
        yield
        for tap in range(3):
            k.op("dve", lambda tap=tap: nc.vector.scalar_tensor_tensor(out=out[:, :], in0=xe[:, tap:tap + T], scalar=w(tap), in1=out[:, :], op0=ALU.mult, op1=ALU.add),
                 reads=[xe, cols, out], writes=[out])
        if offload:
            k.op("pool", lambda: nc.gpsimd.tensor_copy(out=tail[:, ci, :], in_=xe[:, T:T + 3]), reads=[xe], writes=[(tail, ci)])
        else:
            k.op("dve", lambda: nc.vector.tensor_copy(out=tail[:, ci, :], in_=xe[:, T:T + 3]), reads=[xe], writes=[(tail, ci)])
        yield

    def inproj_view(self, sl, j, ps, half):
        k, s, T = self.k, self, self.T
        nc = k.nc
        v = PV(ps, ps[:, half * T:(half + 1) * T])
        for c in range(NCH):
            k.op("pe", lambda c=c: nc.tensor.matmul(v.ap, lhsT=sl[:, c, j * 128:(j + 1) * 128], rhs=s.xnT[:, c, :], start=(c == 0), stop=(c == NCH - 1)),
                 reads=[sl, (s.xnT, c)], writes=[ps], signal=(c == NCH - 1))
        return v

    def lru_tile(self, ci, slx, slz, jx, ts):
        k, s, T = self.k, self, self.T
        nc = k.nc
        V, A, G = nc.vector, nc.scalar, nc.gpsimd
        p1 = s.psA[ci % 2]
        vx = s.inproj_view(slx, jx, p1, 0)
        vz = s.inproj_view(slz, jx, p1, 1)
        yield
        xe = s.xe[ci % 2]
        yield from s.conv_g(vx, xe, s.tail_l, ci, 16, 80, ts.xc)
        k.op("pool", lambda: G.tensor_copy(out=ts.xcb[:, :], in_=ts.xc[:, :]), reads=[ts.xc], writes=[ts.xcb])
        k.op("act", lambda: A.activation(out=ts.sz1[:, :], in_=vz.ap, func=AF.Tanh, scale=0.5), reads=[p1], writes=[ts.sz1])
        k.op("dve", lambda: V.scalar_tensor_tensor(out=ts.sz1[:, :], in0=ts.sz1[:, :], scalar=1.0, in1=vz.ap, op0=ALU.add, op1=ALU.mult), reads=[ts.sz1, p1], writes=[ts.sz1])
        yield
        p2 = p1
        k.op("pe", lambda: nc.tensor.matmul(p2[:, 0:T], lhsT=s.wab[:, ci, :], rhs=ts.xcb[:, :], start=True, stop=True),
             reads=[s.wab, ts.xcb], writes=[p2], signal=False)
        k.op("pe", lambda: nc.tensor.matmul(p2[:, T:2 * T], lhsT=s.wxb[:, ci, :], rhs=ts.xcb[:, :], start=True, stop=True),
             reads=[s.wxb, ts.xcb], writes=[p2])
        yield
        k.op("act", lambda: A.activation(out=ts.rr[:, :], in_=p2[:, 0:T], func=AF.Tanh, scale=0.5, bias=s.colsh[:, 96 + ci:97 + ci]),
             reads=[p2, s.colsh], writes=[ts.rr])
        k.op("act", lambda: A.activation(out=ts.ii[:, :], in_=p2[:, T:2 * T], func=AF.Tanh, scale=0.5, bias=s.colsh[:, 112 + ci:113 + ci]),
             reads=[p2, s.colsh], writes=[ts.ii])
        k.op("act", lambda: A.activation(out=ts.aa[:, :], in_=ts.rr[:, :], func=AF.Exp, scale=s.kcol[:, ci:ci + 1], bias=s.kcol[:, ci:ci + 1]),
             reads=[ts.rr, s.kcol], writes=[ts.aa])
        yield
        k.op("pool", lambda: G.tensor_tensor(out=ts.a2[:, :], in0=ts.aa[:, :], in1=ts.aa[:, :], op=ALU.mult), reads=[ts.aa], writes=[ts.a2])
        k.op("act", lambda: A.activation(out=ts.a2[:, :], in_=ts.a2[:, :], func=AF.Sqrt, scale=-1.0, bias=1.0), reads=[ts.a2], writes=[ts.a2])
        k.op("dve", lambda: V.scalar_tensor_tensor(out=ts.ii[:, :], in0=ts.ii[:, :], scalar=1.0, in1=ts.xc[:, :], op0=ALU.add, op1=ALU.mult),
             reads=[ts.ii, ts.xc], writes=[ts.ii])
        yield
        k.op("dve", lambda: V.scalar_tensor_tensor(out=ts.ii[:, :], in0=ts.ii[:, :], scalar=0.5, in1=ts.a2[:, :], op0=ALU.mult, op1=ALU.mult),
             reads=[ts.ii, ts.a2], writes=[ts.ii])
        k.op("dve", lambda: V.tensor_tensor_scan(out=ts.hh[:, :], data0=ts.aa[:, :], data1=ts.ii[:, :], initial=s.hlast[:, ci:ci + 1], op0=ALU.mult, op1=ALU.add),
             reads=[ts.aa, ts.ii, s.hlast], writes=[ts.hh])
        k.op("dve", lambda: V.tensor_copy(out=s.hlast[:, ci:ci + 1], in_=ts.hh[:, T - 1:T]),
             reads=[ts.hh], writes=[s.hlast])
        k.op("dve", lambda: V.scalar_tensor_tensor(out=s.yT[:, ci, :], in0=ts.hh[:, :], scalar=0.5, in1=ts.sz1[:, :], op0=ALU.mult, op1=ALU.mult),
             reads=[ts.hh, ts.sz1], writes=[(s.yT, ci)])
        yield

    def lru_tiles(self, b):
        s = self
        wv = s.w_in.t.rearrange("(c p) f -> p c f", p=128)
        SW = s.SW
        nt = SW // 128
        for g in range(2048 // SW):
            slx = s.load_slab(wv[:, :, g * SW:(g + 1) * SW])
            slz = s.load_slab(wv[:, :, 4096 + g * SW: 4096 + (g + 1) * SW])
            for j in range(nt):
                ci = g * nt + j
                yield s.lru_tile(ci, slx, slz, j, s.tsets[ci % 2])

    def lru(self, b):
        run_pipe(self.lru_tiles(b), depth=2, skew=LRU_SKEW)

    def lru_mlstm(self, b):
        s = self

        def ml():
            yield from s.mlstm_gates(b)
            yield from s.mlstm_core(b)
        mix(pipe_gen(s.lru_tiles(b), depth=2, skew=LRU_SKEW), ml(), ML_STEPS)

    def outproj_g(self, b, half, store=True, src=None, banks=None, slabpool=None):
        k, s, T, TT = self.k, self, self.T, self.TT
        nc = k.nc
        wv = s.w_out.t.rearrange("(c p) d -> p c d", p=128)
        yT = s.yT if src is None else src
        nb = 0
        for db in range(8):
            sl = s.load_slab(wv[:, half * 16:(half + 1) * 16, db * 256:(db + 1) * 256], pool=slabpool)
            for tt in range(TT):
                xt = s.xt[tt]
                if banks is None:
                    ps = s.nextA()
                else:
                    ps = banks[nb % len(banks)]
                    nb += 1
                for c in range(16):
                    k.op("pe", lambda sl=sl, c=c, tt=tt, ps=ps: nc.tensor.matmul(ps[:, 0:256], lhsT=yT[:, c, tt * 128:(tt + 1) * 128], rhs=sl[:, c, 0:256], start=(c == 0), stop=(c == 15)),
                         reads=[sl, (yT, c)], writes=[ps], signal=(c == 15))
                    if c % 4 == 3:
                        yield
                k.op("dve", lambda xt=xt, ps=ps, db=db: nc.vector.tensor_tensor(out=xt[:, db * 256:(db + 1) * 256], in0=xt[:, db * 256:(db + 1) * 256], in1=ps[:, 0:256], op=ALU.add),
                     reads=[ps, xt], writes=[xt])
        if half == 1 and store:
            for tt in range(TT):
                r0 = b * T + tt * 128
                k.dma("pool", s.xout[r0:r0 + 128, :], s.xt[tt][:, :], reads=[s.xt[tt]], writes=[(s.xout, r0)])

    def outproj(self, b, half, store=True, src=None):
        for _ in self.outproj_g(b, half, store=store, src=src):
            pass

    def conv_tile(self, ci, sl, j, ts, tail, wbase, bbase, dst, raw_dst=None, xe_act=False):
        k, s, T = self.k, self, self.T
        nc = k.nc
        ps = s.nextA()
        v = s.inproj_view(sl, j, ps, 0)
        yield
        if raw_dst is not None:
            k.op("act", lambda: nc.scalar.copy(out=raw_dst[:, ci, :], in_=v.ap), reads=[ps], writes=[(raw_dst, ci)])
        yield from s.conv_g(v, s.xe[ci % 2], tail, ci, wbase, bbase, ts.xc, cols=s.colsh, xe_act=xe_act)
        k.op("act", lambda: nc.scalar.activation(out=ts.th[:, :], in_=ts.xc[:, :], func=AF.Tanh), reads=[ts.xc], writes=[ts.th])
        yield
        k.op("dve", lambda: nc.vector.scalar_tensor_tensor(out=dst[:, ci, :], in0=ts.th[:, :], scalar=1.0, in1=ts.xc[:, :], op0=ALU.add, op1=ALU.mult),
             reads=[ts.th, ts.xc], writes=[(dst, ci)])
        yield

    def silu_tile(self, ci, sl, j, dst, ts):
        k, s, T = self.k, self, self.T
        nc = k.nc
        ps = s.nextA()
        v = s.inproj_view(sl, j, ps, 0)
        yield
        k.op("act", lambda: nc.scalar.activation(out=ts.th[:, :], in_=v.ap, func=AF.Tanh, scale=0.5), reads=[ps], writes=[ts.th])
        yield
        k.op("dve", lambda: nc.vector.scalar_tensor_tensor(out=dst[:, ci, :], in0=ts.th[:, :], scalar=1.0, in1=v.ap, op0=ALU.add, op1=ALU.mult),
             reads=[ts.th, ps], writes=[(dst, ci)])
        yield

    def mlstm_in(self, b):
        s = self
        wv = s.w_in.t.rearrange("(c p) f -> p c f", p=128)
        SW = s.SW
        nt = SW // 128

        def tiles():
            for g in range(2048 // SW):
                sl = s.load_slab(wv[:, :, 2048 + g * SW: 2048 + (g + 1) * SW])
                for j in range(nt):
                    ci = g * nt + j
                    yield s.conv_tile(ci, sl, j, s.tsets[ci % 2], s.tail_m, 144, 208, s.xmcT, raw_dst=s.xmT)
            for g in range(2048 // SW):
                sl = s.load_slab(wv[:, :, 6144 + g * SW: 6144 + (g + 1) * SW])
                for j in range(nt):
                    ci = g * nt + j
                    yield s.silu_tile(ci, sl, j, s.szm, s.tsets[ci % 2])
        run_pipe(tiles(), depth=2, skew=MIN_SKEW)

    def load_w(self, src_ap):
        s = self
        if not hasattr(s, "nwsl"):
            s.nwsl = 0
        w = s.wsl[s.nwsl % len(s.wsl)]
        s.nwsl += 1
        s._cached_load(w, src_ap)
        return w

    def mlstm_gates(self, b):
        k, s, T, TT = self.k, self, self.T, self.TT
        nc = k.nc
        for h in range(4):
            for (wd, dst, scl) in ((s.wq, s.qT, 1.0), (s.wk, s.kT, 1.0 / 16.0)):
                w = s.load_w(wd.t[h].rearrange("(c p) j -> p c j", p=128))
                for jt in range(2):
                    ps = s.nextB()
                    for ic in range(4):
                        yield
                        k.op("pe", lambda w=w, ic=ic, jt=jt, ps=ps, h=h: nc.tensor.matmul(ps[:, 0:T], lhsT=w[:, ic, jt * 128:(jt + 1) * 128], rhs=s.xmcT[:, 4 * h + ic, :], start=(ic == 0), stop=(ic == 3)),
                             reads=[w, (s.xmcT, 4 * h + ic)], writes=[ps], signal=(ic == 3))
                    yield
                    k.op("act", lambda dst=dst, h=h, jt=jt, ps=ps, scl=scl: nc.scalar.mul(out=dst[:, h, jt, :], in_=ps[:, 0:T], mul=scl),
                         reads=[ps], writes=[(dst, (h, jt))])
        n = 0
        def gmm(c, rhs_ap, rbuf, last):
            nonlocal n
            first = (n == 0)
            k.op("pe", lambda: nc.tensor.matmul(s.psG[0:4, 0:T], lhsT=s.wifb[:, c, 0:4], rhs=rhs_ap, start=first, stop=last),
                 reads=[s.wifb, rbuf], writes=[s.psG], signal=False)
            k.op("pe", lambda: nc.tensor.matmul(s.psS[0:4, 0:T], lhsT=s.wifb[:, c, 4:8], rhs=rhs_ap, start=first, stop=last),
                 reads=[s.wifb, rbuf], writes=[s.psS], signal=last)
            n += 1
        for h in range(4):
            for jt in range(2):
                yield
                gmm(h * 2 + jt, s.qT[:, h, jt, :], (s.qT, (h, jt)), False)
                yield
                gmm(8 + h * 2 + jt, s.kT[:, h, jt, :], (s.kT, (h, jt)), False)
        for h in range(4):
            w = s.load_w(s.wv.t[h].rearrange("(c p) j -> p c j", p=128))
            for jt in range(4):
                ps = s.nextB()
                for ic in range(4):
                    yield
                    k.op("pe", lambda w=w, ic=ic, jt=jt, ps=ps, h=h: nc.tensor.matmul(ps[:, 0:T], lhsT=w[:, ic, jt * 128:(jt + 1) * 128], rhs=s.xmT[:, 4 * h + ic, :], start=(ic == 0), stop=(ic == 3)),
                         reads=[w, (s.xmT, 4 * h + ic)], writes=[ps], signal=(ic == 3))
                vt = s.vTt[(h * 4 + jt) % 2]
                yield
                k.op("act", lambda vt=vt, ps=ps: nc.scalar.copy(out=vt[:, :], in_=ps[:, 0:T]), reads=[ps], writes=[vt])
                yield
                gmm(16 + h * 4 + jt, vt[:, :], vt, (h == 3 and jt == 3))
        V, A = nc.vector, nc.scalar
        yield
        k.op("act", lambda: A.activation(out=s.g_ig[:, :], in_=s.psG[0:4, 0:T], func=AF.Identity, bias=s.bif[:, 0:1]), reads=[s.psG, s.bif], writes=[s.g_ig])
        yield
        k.op("act", lambda: A.activation(out=s.g_t1[:, :], in_=s.psS[0:4, 0:T], func=AF.Identity, bias=s.bif[:, 1:2]), reads=[s.psS, s.bif], writes=[s.g_t1])
        yield
        k.op("dve", lambda: V.tensor_scalar(out=s.g_t2[:, :], in0=s.g_t1[:, :], scalar1=-1.0, scalar2=None, op0=ALU.mult), reads=[s.g_t1], writes=[s.g_t2])
        yield
        k.op("dve", lambda: V.tensor_tensor(out=s.g_t2[:, :], in0=s.g_t2[:, :], in1=s.g_t1[:, :], op=ALU.min), reads=[s.g_t1, s.g_t2], writes=[s.g_t2])
        yield
        k.op("act", lambda: A.activation(out=s.g_t2[:, :], in_=s.g_t2[:, :], func=AF.Exp), reads=[s.g_t2], writes=[s.g_t2])
        yield
        k.op("act", lambda: A.activation(out=s.g_t2[:, :], in_=s.g_t2[:, :], func=AF.Ln, bias=1.0), reads=[s.g_t2], writes=[s.g_t2])
        yield
        k.op("dve", lambda: V.tensor_scalar(out=s.g_lf[:, :], in0=s.g_t1[:, :], scalar1=0.0, scalar2=None, op0=ALU.min), reads=[s.g_t1], writes=[s.g_lf])
        yield
        k.op("dve", lambda: V.tensor_tensor(out=s.g_lf[:, :], in0=s.g_lf[:, :], in1=s.g_t2[:, :], op=ALU.subtract), reads=[s.g_lf, s.g_t2], writes=[s.g_lf])
        yield
        k.op("dve", lambda: V.tensor_tensor_scan(out=s.g_G[:, :], data0=s.ones[0:4, 0:T], data1=s.g_lf[:, :], initial=s.Gl[:, 0:1], op0=ALU.mult, op1=ALU.add),
             reads=[s.ones, s.g_lf, s.Gl], writes=[s.g_G])
        yield
        k.op("dve", lambda: V.tensor_copy(out=s.Gl[:, 0:1], in_=s.g_G[:, T - 1:T]), reads=[s.g_G], writes=[s.Gl])
        yield
        k.op("dve", lambda: V.tensor_tensor(out=s.g_a[:, :], in0=s.g_ig[:, :], in1=s.g_G[:, :], op=ALU.subtract), reads=[s.g_ig, s.g_G], writes=[s.g_a])
        yield
        k.op("dve", lambda: V.tensor_reduce(out=s.g_cm[:, 0:TT], in_=s.g_a[:, :].rearrange("p (c l) -> p c l", l=128), axis=AX.X, op=ALU.max), reads=[s.g_a], writes=[s.g_cm])
        yield
        k.op("dve", lambda: V.tensor_tensor_scan(out=s.g_Mn[:, 0:TT], data0=s.g_cm[:, 0:TT], data1=s.g_cm[:, 0:TT], initial=s.Ml[:, 0:1], op0=ALU.max, op1=ALU.max),
             reads=[s.g_cm, s.Ml], writes=[s.g_Mn])
        yield
        k.op("dve", lambda: V.tensor_copy(out=s.g_Mp[:, 0:1], in_=s.Ml[:, 0:1]), reads=[s.Ml], writes=[s.g_Mp])
        if TT > 1:
            yield
            k.op("dve", lambda: V.tensor_copy(out=s.g_Mp[:, 1:TT], in_=s.g_Mn[:, 0:TT - 1]), reads=[s.g_Mn, s.g_Mp], writes=[s.g_Mp])
        yield
        k.op("dve", lambda: V.tensor_copy(out=s.Ml[:, 0:1], in_=s.g_Mn[:, TT - 1:TT]), reads=[s.g_Mn, s.g_Mp], writes=[s.Ml])
        yield
        k.op("dve", lambda: V.tensor_scalar(out=s.g_nMp[:, 0:TT], in0=s.g_Mp[:, 0:TT], scalar1=-1.0, scalar2=None, op0=ALU.mult), reads=[s.g_Mp], writes=[s.g_nMp])
        yield
        k.op("dve", lambda: V.tensor_scalar(out=s.g_nMn[:, 0:TT], in0=s.g_Mn[:, 0:TT], scalar1=-1.0, scalar2=None, op0=ALU.mult), reads=[s.g_Mn], writes=[s.g_nMn])
        yield
        k.op("dve", lambda: V.tensor_tensor(out=s.g_dd[:, 0:TT], in0=s.g_Mp[:, 0:TT], in1=s.g_Mn[:, 0:TT], op=ALU.subtract), reads=[s.g_Mp, s.g_Mn], writes=[s.g_dd])
        for c in range(TT):
            sl_ = slice(c * 128, (c + 1) * 128)
            yield
            k.op("act", lambda c=c, sl_=sl_: A.activation(out=s.g_rows[:, 0, sl_], in_=s.g_a[:, sl_], func=AF.Exp, bias=s.g_nMp[:, c:c + 1]), reads=[s.g_a, s.g_nMp], writes=[s.g_rows])
            yield
            k.op("act", lambda c=c, sl_=sl_: A.activation(out=s.g_rows[:, 1, sl_], in_=s.g_a[:, sl_], func=AF.Exp, bias=s.g_nMn[:, c:c + 1]), reads=[s.g_a, s.g_nMn], writes=[s.g_rows])
            yield
            k.op("act", lambda c=c, sl_=sl_: A.activation(out=s.g_rows[:, 2, sl_], in_=s.g_G[:, sl_], func=AF.Exp, scale=-1.0, bias=s.g_nMp[:, c:c + 1]), reads=[s.g_G, s.g_nMp], writes=[s.g_rows])
            yield
            k.op("act", lambda c=c, sl_=sl_: A.activation(out=s.g_rows[:, 3, sl_], in_=s.g_a[:, sl_], func=AF.Exp, scale=0.0, bias=s.g_dd[:, c:c + 1]), reads=[s.g_a, s.g_dd], writes=[s.g_rows])
        for c in range(TT):
            for q in range(4):
                o = (c * 4 + q) * 4
                last = (c == TT - 1 and q == 3)
                yield
                k.op("pe", lambda c=c, q=q, o=o: nc.tensor.transpose(out=s.psG[:, o:o + 4], in_=s.g_rows[:, q, c * 128:(c + 1) * 128], identity=s.cst[0:4, 0:4]),
                     reads=[s.g_rows, s.cst], writes=[s.psG], signal=last)
        yield
        k.op("dve", lambda: V.tensor_copy(out=s.g_cols[:, :, :, :].rearrange("p c q h -> p (c q h)"), in_=s.psG[:, 0:TT * 16]), reads=[s.psG], writes=[s.g_cols])

    def mlstm_core(self, b):
        k, s, T, TT = self.k, self, self.T, self.TT
        nc = k.nc
        V, A, P = nc.vector, nc.scalar, nc.tensor
        identb = s.cstb[:, 0:128]
        mask01 = s.cstb[:, 128:256]
        it = 0
        for h in range(4):
            wv_ = s.load_w(s.wv.t[h].rearrange("(c p) j -> p c j", p=128))
            wo_ = s.load_w(s.wo.t[h].rearrange("(c p) j -> p c j", p=128))
            for tt in range(TT):
                for (w, dst, fn) in ((wv_, s.vtok, AF.Copy), (wo_, s.otok, AF.Sigmoid)):
                    ps = s.nextB()
                    for ic in range(4):
                        yield
                        k.op("pe", lambda w=w, ic=ic, tt=tt, ps=ps, h=h: P.matmul(ps[:, 0:512], lhsT=s.xmT[:, 4 * h + ic, tt * 128:(tt + 1) * 128], rhs=w[:, ic, 0:512], start=(ic == 0), stop=(ic == 3)),
                             reads=[w, (s.xmT, 4 * h + ic)], writes=[ps], signal=(ic == 3))
                    yield
                    k.op("act", lambda dst=dst, tt=tt, ps=ps, fn=fn: A.activation(out=dst[:, tt, :], in_=ps[:, 0:512], func=fn), reads=[ps], writes=[(dst, tt)])
            for c in range(TT):
                cs = slice(c * 128, (c + 1) * 128)
                col = lambda q: s.g_cols[:, c, q, h:h + 1]
                scT, kws, hb, hb2, ytok, sm = s.scT[it % 2], s.kws[it % 2], s.hb[it % 2], s.hb2[it % 2], s.ytok[it % 2], s.sm[it % 2]
                it += 1
                for dt in range(2):
                    yield
                    k.op("pe", lambda dt=dt: P.matmul(s.psS[:, 0:128], lhsT=s.kT[:, h, dt, cs], rhs=s.qT[:, h, dt, cs], start=(dt == 0), stop=(dt == 1)),
                         reads=[(s.kT, (h, dt)), (s.qT, (h, dt))], writes=[s.psS], signal=(dt == 1))
                yield
                k.op("dve", lambda: V.scalar_tensor_tensor(out=scT[:, :], in0=s.psS[:, 0:128], scalar=col(0), in1=mask01, op0=ALU.mult, op1=ALU.mult),
                     reads=[s.psS, s.g_cols, s.cstb], writes=[scT])
                psN = s.nextB()
                yield
                k.op("pe", lambda: P.matmul(psN[:, 0:512], lhsT=scT[:, :], rhs=s.vtok[:, c, :], start=True, stop=False), reads=[scT, (s.vtok, c)], writes=[psN], signal=False)
                for dt in range(2):
                    yield
                    k.op("pe", lambda dt=dt: P.matmul(psN[:, 0:512], lhsT=s.qT[:, h, dt, cs], rhs=s.CTb[:, h, dt, :], start=False, stop=(dt == 1)),
                         reads=[(s.qT, (h, dt)), (s.CTb, h)], writes=[psN], signal=(dt == 1))
                yield
                k.op("pe", lambda: P.matmul(s.psG[:, 0:1], lhsT=scT[:, :], rhs=s.onesb[:, 0:1], start=True, stop=False), reads=[scT, s.onesb], writes=[s.psG], signal=False)
                for dt in range(2):
                    yield
                    k.op("pe", lambda dt=dt: P.matmul(s.psG[:, 0:1], lhsT=s.qT[:, h, dt, cs], rhs=s.nstb[:, h, dt:dt + 1], start=False, stop=(dt == 1)),
                         reads=[(s.qT, (h, dt)), (s.nstb, h)], writes=[s.psG], signal=(dt == 1))
                yield
                k.op("dve", lambda: V.tensor_scalar(out=sm[:, 3:4], in0=s.psG[:, 0:1], scalar1=-1.0, scalar2=None, op0=ALU.mult), reads=[s.psG], writes=[sm])
                yield
                k.op("dve", lambda: V.tensor_tensor(out=sm[:, 0:1], in0=s.psG[:, 0:1], in1=sm[:, 3:4], op=ALU.max), reads=[s.psG, sm], writes=[sm])
                yield
                k.op("dve", lambda: V.tensor_scalar(out=sm[:, 0:1], in0=sm[:, 0:1], scalar1=col(2), scalar2=None, op0=ALU.max), reads=[sm, s.g_cols], writes=[sm])
                yield
                k.op("dve", lambda: V.reciprocal(out=sm[:, 1:2], in_=sm[:, 0:1]), reads=[sm], writes=[sm])
                yield
                k.op("act", lambda: A.activation(out=hb[:, :], in_=psN[:, 0:512], func=AF.Copy, scale=sm[:, 1:2]), reads=[psN, sm], writes=[hb])
                yield
                k.op("act", lambda: A.activation(out=hb2[:, :], in_=hb[:, :], func=AF.Square, accum_out=sm[:, 2:3]), reads=[hb], writes=[hb2, sm])
                yield
                s.rstd(sm, sm[:, 2:3], 512)
                yield
                k.op("dve", lambda: V.scalar_tensor_tensor(out=ytok[:, :], in0=hb[:, :], scalar=sm[:, 2:3], in1=s.otok[:, c, :], op0=ALU.mult, op1=ALU.mult),
                     reads=[hb, sm, (s.otok, c)], writes=[ytok])
                i = s.nextT()
                for vt in range(4):
                    yield
                    k.op("pe", lambda vt=vt: P.transpose(out=s.psT[i][:, vt * 128:(vt + 1) * 128], in_=ytok[:, vt * 128:(vt + 1) * 128], identity=identb),
                         reads=[ytok, s.cstb], writes=[s.psT[i]], signal=(vt == 3))
                for vt in range(4):
                    ft = 4 * h + vt
                    yield
                    k.op("dve", lambda vt=vt, ft=ft: V.scalar_tensor_tensor(out=s.xmcT[:, ft, cs], in0=s.psT[i][:, vt * 128:(vt + 1) * 128], scalar=s.mncol[:, ft:ft + 1], in1=s.szm[:, ft, cs], op0=ALU.mult, op1=ALU.mult),
                         reads=[s.psT[i], s.mncol, (s.szm, ft)], writes=[(s.xmcT, ft)])
                i2 = s.nextT()
                for dt in range(2):
                    yield
                    k.op("pe", lambda dt=dt: P.transpose(out=s.psT[i2][:, dt * 128:(dt + 1) * 128], in_=s.kT[:, h, dt, cs], identity=identb),
                         reads=[(s.kT, (h, dt)), s.cstb], writes=[s.psT[i2]], signal=(dt == 1))
                yield
                k.op("dve", lambda: V.tensor_scalar(out=kws[:, :], in0=s.psT[i2][:, 0:256], scalar1=col(1), scalar2=None, op0=ALU.mult), reads=[s.psT[i2], s.g_cols], writes=[kws])
                for dt in range(2):
                    psC = s.nextB()
                    yield
                    k.op("pe", lambda dt=dt, psC=psC: P.matmul(psC[:, 0:512], lhsT=kws[:, dt * 128:(dt + 1) * 128], rhs=s.vtok[:, c, :], start=True, stop=True), reads=[kws, (s.vtok, c)], writes=[psC])
                    yield
                    k.op("dve", lambda dt=dt, psC=psC: V.scalar_tensor_tensor(out=s.CT[:, h, dt, :], in0=s.CT[:, h, dt, :], scalar=col(3), in1=psC[:, 0:512], op0=ALU.mult, op1=ALU.add),
                         reads=[(s.CT, h), s.g_cols, psC], writes=[(s.CT, h)])
                    yield
                    k.op("act", lambda dt=dt: A.copy(out=s.CTb[:, h, dt, :], in_=s.CT[:, h, dt, :]), reads=[(s.CT, h)], writes=[(s.CTb, h)])
                    yield
                    k.op("pe", lambda dt=dt: P.matmul(s.psG[:, 8 + dt:9 + dt], lhsT=kws[:, dt * 128:(dt + 1) * 128], rhs=s.onesb[:, 0:1], start=True, stop=True), reads=[kws, s.onesb], writes=[s.psG])
                    yield
                    k.op("dve", lambda dt=dt: V.scalar_tensor_tensor(out=s.nst[:, h, dt:dt + 1], in0=s.nst[:, h, dt:dt + 1], scalar=col(3), in1=s.psG[:, 8 + dt:9 + dt], op0=ALU.mult, op1=ALU.add),
                         reads=[(s.nst, h), s.g_cols, s.psG], writes=[(s.nst, h)])
                    yield
                    k.op("act", lambda dt=dt: A.copy(out=s.nstb[:, h, dt:dt + 1], in_=s.nst[:, h, dt:dt + 1]), reads=[(s.nst, h)], writes=[(s.nstb, h)])


SSD_IN = 10304
XE_ACT = True


def host_cols_l1(p):
    def col(v):
        return np.ascontiguousarray(v.reshape(-1, 128).T)
    def col4(w):
        return np.ascontiguousarray(w.reshape(4, -1, 128).transpose(2, 1, 0).reshape(128, -1))
    cols = np.concatenate([col(p["o_norm"][0]), col4(p["o_conv_w"][0]), col(p["o_conv_b"][0]), col(p["o_gnorm"][0])], axis=1).astype(np.float32)
    rep = lambda v: np.broadcast_to(v.reshape(1, -1), (128, v.size))
    reps = np.ascontiguousarray(np.concatenate([rep(p["o_dt_bias"][0]), rep(p["o_a_log"][0]), rep(p["o_d_skip"][0])], axis=1)).astype(np.float32)
    return cols, reps


class L1(L0):
    def __init__(self, k, T, xin, xout, out):
        self.k, self.T, self.TT = k, T, T // 128
        self.xin, self.xout, self.out = xin, xout, out
        self.w_in = k.dram("o_w_in", [D, SSD_IN])
        self.w_out = k.dram("o_w_out", [4096, D])
        self.cols_d = k.dram("o_cols", [128, 288])
        self.reps_d = k.dram("o_reps", [128, 192])
        self.frep_d = k.dram("final_rep", [128, D])
        self.const_d = k.dram("consts1", [128, 512])
        self.NSLOT = 64
        self.scr = k.dram("wscr1", [self.NSLOT, 128, 4096], BF16, kind="Internal")

    def alloc(self):
        k, T, TT, s = self.k, self.T, self.TT, self
        s.cols = k.sb("cols1", [128, 288])
        s.colsh = k.sb("colsh1", [128, 288])
        s.mhalf = k.sb("mhalf1", [128, 2])
        s.SW = 256
        s.reps = k.sb("reps1", [128, 192])
        s.arep = k.sb("arep", [128, 64])
        s.frep = k.sb("frep", [128, D])
        s.cst = k.sb("cst1", [128, 512])
        s.cstb = k.sb("cstb1", [128, 128], BF16)
        s.ones = k.sb("ones1", [128, 128])
        s.tail = k.sb("tail1", [128, 48, 3])
        s.S = k.sb("S", [128, 8, 512])
        s.Sb = k.sb("Sb", [128, 8, 512], BF16)
        s.xt = [k.sb("xt1%d" % i, [128, D]) for i in range(2)]
        s.st4 = k.sb("st41", [128, 8])
        s.xnT = k.sb("xnT1", [128, NCH, T], BF16)
        s.yT = k.sb("yT1", [128, 16, T], BF16)
        s.slab = [k.sb("slab1%d" % i, [128, 16, 256], BF16) for i in range(5)]
        s.nslab = 0
        s.xe = [k.sb("xe1%d" % i, [128, T + 3]) for i in range(2)]
        s.tsets = []
        for i in range(2):
            ts = TS()
            ts.xc = k.sb("xc1%d" % i, [128, T])
            ts.th = k.sb("th1%d" % i, [128, T])
            s.tsets.append(ts)
        s.xbcT = k.sb("xbcT", [128, 48, T], BF16)
        s.xtok = k.sb("xtok", [128, TT, 8, 512], BF16)
        s.btok = k.sb("btok", [128, TT, 8, 128], BF16)
        s.dt = k.sb("dt", [128, TT, 64])
        s.t1 = k.sb("t1", [128, 64])
        s.t2 = k.sb("t2", [128, 64])
        s.dA = k.sb("dA", [128, TT, 64])
        s.Acs = k.sb("Acs", [128, TT, 64])
        s.bcol = k.sb("bcol", [128, TT, 64])
        s.dcol = k.sb("dcol", [128, TT, 64])
        s.wcol = k.sb("wcol", [128, TT, 64])
        s.crep = k.sb("crep", [128, TT, 64])
        s.wsets = []
        for i in range(2):
            W = TS()
            W.Z = k.sb("Zb%d" % i, [128, 8, 128])
            W.Lp = k.sb("Lpb%d" % i, [128, 8, 128])
            W.MT = k.sb("MTb%d" % i, [128, 8, 128], BF16)
            W.cbS = k.sb("cbS%d" % i, [128, 128])
            W.ysb = k.sb("ysb%d" % i, [128, 512])
            W.y2 = k.sb("y2%d" % i, [128, 512])
            W.szg = k.sb("szg%d" % i, [128, 512], BF16)
            W.xw = k.sb("xw%d" % i, [128, 512], BF16)
            W.ytok = k.sb("ytokL%d" % i, [128, 512], BF16)
            W.sm = k.sb("smL%d" % i, [128, 8])
            s.wsets.append(W)
        s.psA = [k.ps("qsA%d" % i, [128, 512]) for i in range(2)]
        s.npsA = 0
        s.psT = [k.ps("qsT%d" % i, [128, 1024], BF16) for i in range(2)]
        s.npsT = 0
        s.stage = [(s.xbcT.t[:, 16 * i:16 * i + 16, :].rearrange("p c t -> p (c t)").bitcast(F32), [(s.xbcT, ci) for ci in range(16 * i, 16 * i + 16)]) for i in range(2)]
        s.ipool = ([0, 1, 2], [0])
        s.opool = ([3, 4], [0])
        for i in range(2):
            W = s.wsets[i]
            W.w = s.psA[i]
            W.T = s.psT[i]
            W.L = k.ps("qsL%d" % i, [128, 512])
            W.Y = k.ps("qsY%d" % i, [128, 512])

    def setup(self):
        k, s = self.k, self
        nc = k.nc
        V, A = nc.vector, nc.scalar
        k.dma("sp", s.cols[:, :], s.cols_d[:, :], writes=[s.cols])
        k.dma("sp", s.reps[:, :], s.reps_d[:, :], writes=[s.reps])
        k.dma("sp", s.frep[:, :], s.frep_d[:, :], writes=[s.frep])
        k.dma("sp", s.cst[:, :], s.const_d[:, :], writes=[s.cst])
        k.dma("pool", s.cstb[:, :], s.const_d[:, 0:128], writes=[s.cstb])
        k.op("dve", lambda: V.memset(s.ones[:, :], 1.0), writes=[s.ones])
        k.op("dve", lambda: V.memset(s.mhalf[:, 0:1], -0.5), writes=[s.mhalf])
        k.op("dve", lambda: V.memset(s.mhalf[:, 1:2], 0.5), reads=[s.mhalf], writes=[s.mhalf])
        k.op("dve", lambda: V.tensor_scalar(out=s.colsh[:, :], in0=s.cols[:, :], scalar1=0.5, scalar2=None, op0=ALU.mult), reads=[s.cols], writes=[s.colsh])
        for b in [s.tail, s.S, s.Sb]:
            k.op("dve", (lambda b=b: V.memset(b.t[:], 0.0)), writes=[b])
        k.op("act", lambda: A.activation(out=s.arep[:, :], in_=s.reps[:, 64:128], func=AF.Exp), reads=[s.reps], writes=[s.arep])
        k.op("dve", lambda: V.tensor_scalar(out=s.arep[:, :], in0=s.arep[:, :], scalar1=-1.0, scalar2=None, op0=ALU.mult), reads=[s.arep], writes=[s.arep])

    def ssd_in(self, b):
        for _ in self.ssd_in_g(b):
            pass

    def ssd_in_g(self, b, slabpool=None):
        k, s, T, TT = self.k, self, self.T, self.TT
        nc = k.nc
        V, A, P = nc.vector, nc.scalar, nc.tensor
        wv = s.w_in.t.rearrange("(c p) f -> p c f", p=128)
        def tiles():
            for g in range(24):
                sl = s.load_slab(wv[:, :, 4096 + g * 256: 4096 + (g + 1) * 256], pool=slabpool)
                for j in range(2):
                    ci = g * 2 + j
                    yield s.conv_tile(ci, sl, j, s.tsets[ci % 2], s.tail, 16, 208, s.xbcT, xe_act=XE_ACT)
        yield from pipe_gen(tiles(), depth=2, skew=SIN_SKEW)
        sl = s.load_slab(wv[:, :, 10240:10304], pool=slabpool)
        tri = s.cst[:, 128:256]
        for tt in range(TT):
            ps = s.nextA()
            for c in range(NCH):
                k.op("pe", lambda c=c, tt=tt, ps=ps: P.matmul(ps[:, 0:64], lhsT=s.xnT[:, c, tt * 128:(tt + 1) * 128], rhs=sl[:, c, 0:64], start=(c == 0), stop=(c == NCH - 1)),
                     reads=[sl, (s.xnT, c)], writes=[ps], signal=(c == NCH - 1))
            k.op("dve", lambda ps=ps: V.tensor_tensor(out=s.t1[:, :], in0=ps[:, 0:64], in1=s.reps[:, 0:64], op=ALU.add), reads=[ps, s.reps], writes=[s.t1])
            k.op("dve", lambda: V.tensor_scalar(out=s.t2[:, :], in0=s.t1[:, :], scalar1=-1.0, scalar2=None, op0=ALU.mult), reads=[s.t1], writes=[s.t2])
            k.op("dve", lambda: V.tensor_tensor(out=s.t2[:, :], in0=s.t2[:, :], in1=s.t1[:, :], op=ALU.min), reads=[s.t1, s.t2], writes=[s.t2])
            k.op("act", lambda: A.activation(out=s.t2[:, :], in_=s.t2[:, :], func=AF.Exp), reads=[s.t2], writes=[s.t2])
            k.op("act", lambda: A.activation(out=s.t2[:, :], in_=s.t2[:, :], func=AF.Ln, bias=1.0), reads=[s.t2], writes=[s.t2])
            k.op("dve", lambda: V.tensor_scalar(out=s.t1[:, :], in0=s.t1[:, :], scalar1=0.0, scalar2=None, op0=ALU.max), reads=[s.t1], writes=[s.t1])
            k.op("dve", lambda tt=tt: V.tensor_tensor(out=s.dt[:, tt, :], in0=s.t1[:, :], in1=s.t2[:, :], op=ALU.add), reads=[s.t1, s.t2], writes=[s.dt])
            k.op("dve", lambda tt=tt: V.tensor_tensor(out=s.dA[:, tt, :], in0=s.dt[:, tt, :], in1=s.arep[:, :], op=ALU.mult), reads=[s.dt, s.arep], writes=[s.dA])
            psc = s.nextA()
            k.op("pe", lambda tt=tt, psc=psc: P.matmul(psc[:, 0:64], lhsT=tri, rhs=s.dA[:, tt, :], start=True, stop=True), reads=[s.cst, s.dA], writes=[psc])
            k.op("act", lambda tt=tt, psc=psc: A.copy(out=s.Acs[:, tt, :], in_=psc[:, 0:64]), reads=[psc], writes=[s.Acs])
            pse = s.nextA()
            k.op("pe", lambda tt=tt, pse=pse: P.matmul(pse[:, 0:64], lhsT=s.ones[:, :], rhs=s.dA[:, tt, :], start=True, stop=True), reads=[s.ones, s.dA], writes=[pse])
            k.op("act", lambda tt=tt, pse=pse: A.activation(out=s.crep[:, tt, :], in_=pse[:, 0:64], func=AF.Exp), reads=[pse], writes=[s.crep])
            k.op("act", lambda tt=tt: A.activation(out=s.t1[:, :], in_=s.dt[:, tt, :], func=AF.Ln), reads=[s.dt], writes=[s.t1])
            k.op("dve", lambda tt=tt: V.tensor_tensor(out=s.bcol[:, tt, :], in0=s.t1[:, :], in1=s.Acs[:, tt, :], op=ALU.subtract), reads=[s.t1, s.Acs], writes=[s.bcol])
            k.op("act", lambda tt=tt: A.activation(out=s.dcol[:, tt, :], in_=s.Acs[:, tt, :], func=AF.Exp), reads=[s.Acs], writes=[s.dcol])
            k.op("dve", lambda tt=tt, pse=pse: V.tensor_tensor(out=s.t2[:, :], in0=pse[:, 0:64], in1=s.bcol[:, tt, :], op=ALU.add), reads=[pse, s.bcol], writes=[s.t2])
            k.op("act", lambda tt=tt: A.activation(out=s.wcol[:, tt, :], in_=s.t2[:, :], func=AF.Exp), reads=[s.t2], writes=[s.wcol])
        yield
        identb = s.cstb[:, 0:128]
        for tt in range(TT):
            yield
            for g in range(8):
                i = s.nextT()
                for j in range(4):
                    k.op("pe", lambda tt=tt, g=g, j=j, i=i: P.transpose(out=s.psT[i][:, j * 128:(j + 1) * 128], in_=s.xbcT[:, 4 * g + j, tt * 128:(tt + 1) * 128], identity=identb),
                         reads=[(s.xbcT, 4 * g + j), s.cstb], writes=[s.psT[i]], signal=(j == 3))
                k.op("dve", lambda tt=tt, g=g, i=i: V.tensor_copy(out=s.xtok[:, tt, g, :], in_=s.psT[i][:, 0:512]), reads=[s.psT[i]], writes=[(s.xtok, (tt, g))])
            for g2 in range(2):
                i = s.nextT()
                for j in range(4):
                    k.op("pe", lambda tt=tt, g2=g2, j=j, i=i: P.transpose(out=s.psT[i][:, j * 128:(j + 1) * 128], in_=s.xbcT[:, 32 + 4 * g2 + j, tt * 128:(tt + 1) * 128], identity=identb),
                         reads=[(s.xbcT, 32 + 4 * g2 + j), s.cstb], writes=[s.psT[i]], signal=(j == 3))
                k.op("dve", lambda tt=tt, g2=g2, i=i: V.tensor_copy(out=s.btok[:, tt, 4 * g2:4 * g2 + 4, :].rearrange("p a n -> p (a n)"), in_=s.psT[i][:, 0:512]), reads=[s.psT[i]], writes=[s.btok])

    def ssd_stream(self, g, gl, zs, W):
        k, s, T, TT = self.k, self, self.T, self.TT
        nc = k.nc
        V, A, P, G = nc.vector, nc.scalar, nc.tensor, nc.gpsimd
        identb = s.cstb[:, 0:128]
        ident = s.cst[:, 0:128]
        negmaskT = s.cst[:, 384:512]
        bc3 = lambda ap, shape, ax: ap.unsqueeze(ax).to_broadcast(shape)
        v3 = lambda ap: ap.rearrange("p (e j) -> p e j", j=64)
        g8 = slice(g * 8, g * 8 + 8)
        for c in range(TT):
            cs = slice(c * 128, (c + 1) * 128)
            xg = s.xtok[:, c, g, :]
            k.op("pe", lambda: P.matmul(W.w[:, 0:128], lhsT=s.xbcT[:, 32 + g, cs], rhs=s.xbcT[:, 40 + g, cs], start=True, stop=True),
                 reads=[(s.xbcT, 32 + g), (s.xbcT, 40 + g)], writes=[W.w])
            k.op("dve", lambda: V.tensor_tensor(out=W.Z[:, :, :], in0=bc3(negmaskT, [128, 8, 128], 1), in1=bc3(s.Acs[:, c, g8], [128, 8, 128], 2), op=ALU.add),
                 reads=[s.cst, s.Acs], writes=[W.Z])
            yield
            k.op("dve", lambda: V.tensor_copy(out=W.cbS[:, :], in_=W.w[:, 0:128]), reads=[W.w], writes=[W.cbS])
            for hf in range(2):
                for e4 in range(4):
                    e = hf * 4 + e4
                    r = slice(e4 * 128, (e4 + 1) * 128)
                    k.op("pe", lambda e=e, r=r: P.transpose(out=W.L[:, r], in_=W.Z[:, e, :], identity=ident), reads=[W.Z, s.cst], writes=[W.L], signal=(e4 == 3))
                yield
                for e4 in range(4):
                    e = hf * 4 + e4
                    hh = g * 8 + e
                    r = slice(e4 * 128, (e4 + 1) * 128)
                    k.op("act", lambda e=e, r=r, hh=hh: A.activation(out=W.Lp[:, e, :], in_=W.L[:, r], func=AF.Exp, bias=s.bcol[:, c, hh:hh + 1]), reads=[W.L, s.bcol], writes=[(W.Lp, hf)])
                yield
                k.op("dve", lambda hf=hf: V.tensor_tensor(out=W.MT[:, hf * 4:(hf + 1) * 4, :], in0=W.Lp[:, hf * 4:(hf + 1) * 4, :], in1=bc3(W.cbS[:, :], [128, 4, 128], 1), op=ALU.mult),
                     reads=[(W.Lp, hf), W.cbS], writes=[(W.MT, hf)])
                yield
                for e4 in range(4):
                    e = hf * 4 + e4
                    k.op("pe", lambda e=e, hf=hf: P.matmul(W.Y[:, e * 64:(e + 1) * 64], lhsT=W.MT[:, e, :], rhs=s.xtok[:, c, g, e * 64:(e + 1) * 64], start=True, stop=True),
                         reads=[(W.MT, hf), (s.xtok, (c, g))], writes=[W.Y], signal=(e == 7))
            for hz in range(2):
                for kk in range(NCH):
                    k.op("pe", lambda kk=kk, hz=hz: P.matmul(W.w[:, hz * 256:(hz + 1) * 256], lhsT=s.xnT[:, kk, cs], rhs=zs[hz][:, kk, 0:256], start=(kk == 0), stop=(kk == NCH - 1)),
                         reads=[zs[hz], (s.xnT, kk)], writes=[W.w], signal=(kk == NCH - 1 and hz == 1))
            k.op("pool", lambda: G.tensor_tensor(out=v3(W.y2[:, :]), in0=v3(xg), in1=bc3(s.reps[:, 128 + g * 8:136 + g * 8], [128, 8, 64], 2), op=ALU.mult),
                 reads=[(s.xtok, (c, g)), s.reps], writes=[W.y2])
            yield
            k.op("act", lambda: A.activation(out=W.szg[:, :], in_=W.w[:, 0:512], func=AF.Tanh, scale=0.5), reads=[W.w], writes=[W.szg])
            yield
            k.op("dve", lambda: V.scalar_tensor_tensor(out=W.szg[:, :], in0=W.szg[:, :], scalar=1.0, in1=W.w[:, 0:512], op0=ALU.add, op1=ALU.mult), reads=[W.szg, W.w], writes=[W.szg])
            k.op("pe", lambda: P.matmul(W.w[:, 0:512], lhsT=s.xbcT[:, 40 + g, cs], rhs=s.Sb[:, g, :], start=True, stop=True), reads=[(s.xbcT, 40 + g), (s.Sb, g)], writes=[W.w])
            yield
            k.op("dve", lambda: V.tensor_tensor(out=v3(W.ysb[:, :]), in0=v3(W.w[:, 0:512]), in1=bc3(s.dcol[:, c, g8], [128, 8, 64], 2), op=ALU.mult),
                 reads=[W.w, s.dcol], writes=[W.ysb])
            k.op("dve", lambda: V.tensor_tensor(out=W.ysb[:, :], in0=W.ysb[:, :], in1=W.y2[:, :], op=ALU.add), reads=[W.ysb, W.y2], writes=[W.ysb])
            yield
            k.op("dve", lambda: V.tensor_tensor(out=W.ysb[:, :], in0=W.ysb[:, :], in1=W.Y[:, 0:512], op=ALU.add), reads=[W.ysb, W.Y], writes=[W.ysb])
            k.op("dve", lambda: V.scalar_tensor_tensor(out=W.y2[:, :], in0=W.ysb[:, :], scalar=0.5, in1=W.szg[:, :], op0=ALU.mult, op1=ALU.mult), reads=[W.ysb, W.szg], writes=[W.y2])
            k.op("pool", lambda: G.tensor_tensor(out=v3(W.xw[:, :]), in0=v3(xg), in1=bc3(s.wcol[:, c, g8], [128, 8, 64], 2), op=ALU.mult),
                 reads=[(s.xtok, (c, g)), s.wcol], writes=[W.xw])
            yield
            k.op("act", lambda: A.activation(out=W.ysb[:, :], in_=W.y2[:, :], func=AF.Square, accum_out=W.sm[:, 0:1]), reads=[W.y2], writes=[W.ysb, W.sm])
            k.op("pe", lambda: P.matmul(W.w[:, 0:512], lhsT=s.btok[:, c, g, :], rhs=W.xw[:, :], start=True, stop=True), reads=[s.btok, W.xw], writes=[W.w])
            k.op("pool", lambda: G.tensor_tensor(out=v3(s.S[:, g, :]), in0=v3(s.S[:, g, :]), in1=bc3(s.crep[:, c, g8], [128, 8, 64], 2), op=ALU.mult),
                 reads=[(s.S, g), s.crep], writes=[(s.S, g)])
            yield
            s.rstd(W.sm, W.sm[:, 0:1], 512)
            yield
            k.op("dve", lambda: V.tensor_scalar(out=W.ytok[:, :], in0=W.y2[:, :], scalar1=W.sm[:, 0:1], scalar2=None, op0=ALU.mult), reads=[W.y2, W.sm], writes=[W.ytok])
            k.op("dve", lambda: V.tensor_tensor(out=s.S[:, g, :], in0=s.S[:, g, :], in1=W.w[:, 0:512], op=ALU.add), reads=[(s.S, g), W.w], writes=[(s.S, g)])
            yield
            for vt in range(4):
                k.op("pe", lambda vt=vt: P.transpose(out=W.T[:, vt * 128:(vt + 1) * 128], in_=W.ytok[:, vt * 128:(vt + 1) * 128], identity=identb),
                     reads=[W.ytok, s.cstb], writes=[W.T], signal=(vt == 3))
            k.op("act", lambda: A.copy(out=s.Sb[:, g, :], in_=s.S[:, g, :]), reads=[(s.S, g)], writes=[(s.Sb, g)])
            yield
            for vt in range(4):
                ft = 4 * gl + vt
                gcol = 256 + 4 * g + vt
                k.op("dve", lambda vt=vt, ft=ft, gcol=gcol: V.tensor_scalar(out=s.yT[:, ft, cs], in0=W.T[:, vt * 128:(vt + 1) * 128], scalar1=s.cols[:, gcol:gcol + 1], scalar2=None, op0=ALU.mult),
                     reads=[W.T, s.cols], writes=[(s.yT, ft)])
            yield

    def ssd_core(self, b, half):
        s = self
        wv = s.w_in.t.rearrange("(c p) f -> p c f", p=128)
        for pair in range(2):
            gens = []
            for i in range(2):
                gl = pair * 2 + i
                g = half * 4 + gl
                zs = [s.load_slab(wv[:, :, g * 512 + hz * 256: g * 512 + (hz + 1) * 256]) for hz in range(2)]
                gens.append(s.ssd_stream(g, gl, zs, s.wsets[i]))
            run_pipe(gens, depth=2, skew=SSD_SKEW)

    def final(self, b):
        k, s, T, TT = self.k, self, self.T, self.TT
        nc = k.nc
        V, A = nc.vector, nc.scalar
        for tt in range(TT):
            xt = s.xt[tt]
            r0 = b * T + tt * 128
            k.op("act", lambda xt=xt, tt=tt: A.activation(out=s.xnb_all[tt][:, :], in_=xt[:, :], func=AF.Square, accum_out=s.st4[:, tt:tt + 1]), reads=[xt], writes=[s.st4, s.xnb_all[tt]])
            s.rstd(s.st4, s.st4[:, tt:tt + 1], D)
            k.op("dve", lambda xt=xt, tt=tt: V.scalar_tensor_tensor(out=xt[:, :], in0=xt[:, :], scalar=s.st4[:, tt:tt + 1], in1=s.frep[:, :], op0=ALU.mult, op1=ALU.mult), reads=[xt, s.st4, s.frep], writes=[xt])
            k.dma("pool", s.out[r0:r0 + 128, :], xt[:, :], reads=[xt], writes=[(s.out, r0)])


from concourse.bass_utils import run_bass_kernel_spmd

T_BLK = 256
NMIX = 2
L1_CUT = None


def build(S):
    k = K()
    x = k.dram("x", [S, D])
    x1 = k.dram("x1_scratch", [S, D], kind="Internal")
    x2 = k.dram("x2_scratch", [S, D], kind="Internal")
    out = k.dram("out", [S, D], kind="ExternalOutput")
    nb = S // T_BLK
    k.les = contextlib.ExitStack()
    A0 = L0(k, T_BLK, x, x1)
    A0.alloc()
    A0.setup()
    for b in range(nb):
        A0.rmsnorm_T(b)
        A0.mlstm_in(b)
        A0.lru_mlstm(b)
        A0.outproj(b, 0)
        A0.outproj(b, 1, src=A0.xmcT)
    k.barrier()
    k.les.close()
    k.les = contextlib.ExitStack()
    A1 = L1(k, T_BLK, x1, x2, out)
    A1.alloc()
    A1.setup()
    k.cut = L1_CUT
    def chain(*gs):
        for g in gs:
            yield from g
    obanks = [A1.wsets[0].L, A1.wsets[1].L, A1.wsets[0].Y, A1.wsets[1].Y]
    for _ in A1.rmsnorm_g(0, A1.stage):
        pass
    A1.load_resid(0)
    A1.ssd_in(0)
    for b in range(nb):
        A1.ssd_core(b, 0)
        if b > 0:
            A1.load_resid(b)
        A1.outproj(b, 0)
        A1.ssd_core(b, 1)
        if b + 1 < nb:
            mix(A1.outproj_g(b, 1, store=False, banks=obanks, slabpool=A1.opool),
                chain(A1.rmsnorm_g(b + 1, A1.stage), A1.ssd_in_g(b + 1, slabpool=A1.ipool)), NMIX)
        else:
            A1.outproj(b, 1, store=False)
        A1.final(b)
    k.cut = None
    k.finish([out])
    k.barrier()
    k.les.close()
    return k


def make_maps(p, xs):
    cols1, reps1 = host_cols_l1(p)
    base = {"e_w_in": p["e_w_in"][0], "e_w_out": p["e_w_out"][0], "e_lru_wa": p["e_lru_wa"][0], "e_lru_wx": p["e_lru_wx"][0],
            "e_w_q": p["e_w_q"][0], "e_w_k": p["e_w_k"][0], "e_w_v": p["e_w_v"][0], "e_w_o": p["e_w_o"][0], "e_w_if": p["e_w_if"][0],
            "e_cols": host_cols_l0(p), "e_bif": np.ascontiguousarray(p["e_b_if"][0].reshape(2, 4).T),
            "e_mncol": np.ascontiguousarray(p["e_m_norm"][0].reshape(16, 128).T), "consts": consts_host(), "consts1": consts_host(),
            "o_w_in": p["o_w_in"][0], "o_w_out": p["o_w_out"][0], "o_cols": cols1, "o_reps": reps1,
            "final_rep": np.ascontiguousarray(np.broadcast_to(p["final_norm"].reshape(1, D), (128, D)))}
    base = {kk: np.ascontiguousarray(np.asarray(v, dtype=np.float32)) for kk, v in base.items()}
    return [dict(base, x=np.ascontiguousarray(x_)) for x_ in xs]


def kernel(**inputs):
    p = {kk: np.asarray(v) for kk, v in inputs.items()}
    x = p["x"]
    B, S, _ = x.shape
    k = build(S)
    maps = make_maps(p, [x[c % B] for c in range(8)])
    res = run_bass_kernel_spmd(k.nc, maps, core_ids=list(range(8)))
    return np.stack([res.results[b]["out"] for b in range(B)], axis=0).astype(np.float32)
```

```python
import contextlib
import numpy as np
import concourse.bass as bass
import concourse.mybir as mybir

F32 = mybir.dt.float32
BF16 = mybir.dt.bfloat16
AF = mybir.ActivationFunctionType
ALU = mybir.AluOpType
AX = mybir.AxisListType

NDMA_SEM = 6


class Buf:
    def __init__(self, name, t):
        self.name = name
        self.t = t
        self.st = {}

    def __getitem__(self, idx):
        return self.t[idx]


class BufView(Buf):
    def __init__(self, base, ap):
        self.name = base.name
        self.t = ap
        self.st = base.st
        if getattr(base, "excl", False):
            self.excl = True


class K:
    def __init__(self):
        self.nc = bass.Bass("TRN2", target_bir_lowering=False)
        self.es = contextlib.ExitStack()
        nc = self.nc
        self.eng = {"pe": nc.tensor, "act": nc.scalar, "dve": nc.vector, "pool": nc.gpsimd, "sp": nc.sync}
        self.csem = {e: self.es.enter_context(nc.semaphore("s_" + e)) for e in ["pe", "act", "dve", "pool"]}
        self.ccnt = {e: 0 for e in self.csem}
        self.dsem = {q: [self.es.enter_context(nc.semaphore("d_%s%d" % (q, i))) for i in range(NDMA_SEM)]
                     for q in ["sp", "pool"]}
        self.dcnt = {q: 0 for q in self.dsem}
        self.dtok = {q: [None] * NDMA_SEM for q in self.dsem}
        self.seen = {e: {} for e in self.eng}
        self.pe_pending = []
        self.nbuf = 0
        self.ninst = 0

    def sb(self, name, shape, dt=F32):
        t = getattr(self, "les", self.es).enter_context(self.nc.sbuf_tensor(name, list(shape), dt))
        return Buf(name, t)

    def ps(self, name, shape, dt=F32):
        t = getattr(self, "les", self.es).enter_context(self.nc.psum_tensor(name, list(shape), dt))
        b = Buf(name, t)
        b.excl = True
        return b

    def dram(self, name, shape, dt=F32, kind="ExternalInput"):
        t = self.nc.dram_tensor(name, list(shape), dt, kind=kind)
        b = Buf(name, t.ap())
        return b

    def _wait(self, e, tok):
        if tok is None:
            return
        sem, val, semname = tok
        if self.seen[e].get(semname, 0) >= val:
            return
        self.eng[e].wait_ge(sem, val)
        self.seen[e][semname] = val

    def _deps(self, e, reads, writes):
        toks = []
        for (b, k) in reads:
            s = b.st.get(k)
            if s and s[0] is not None:
                toks.append(s[0])
            if s and getattr(b, "excl", False):
                toks.extend(t for (eng, t) in s[1].items() if eng != e)
        for (b, k) in writes:
            s = b.st.get(k)
            if s:
                if s[0] is not None:
                    toks.append(s[0])
                toks.extend(s[1].values())
        return toks

    def _commit(self, e, tok, reads, writes):
        for (b, k) in reads:
            s = b.st.setdefault(k, [None, {}])
            s[1][e] = tok
        for (b, k) in writes:
            b.st[k] = [tok, {}]

    def op(self, e, fn, reads=(), writes=(), signal=True):
        if getattr(self, "cut", None) is not None:
            if self.cut <= 0:
                return None
            self.cut -= 1
        reads = [r if isinstance(r, tuple) else (r, 0) for r in reads]
        writes = [w if isinstance(w, tuple) else (w, 0) for w in writes]
        for tok in self._deps(e, reads, writes):
            if e == "pe" and tok[2] == "c_pe":
                continue
            self._wait(e, tok)
        ins = fn()
        self.ninst += 1
        if e == "pe" and not signal:
            self.pe_pending.append((reads, writes))
            return ins
        self.ccnt[e] += 1
        tok = (self.csem[e], self.ccnt[e], "c_" + e)
        ins.then_inc(self.csem[e], 1)
        if e == "pe":
            for (r, w) in self.pe_pending:
                self._commit(e, tok, r, w)
            self.pe_pending = []
        self._commit(e, tok, reads, writes)
        return ins

    def dma(self, q, out_ap, in_ap, reads=(), writes=()):
        reads = [r if isinstance(r, tuple) else (r, 0) for r in reads]
        writes = [w if isinstance(w, tuple) else (w, 0) for w in writes]
        if getattr(self, "cut", None) is not None and self.cut <= 0:
            return None
        for tok in self._deps(q, reads, writes):
            self._wait(q, tok)
        i = self.dcnt[q] % NDMA_SEM
        self._wait(q, self.dtok[q][i])
        n = self.dcnt[q] // NDMA_SEM + 1
        self.dcnt[q] += 1
        sem = self.dsem[q][i]
        ins = self.eng[q].dma_start(out=out_ap, in_=in_ap)
        ins.then_inc(sem, 16)
        self.ninst += 1
        tok = (sem, 16 * n, "d_%s%d" % (q, i))
        self.dtok[q][i] = tok
        self._commit(q, tok, reads, writes)
        return ins

    def finish(self, out_bufs):
        for b in out_bufs:
            for k, s in b.st.items():
                self._wait("sp", s[0])

    def barrier(self):
        toks = [(self.csem[e], self.ccnt[e], "c_" + e) for e in self.csem if self.ccnt[e] > 0]
        for q in self.dtok:
            toks += [t for t in self.dtok[q] if t is not None]
        for e in self.eng:
            for t in toks:
                self._wait(e, t)


class PV:
    def __init__(self, buf, ap):
        self.buf, self.ap = buf, ap


class TS:
    pass


def pipe_gen(gens, depth=2, skew=3):
    it = iter(gens)
    active = []
    since = skew
    done = False
    while True:
        if not done and len(active) < depth and since >= skew:
            try:
                active.append(next(it))
                since = 0
            except StopIteration:
                done = True
        if not active:
            if done:
                break
            since = skew
            continue
        for g in list(active):
            try:
                next(g)
            except StopIteration:
                active.remove(g)
        since += 1
        yield


def run_pipe(gens, depth=2, skew=3):
    for _ in pipe_gen(gens, depth, skew):
        pass


def mix(ga, gb, nb):
    da = db = False
    while not (da and db):
        if not da:
            try:
                next(ga)
            except StopIteration:
                da = True
        for _ in range(nb if not da else 1000000):
            if db:
                break
            try:
                next(gb)
            except StopIteration:
                db = True

D = 2048
NCH = 16
EVEN_IN = 8192
EPS = 1e-6
LRU_SKEW = 3
SSD_SKEW = 1
MIN_SKEW = 2
SIN_SKEW = 2
ML_STEPS = 12


def host_cols_l0(p):
    def col(v):
        return np.ascontiguousarray(v.reshape(-1, 128).T)
    def col4(w):
        return np.ascontiguousarray(w.reshape(4, -1, 128).transpose(2, 1, 0).reshape(128, -1))
    cols = np.concatenate([
        col(p["e_norm"][0]),
        col4(p["e_conv_l_w"][0]),
        col(p["e_conv_l_b"][0]),
        col(p["e_lru_ba"][0]),
        col(p["e_lru_bx"][0]),
        col(p["e_lru_lam"][0]),
        col4(p["e_conv_m_w"][0]),
        col(p["e_conv_m_b"][0]),
    ], axis=1).astype(np.float32)
    return cols


def consts_host():
    ident = np.eye(128, dtype=np.float32)
    mask01 = np.triu(np.ones((128, 128), np.float32))
    negmask = np.where(mask01 > 0, 0.0, -30000.0).astype(np.float32)
    return np.concatenate([ident, mask01, negmask, np.ascontiguousarray(negmask.T)], axis=1)


class L0:
    def __init__(self, k, T, xin, xout):
        self.k = k
        self.T = T
        self.TT = T // 128
        self.xin, self.xout = xin, xout
        nc = k.nc
        self.w_in = k.dram("e_w_in", [D, EVEN_IN])
        self.w_out = k.dram("e_w_out", [4096, D])
        self.wa = k.dram("e_lru_wa", [16, 128, 128])
        self.wx = k.dram("e_lru_wx", [16, 128, 128])
        self.wq = k.dram("e_w_q", [4, 512, 256])
        self.wk = k.dram("e_w_k", [4, 512, 256])
        self.wv = k.dram("e_w_v", [4, 512, 512])
        self.wo = k.dram("e_w_o", [4, 512, 512])
        self.wif = k.dram("e_w_if", [4096, 8])
        self.cols_d = k.dram("e_cols", [128, 224])
        self.bif_d = k.dram("e_bif", [4, 2])
        self.mnorm_d = k.dram("e_mncol", [128, 16])
        self.const_d = k.dram("consts", [128, 512])
        self.NSLOT = 72
        self.scr = k.dram("wscr0", [self.NSLOT, 128, 4096], BF16, kind="Internal")

    def alloc(self):
        k, T, TT = self.k, self.T, self.TT
        s = self
        s.cols = k.sb("cols", [128, 224])
        s.bif = k.sb("bif", [4, 2])
        s.mncol = k.sb("mncol", [128, 16])
        s.cst = k.sb("cst", [128, 512])
        s.cstb = k.sb("cstb", [128, 512], BF16)
        s.ones = k.sb("ones", [128, 512])
        s.onesb = k.sb("onesb", [128, 8], BF16)
        s.kcol = k.sb("kcol", [128, 32])
        s.colsh = k.sb("colsh", [128, 224])
        s.mhalf = k.sb("mhalf", [128, 2])
        s.phalf = k.sb("phalf", [128, T])
        s.wab = k.sb("wab", [128, 16, 128], BF16)
        s.wxb = k.sb("wxb", [128, 16, 128], BF16)
        s.wifb = k.sb("wifb", [128, 32, 8], BF16)
        s.tail_l = k.sb("tail_l", [128, 16, 3])
        s.tail_m = k.sb("tail_m", [128, 16, 3])
        s.hlast = k.sb("hlast", [128, 16])
        s.CT = k.sb("CT", [128, 4, 2, 512])
        s.CTb = k.sb("CTb", [128, 4, 2, 512], BF16)
        s.nst = k.sb("nst", [128, 4, 2])
        s.nstb = k.sb("nstb", [128, 4, 2], BF16)
        s.Gl = k.sb("Gl", [4, 1])
        s.Ml = k.sb("Ml", [4, 1])
        s.xt = [k.sb("xt%d" % i, [128, D]) for i in range(2)]
        s.st4 = k.sb("st4", [128, 8])
        s.xnT = k.sb("xnT", [128, NCH, T], BF16)
        s.yT = k.sb("yT", [128, 16, T], BF16)
        s.SW = 256
        s.slab = [k.sb("slab%d" % i, [128, 16, 256], BF16) for i in range(4)]
        s.nslab = 0
        s.xe = [k.sb("xe%d" % i, [128, T + 3]) for i in range(2)]
        s.tsets = []
        for i in range(2):
            ts = TS()
            ts.xc = k.sb("xc%d" % i, [128, T])
            ts.xcb = k.sb("xcb%d" % i, [128, T], BF16)
            for nm in ["rr", "ii", "aa", "a2", "hh", "sz1", "th"]:
                setattr(ts, nm, k.sb("%s%d" % (nm, i), [128, T]))
            s.tsets.append(ts)
        s.xmT = k.sb("xmT", [128, 16, T], BF16)
        s.xmcT = k.sb("xmcT", [128, 16, T], BF16)
        s.szm = k.sb("szm", [128, 16, T], BF16)
        s.qT = k.sb("qT", [128, 4, 2, T], BF16)
        s.kT = k.sb("kT", [128, 4, 2, T], BF16)
        s.vTt = [k.sb("vTt%d" % i, [128, T], BF16) for i in range(2)]
        s.vtok = k.sb("vtok", [128, TT, 512], BF16)
        s.otok = k.sb("otok", [128, TT, 512], BF16)
        s.wsl = [k.sb("wsl%d" % i, [128, 4, 512], BF16) for i in range(2)]
        s.g_ig = k.sb("g_ig", [4, T])
        s.g_t1 = k.sb("g_t1", [4, T])
        s.g_t2 = k.sb("g_t2", [4, T])
        s.g_lf = k.sb("g_lf", [4, T])
        s.g_G = k.sb("g_G", [4, T])
        s.g_a = k.sb("g_a", [4, T])
        s.g_cm = k.sb("g_cm", [4, 8])
        s.g_Mn = k.sb("g_Mn", [4, 8])
        s.g_Mp = k.sb("g_Mp", [4, 8])
        s.g_nMp = k.sb("g_nMp", [4, 8])
        s.g_nMn = k.sb("g_nMn", [4, 8])
        s.g_dd = k.sb("g_dd", [4, 8])
        s.g_rows = k.sb("g_rows", [4, 4, T])
        s.g_cols = k.sb("g_cols", [128, TT, 4, 4])
        s.scT = [k.sb("scT%d" % i, [128, 128], BF16) for i in range(2)]
        s.kws = [k.sb("kws%d" % i, [128, 256], BF16) for i in range(2)]
        s.hb = [k.sb("hb%d" % i, [128, 512]) for i in range(2)]
        s.hb2 = [k.sb("hb2%d" % i, [128, 512]) for i in range(2)]
        s.ytok = [k.sb("ytok%d" % i, [128, 512], BF16) for i in range(2)]
        s.sm = [k.sb("sm%d" % i, [128, 8]) for i in range(2)]
        s.psA = [k.ps("psA%d" % i, [128, 512]) for i in range(4)]
        s.npsA = 0
        s.npsB = 0
        s.psT = [k.ps("psT%d" % i, [128, 1024], BF16) for i in range(2)]
        s.npsT = 0
        s.psS = k.ps("psS", [128, 512])
        s.psG = k.ps("psG", [128, 512])

    def rstd(self, buf, ap, n):
        k, s = self.k, self
        nc = k.nc
        P = ap.shape[0]
        k.op("dve", lambda: nc.vector.tensor_scalar(out=ap, in0=ap, scalar1=1.0 / n, scalar2=EPS, op0=ALU.mult, op1=ALU.add), reads=[buf], writes=[buf])
        k.op("act", lambda: nc.scalar.activation(out=ap, in_=ap, func=AF.Ln), reads=[buf], writes=[buf])
        k.op("act", lambda: nc.scalar.activation(out=ap, in_=ap, func=AF.Exp, scale=-0.5), reads=[buf], writes=[buf])

    def nextA(self):
        b = self.psA[self.npsA % len(self.psA)]
        self.npsA += 1
        return b

    def nextB(self):
        b = self.psA[2 + self.npsB % 2]
        self.npsB += 1
        return b

    def nextT(self):
        i = self.npsT % 2
        self.npsT += 1
        return i

    def _cached_load(self, dst, src_ap):
        k, s = self.k, self
        if not hasattr(s, "wcache"):
            s.wcache = {}
            s.nq = 0
        shp = tuple(src_ap.shape)
        key = (src_ap.name, src_ap.offset, shp)
        dview = dst.t[:, 0:shp[1], 0:shp[2]]
        if key not in s.wcache:
            slot = len(s.wcache)
            assert slot < s.NSLOT, slot
            s.wcache[key] = slot
            k.dma("pool", dview, src_ap, writes=[dst])
            sv = s.scr.t[slot][:, 0:shp[1] * shp[2]].rearrange("p (c f) -> p c f", f=shp[2])
            k.dma("sp", sv, dview, reads=[dst], writes=[(s.scr, slot)])
        else:
            slot = s.wcache[key]
            sv = s.scr.t[slot][:, 0:shp[1] * shp[2]].rearrange("p (c f) -> p c f", f=shp[2])
            k.dma("sp", dview, sv, reads=[(s.scr, slot)], writes=[dst])

    def load_slab(self, src_ap, pool=None):
        if pool is None:
            sl = self.slab[self.nslab % len(self.slab)]
            self.nslab += 1
        else:
            idx, cnt = pool
            sl = self.slab[idx[cnt[0] % len(idx)]]
            cnt[0] += 1
        self._cached_load(sl, src_ap)
        return sl

    def setup(self):
        k, s = self.k, self
        nc = k.nc
        k.dma("sp", s.cols[:, :], s.cols_d[:, :], writes=[s.cols])
        k.dma("sp", s.bif[:, :], s.bif_d[:, :], writes=[s.bif])
        k.dma("sp", s.mncol[:, :], s.mnorm_d[:, :], writes=[s.mncol])
        k.dma("sp", s.cst[:, :], s.const_d[:, :], writes=[s.cst])
        k.dma("pool", s.cstb[:, :], s.const_d[:, :], writes=[s.cstb])
        k.dma("pool", s.wab[:, :, :], s.wa.t.rearrange("n i j -> i n j"), writes=[s.wab])
        k.dma("pool", s.wxb[:, :, :], s.wx.t.rearrange("n i j -> i n j"), writes=[s.wxb])
        k.dma("pool", s.wifb[:, :, :], s.wif.t.rearrange("(c p) g -> p c g", p=128), writes=[s.wifb])
        k.op("dve", lambda: nc.vector.memset(s.ones[:, :], 1.0), writes=[s.ones])
        k.op("dve", lambda: nc.vector.memset(s.onesb[:, :], 1.0), writes=[s.onesb])
        k.op("dve", lambda: nc.vector.memset(s.phalf[:, :], 0.5), writes=[s.phalf])
        k.op("dve", lambda: nc.vector.memset(s.mhalf[:, 0:1], -0.5), writes=[s.mhalf])
        k.op("dve", lambda: nc.vector.memset(s.mhalf[:, 1:2], 0.5), reads=[s.mhalf], writes=[s.mhalf])
        k.op("dve", lambda: nc.vector.tensor_scalar(out=s.colsh[:, :], in0=s.cols[:, :], scalar1=0.5, scalar2=None, op0=ALU.mult), reads=[s.cols], writes=[s.colsh])
        k.op("dve", lambda: nc.vector.tensor_scalar(out=s.mncol[:, :], in0=s.mncol[:, :], scalar1=0.5, scalar2=None, op0=ALU.mult), reads=[s.mncol], writes=[s.mncol])
        for b in [s.tail_l, s.tail_m, s.hlast, s.CT, s.CTb, s.nst, s.nstb, s.Gl, s.Ml]:
            k.op("dve", (lambda b=b: nc.vector.memset(b.t[:], 0.0)), writes=[b])
        lam = s.cols[:, 128:144]
        k.op("act", lambda: nc.scalar.activation(out=s.kcol[:, 0:16], in_=lam, func=AF.Exp, scale=-1.0),
             reads=[s.cols], writes=[s.kcol])
        k.op("act", lambda: nc.scalar.activation(out=s.kcol[:, 0:16], in_=s.kcol[:, 0:16], func=AF.Ln, bias=1.0),
             reads=[s.kcol], writes=[s.kcol])
        k.op("dve", lambda: nc.vector.tensor_scalar(out=s.kcol[:, 0:16], in0=s.kcol[:, 0:16], scalar1=-4.0, scalar2=None, op0=ALU.mult),
             reads=[s.kcol], writes=[s.kcol])

    def rmsnorm_g(self, b, stage=None):
        k, s, T, TT = self.k, self, self.T, self.TT
        nc = k.nc
        if not hasattr(s, "xnb_all"):
            s.xnb_all = [k.sb(s.__class__.__name__ + "xnba%d" % i, [128, D], BF16) for i in range(TT)]
        for tt in range(TT):
            r0 = b * T + tt * 128
            if stage is None:
                src, keys = s.xt[tt][:, :], [s.xt[tt]]
            else:
                src, keys = stage[tt]
            k.dma("sp", src, s.xin[r0:r0 + 128, :], reads=[(s.xin, r0)], writes=keys)
            yield
            k.op("act", lambda src=src, tt=tt: nc.scalar.activation(out=s.xnb_all[tt][:, :], in_=src, func=AF.Square, accum_out=s.st4[:, tt:tt + 1]),
                 reads=keys, writes=[s.st4, s.xnb_all[tt]])
            yield
            s.rstd(s.st4, s.st4[:, tt:tt + 1], D)
            yield
            k.op("act", lambda src=src, tt=tt: nc.scalar.activation(out=s.xnb_all[tt][:, :], in_=src, func=AF.Copy, scale=s.st4[:, tt:tt + 1]),
                 reads=keys + [s.st4], writes=[s.xnb_all[tt]])
            yield
        ident = s.cstb[:, 0:128]
        for c in range(NCH):
            i = s.nextT()
            for tt in range(TT):
                k.op("pe", lambda c=c, tt=tt, i=i: nc.tensor.transpose(out=s.psT[i][:, tt * 128:(tt + 1) * 128], in_=s.xnb_all[tt][:, c * 128:(c + 1) * 128], identity=ident),
                     reads=[s.xnb_all[tt], s.cstb], writes=[s.psT[i]], signal=(tt == TT - 1))
            k.op("dve", lambda c=c, i=i: nc.vector.tensor_scalar(out=s.xnT[:, c, :], in0=s.psT[i][:, 0:T], scalar1=s.cols[:, c:c + 1], scalar2=None, op0=ALU.mult),
                 reads=[s.psT[i], s.cols], writes=[(s.xnT, c)])
            yield

    def rmsnorm_T(self, b):
        for _ in self.rmsnorm_g(b):
            pass

    def load_resid(self, b):
        k, s, T, TT = self.k, self, self.T, self.TT
        for tt in range(TT):
            r0 = b * T + tt * 128
            k.dma("sp", s.xt[tt][:, :], s.xin[r0:r0 + 128, :], reads=[(s.xin, r0)], writes=[s.xt[tt]])

    def inproj_tile(self, sl, j, ps):
        k, s, T = self.k, self, self.T
        nc = k.nc
        for c in range(NCH):
            k.op("pe", lambda c=c: nc.tensor.matmul(ps[:, 0:T], lhsT=sl[:, c, j * 128:(j + 1) * 128], rhs=s.xnT[:, c, :], start=(c == 0), stop=(c == NCH - 1)),
                 reads=[sl, (s.xnT, c)], writes=[ps], signal=(c == NCH - 1))

    def conv_g(self, ps, xe, tail, ci, wbase, bbase, out, cols=None, offload=False, xe_act=False):
        k, s, T = self.k, self, self.T
        nc = k.nc
        cols = s.cols if cols is None else cols
        w = lambda tap: cols[:, wbase + ci * 4 + tap: wbase + ci * 4 + tap + 1]
        k.op("dve", lambda: nc.vector.tensor_copy(out=xe[:, 0:3], in_=tail[:, ci, :]), reads=[(tail, ci)], writes=[xe])
        if offload or xe_act:
            k.op("act", lambda: nc.scalar.copy(out=xe[:, 3:T + 3], in_=ps.ap), reads=[ps.buf, xe], writes=[xe])
        else:
            k.op("dve", lambda: nc.vector.tensor_copy(out=xe[:, 3:T + 3], in_=ps.ap), reads=[ps.buf, xe], writes=[xe])
        k.op("act", lambda: nc.scalar.activation(out=out[:, :], in_=ps.ap, func=AF.Identity, scale=w(3), bias=cols[:, bbase + ci: bbase + ci + 1]),
             reads=[ps.buf, cols], writes=[out])
        yield
        for tap in range(3):
            k.op("dve", lambda tap=tap: nc.vector.scalar_tensor_tensor(out=out[:, :], in0=xe[:, tap:tap + T], scalar=w(tap), in1=out[:, :], op0=ALU.mult, op1=ALU.add),
                 reads=[xe, cols, out], writes=[out])
        if offload:
            k.op("pool", lambda: nc.gpsimd.tensor_copy(out=tail[:, ci, :], in_=xe[:, T:T + 3]), reads=[xe], writes=[(tail, ci)])
        else:
            k.op("dve", lambda: nc.vector.tensor_copy(out=tail[:, ci, :], in_=xe[:, T:T + 3]), reads=[xe], writes=[(tail, ci)])
        yield

    def inproj_view(self, sl, j, ps, half):
        k, s, T = self.k, self, self.T
        nc = k.nc
        v = PV(ps, ps[:, half * T:(half + 1) * T])
        for c in range(NCH):
            k.op("pe", lambda c=c: nc.tensor.matmul(v.ap, lhsT=sl[:, c, j * 128:(j + 1) * 128], rhs=s.xnT[:, c, :], start=(c == 0), stop=(c == NCH - 1)),
                 reads=[sl, (s.xnT, c)], writes=[ps], signal=(c == NCH - 1))
        return v

    def lru_tile(self, ci, slx, slz, jx, ts):
        k, s, T = self.k, self, self.T
        nc = k.nc
        V, A, G = nc.vector, nc.scalar, nc.gpsimd
        p1 = s.psA[ci % 2]
        vx = s.inproj_view(slx, jx, p1, 0)
        vz = s.inproj_view(slz, jx, p1, 1)
        yield
        xe = s.xe[ci % 2]
        yield from s.conv_g(vx, xe, s.tail_l, ci, 16, 80, ts.xc)
        k.op("pool", lambda: G.tensor_copy(out=ts.xcb[:, :], in_=ts.xc[:, :]), reads=[ts.xc], writes=[ts.xcb])
        k.op("act", lambda: A.activation(out=ts.sz1[:, :], in_=vz.ap, func=AF.Tanh, scale=0.5), reads=[p1], writes=[ts.sz1])
        k.op("dve", lambda: V.scalar_tensor_tensor(out=ts.sz1[:, :], in0=ts.sz1[:, :], scalar=1.0, in1=vz.ap, op0=ALU.add, op1=ALU.mult), reads=[ts.sz1, p1], writes=[ts.sz1])
        yield
        p2 = p1
        k.op("pe", lambda: nc.tensor.matmul(p2[:, 0:T], lhsT=s.wab[:, ci, :], rhs=ts.xcb[:, :], start=True, stop=True),
             reads=[s.wab, ts.xcb], writes=[p2], signal=False)
        k.op("pe", lambda: nc.tensor.matmul(p2[:, T:2 * T], lhsT=s.wxb[:, ci, :], rhs=ts.xcb[:, :], start=True, stop=True),
             reads=[s.wxb, ts.xcb], writes=[p2])
        yield
        k.op("act", lambda: A.activation(out=ts.rr[:, :], in_=p2[:, 0:T], func=AF.Tanh, scale=0.5, bias=s.colsh[:, 96 + ci:97 + ci]),
             reads=[p2, s.colsh], writes=[ts.rr])
        k.op("act", lambda: A.activation(out=ts.ii[:, :], in_=p2[:, T:2 * T], func=AF.Tanh, scale=0.5, bias=s.colsh[:, 112 + ci:113 + ci]),
             reads=[p2, s.colsh], writes=[ts.ii])
        k.op("act", lambda: A.activation(out=ts.aa[:, :], in_=ts.rr[:, :], func=AF.Exp, scale=s.kcol[:, ci:ci + 1], bias=s.kcol[:, ci:ci + 1]),
             reads=[ts.rr, s.kcol], writes=[ts.aa])
        yield
        k.op("pool", lambda: G.tensor_tensor(out=ts.a2[:, :], in0=ts.aa[:, :], in1=ts.aa[:, :], op=ALU.mult), reads=[ts.aa], writes=[ts.a2])
        k.op("act", lambda: A.activation(out=ts.a2[:, :], in_=ts.a2[:, :], func=AF.Sqrt, scale=-1.0, bias=1.0), reads=[ts.a2], writes=[ts.a2])
        k.op("dve", lambda: V.scalar_tensor_tensor(out=ts.ii[:, :], in0=ts.ii[:, :], scalar=1.0, in1=ts.xc[:, :], op0=ALU.add, op1=ALU.mult),
             reads=[ts.ii, ts.xc], writes=[ts.ii])
        yield
        k.op("dve", lambda: V.scalar_tensor_tensor(out=ts.ii[:, :], in0=ts.ii[:, :], scalar=0.5, in1=ts.a2[:, :], op0=ALU.mult, op1=ALU.mult),
             reads=[ts.ii, ts.a2], writes=[ts.ii])
        k.op("dve", lambda: V.tensor_tensor_scan(out=ts.hh[:, :], data0=ts.aa[:, :], data1=ts.ii[:, :], initial=s.hlast[:, ci:ci + 1], op0=ALU.mult, op1=ALU.add),
             reads=[ts.aa, ts.ii, s.hlast], writes=[ts.hh])
        k.op("dve", lambda: V.tensor_copy(out=s.hlast[:, ci:ci + 1], in_=ts.hh[:, T - 1:T]),
             reads=[ts.hh], writes=[s.hlast])
        k.op("dve", lambda: V.scalar_tensor_tensor(out=s.yT[:, ci, :], in0=ts.hh[:, :], scalar=0.5, in1=ts.sz1[:, :], op0=ALU.mult, op1=ALU.mult),
             reads=[ts.hh, ts.sz1], writes=[(s.yT, ci)])
        yield

    def lru_tiles(self, b):
        s = self
        wv = s.w_in.t.rearrange("(c p) f -> p c f", p=128)
        SW = s.SW
        nt = SW // 128
        for g in range(2048 // SW):
            slx = s.load_slab(wv[:, :, g * SW:(g + 1) * SW])
            slz = s.load_slab(wv[:, :, 4096 + g * SW: 4096 + (g + 1) * SW])
            for j in range(nt):
                ci = g * nt + j
                yield s.lru_tile(ci, slx, slz, j, s.tsets[ci % 2])

    def lru(self, b):
        run_pipe(self.lru_tiles(b), depth=2, skew=LRU_SKEW)

    def lru_mlstm(self, b):
        s = self

        def ml():
            yield from s.mlstm_gates(b)
            yield from s.mlstm_core(b)
        mix(pipe_gen(s.lru_tiles(b), depth=2, skew=LRU_SKEW), ml(), ML_STEPS)

    def outproj_g(self, b, half, store=True, src=None, banks=None, slabpool=None):
        k, s, T, TT = self.k, self, self.T, self.TT
        nc = k.nc
        wv = s.w_out.t.rearrange("(c p) d -> p c d", p=128)
        yT = s.yT if src is None else src
        nb = 0
        for db in range(8):
            sl = s.load_slab(wv[:, half * 16:(half + 1) * 16, db * 256:(db + 1) * 256], pool=slabpool)
            for tt in range(TT):
                xt = s.xt[tt]
                if banks is None:
                    ps = s.nextA()
                else:
                    ps = banks[nb % len(banks)]
                    nb += 1
                for c in range(16):
                    k.op("pe", lambda sl=sl, c=c, tt=tt, ps=ps: nc.tensor.matmul(ps[:, 0:256], lhsT=yT[:, c, tt * 128:(tt + 1) * 128], rhs=sl[:, c, 0:256], start=(c == 0), stop=(c == 15)),
                         reads=[sl, (yT, c)], writes=[ps], signal=(c == 15))
                    if c % 4 == 3:
                        yield
                k.op("dve", lambda xt=xt, ps=ps, db=db: nc.vector.tensor_tensor(out=xt[:, db * 256:(db + 1) * 256], in0=xt[:, db * 256:(db + 1) * 256], in1=ps[:, 0:256], op=ALU.add),
                     reads=[ps, xt], writes=[xt])
        if half == 1 and store:
            for tt in range(TT):
                r0 = b * T + tt * 128
                k.dma("pool", s.xout[r0:r0 + 128, :], s.xt[tt][:, :], reads=[s.xt[tt]], writes=[(s.xout, r0)])

    def outproj(self, b, half, store=True, src=None):
        for _ in self.outproj_g(b, half, store=store, src=src):
            pass

    def conv_tile(self, ci, sl, j, ts, tail, wbase, bbase, dst, raw_dst=None, xe_act=False):
        k, s, T = self.k, self, self.T
        nc = k.nc
        ps = s.nextA()
        v = s.inproj_view(sl, j, ps, 0)
        yield
        if raw_dst is not None:
            k.op("act", lambda: nc.scalar.copy(out=raw_dst[:, ci, :], in_=v.ap), reads=[ps], writes=[(raw_dst, ci)])
        yield from s.conv_g(v, s.xe[ci % 2], tail, ci, wbase, bbase, ts.xc, cols=s.colsh, xe_act=xe_act)
        k.op("act", lambda: nc.scalar.activation(out=ts.th[:, :], in_=ts.xc[:, :], func=AF.Tanh), reads=[ts.xc], writes=[ts.th])
        yield
        k.op("dve", lambda: nc.vector.scalar_tensor_tensor(out=dst[:, ci, :], in0=ts.th[:, :], scalar=1.0, in1=ts.xc[:, :], op0=ALU.add, op1=ALU.mult),
             reads=[ts.th, ts.xc], writes=[(dst, ci)])
        yield

    def silu_tile(self, ci, sl, j, dst, ts):
        k, s, T = self.k, self, self.T
        nc = k.nc
        ps = s.nextA()
        v = s.inproj_view(sl, j, ps, 0)
        yield
        k.op("act", lambda: nc.scalar.activation(out=ts.th[:, :], in_=v.ap, func=AF.Tanh, scale=0.5), reads=[ps], writes=[ts.th])
        yield
        k.op("dve", lambda: nc.vector.scalar_tensor_tensor(out=dst[:, ci, :], in0=ts.th[:, :], scalar=1.0, in1=v.ap, op0=ALU.add, op1=ALU.mult),
             reads=[ts.th, ps], writes=[(dst, ci)])
        yield

    def mlstm_in(self, b):
        s = self
        wv = s.w_in.t.rearrange("(c p) f -> p c f", p=128)
        SW = s.SW
        nt = SW // 128

        def tiles():
            for g in range(2048 // SW):
                sl = s.load_slab(wv[:, :, 2048 + g * SW: 2048 + (g + 1) * SW])
                for j in range(nt):
                    ci = g * nt + j
                    yield s.conv_tile(ci, sl, j, s.tsets[ci % 2], s.tail_m, 144, 208, s.xmcT, raw_dst=s.xmT)
            for g in range(2048 // SW):
                sl = s.load_slab(wv[:, :, 6144 + g * SW: 6144 + (g + 1) * SW])
                for j in range(nt):
                    ci = g * nt + j
                    yield s.silu_tile(ci, sl, j, s.szm, s.tsets[ci % 2])
        run_pipe(tiles(), depth=2, skew=MIN_SKEW)

    def load_w(self, src_ap):
        s = self
        if not hasattr(s, "nwsl"):
            s.nwsl = 0
        w = s.wsl[s.nwsl % len(s.wsl)]
        s.nwsl += 1
        s._cached_load(w, src_ap)
        return w

    def mlstm_gates(self, b):
        k, s, T, TT = self.k, self, self.T, self.TT
        nc = k.nc
        for h in range(4):
            for (wd, dst, scl) in ((s.wq, s.qT, 1.0), (s.wk, s.kT, 1.0 / 16.0)):
                w = s.load_w(wd.t[h].rearrange("(c p) j -> p c j", p=128))
                for jt in range(2):
                    ps = s.nextB()
                    for ic in range(4):
                        yield
                        k.op("pe", lambda w=w, ic=ic, jt=jt, ps=ps, h=h: nc.tensor.matmul(ps[:, 0:T], lhsT=w[:, ic, jt * 128:(jt + 1) * 128], rhs=s.xmcT[:, 4 * h + ic, :], start=(ic == 0), stop=(ic == 3)),
                             reads=[w, (s.xmcT, 4 * h + ic)], writes=[ps], signal=(ic == 3))
                    yield
                    k.op("act", lambda dst=dst, h=h, jt=jt, ps=ps, scl=scl: nc.scalar.mul(out=dst[:, h, jt, :], in_=ps[:, 0:T], mul=scl),
                         reads=[ps], writes=[(dst, (h, jt))])
        n = 0
        def gmm(c, rhs_ap, rbuf, last):
            nonlocal n
            first = (n == 0)
            k.op("pe", lambda: nc.tensor.matmul(s.psG[0:4, 0:T], lhsT=s.wifb[:, c, 0:4], rhs=rhs_ap, start=first, stop=last),
                 reads=[s.wifb, rbuf], writes=[s.psG], signal=False)
            k.op("pe", lambda: nc.tensor.matmul(s.psS[0:4, 0:T], lhsT=s.wifb[:, c, 4:8], rhs=rhs_ap, start=first, stop=last),
                 reads=[s.wifb, rbuf], writes=[s.psS], signal=last)
            n += 1
        for h in range(4):
            for jt in range(2):
                yield
                gmm(h * 2 + jt, s.qT[:, h, jt, :], (s.qT, (h, jt)), False)
                yield
                gmm(8 + h * 2 + jt, s.kT[:, h, jt, :], (s.kT, (h, jt)), False)
        for h in range(4):
            w = s.load_w(s.wv.t[h].rearrange("(c p) j -> p c j", p=128))
            for jt in range(4):
                ps = s.nextB()
                for ic in range(4):
                    yield
                    k.op("pe", lambda w=w, ic=ic, jt=jt, ps=ps, h=h: nc.tensor.matmul(ps[:, 0:T], lhsT=w[:, ic, jt * 128:(jt + 1) * 128], rhs=s.xmT[:, 4 * h + ic, :], start=(ic == 0), stop=(ic == 3)),
                         reads=[w, (s.xmT, 4 * h + ic)], writes=[ps], signal=(ic == 3))
                vt = s.vTt[(h * 4 + jt) % 2]
                yield
                k.op("act", lambda vt=vt, ps=ps: nc.scalar.copy(out=vt[:, :], in_=ps[:, 0:T]), reads=[ps], writes=[vt])
                yield
                gmm(16 + h * 4 + jt, vt[:, :], vt, (h == 3 and jt == 3))
        V, A = nc.vector, nc.scalar
        yield
        k.op("act", lambda: A.activation(out=s.g_ig[:, :], in_=s.psG[0:4, 0:T], func=AF.Identity, bias=s.bif[:, 0:1]), reads=[s.psG, s.bif], writes=[s.g_ig])
        yield
        k.op("act", lambda: A.activation(out=s.g_t1[:, :], in_=s.psS[0:4, 0:T], func=AF.Identity, bias=s.bif[:, 1:2]), reads=[s.psS, s.bif], writes=[s.g_t1])
        yield
        k.op("dve", lambda: V.tensor_scalar(out=s.g_t2[:, :], in0=s.g_t1[:, :], scalar1=-1.0, scalar2=None, op0=ALU.mult), reads=[s.g_t1], writes=[s.g_t2])
        yield
        k.op("dve", lambda: V.tensor_tensor(out=s.g_t2[:, :], in0=s.g_t2[:, :], in1=s.g_t1[:, :], op=ALU.min), reads=[s.g_t1, s.g_t2], writes=[s.g_t2])
        yield
        k.op("act", lambda: A.activation(out=s.g_t2[:, :], in_=s.g_t2[:, :], func=AF.Exp), reads=[s.g_t2], writes=[s.g_t2])
        yield
        k.op("act", lambda: A.activation(out=s.g_t2[:, :], in_=s.g_t2[:, :], func=AF.Ln, bias=1.0), reads=[s.g_t2], writes=[s.g_t2])
        yield
        k.op("dve", lambda: V.tensor_scalar(out=s.g_lf[:, :], in0=s.g_t1[:, :], scalar1=0.0, scalar2=None, op0=ALU.min), reads=[s.g_t1], writes=[s.g_lf])
        yield
        k.op("dve", lambda: V.tensor_tensor(out=s.g_lf[:, :], in0=s.g_lf[:, :], in1=s.g_t2[:, :], op=ALU.subtract), reads=[s.g_lf, s.g_t2], writes=[s.g_lf])
        yield
        k.op("dve", lambda: V.tensor_tensor_scan(out=s.g_G[:, :], data0=s.ones[0:4, 0:T], data1=s.g_lf[:, :], initial=s.Gl[:, 0:1], op0=ALU.mult, op1=ALU.add),
             reads=[s.ones, s.g_lf, s.Gl], writes=[s.g_G])
        yield
        k.op("dve", lambda: V.tensor_copy(out=s.Gl[:, 0:1], in_=s.g_G[:, T - 1:T]), reads=[s.g_G], writes=[s.Gl])
        yield
        k.op("dve", lambda: V.tensor_tensor(out=s.g_a[:, :], in0=s.g_ig[:, :], in1=s.g_G[:, :], op=ALU.subtract), reads=[s.g_ig, s.g_G], writes=[s.g_a])
        yield
        k.op("dve", lambda: V.tensor_reduce(out=s.g_cm[:, 0:TT], in_=s.g_a[:, :].rearrange("p (c l) -> p c l", l=128), axis=AX.X, op=ALU.max), reads=[s.g_a], writes=[s.g_cm])
        yield
        k.op("dve", lambda: V.tensor_tensor_scan(out=s.g_Mn[:, 0:TT], data0=s.g_cm[:, 0:TT], data1=s.g_cm[:, 0:TT], initial=s.Ml[:, 0:1], op0=ALU.max, op1=ALU.max),
             reads=[s.g_cm, s.Ml], writes=[s.g_Mn])
        yield
        k.op("dve", lambda: V.tensor_copy(out=s.g_Mp[:, 0:1], in_=s.Ml[:, 0:1]), reads=[s.Ml], writes=[s.g_Mp])
        if TT > 1:
            yield
            k.op("dve", lambda: V.tensor_copy(out=s.g_Mp[:, 1:TT], in_=s.g_Mn[:, 0:TT - 1]), reads=[s.g_Mn, s.g_Mp], writes=[s.g_Mp])
        yield
        k.op("dve", lambda: V.tensor_copy(out=s.Ml[:, 0:1], in_=s.g_Mn[:, TT - 1:TT]), reads=[s.g_Mn, s.g_Mp], writes=[s.Ml])
        yield
        k.op("dve", lambda: V.tensor_scalar(out=s.g_nMp[:, 0:TT], in0=s.g_Mp[:, 0:TT], scalar1=-1.0, scalar2=None, op0=ALU.mult), reads=[s.g_Mp], writes=[s.g_nMp])
        yield
        k.op("dve", lambda: V.tensor_scalar(out=s.g_nMn[:, 0:TT], in0=s.g_Mn[:, 0:TT], scalar1=-1.0, scalar2=None, op0=ALU.mult), reads=[s.g_Mn], writes=[s.g_nMn])
        yield
        k.op("dve", lambda: V.tensor_tensor(out=s.g_dd[:, 0:TT], in0=s.g_Mp[:, 0:TT], in1=s.g_Mn[:, 0:TT], op=ALU.subtract), reads=[s.g_Mp, s.g_Mn], writes=[s.g_dd])
        for c in range(TT):
            sl_ = slice(c * 128, (c + 1) * 128)
            yield
            k.op("act", lambda c=c, sl_=sl_: A.activation(out=s.g_rows[:, 0, sl_], in_=s.g_a[:, sl_], func=AF.Exp, bias=s.g_nMp[:, c:c + 1]), reads=[s.g_a, s.g_nMp], writes=[s.g_rows])
            yield
            k.op("act", lambda c=c, sl_=sl_: A.activation(out=s.g_rows[:, 1, sl_], in_=s.g_a[:, sl_], func=AF.Exp, bias=s.g_nMn[:, c:c + 1]), reads=[s.g_a, s.g_nMn], writes=[s.g_rows])
            yield
            k.op("act", lambda c=c, sl_=sl_: A.activation(out=s.g_rows[:, 2, sl_], in_=s.g_G[:, sl_], func=AF.Exp, scale=-1.0, bias=s.g_nMp[:, c:c + 1]), reads=[s.g_G, s.g_nMp], writes=[s.g_rows])
            yield
            k.op("act", lambda c=c, sl_=sl_: A.activation(out=s.g_rows[:, 3, sl_], in_=s.g_a[:, sl_], func=AF.Exp, scale=0.0, bias=s.g_dd[:, c:c + 1]), reads=[s.g_a, s.g_dd], writes=[s.g_rows])
        for c in range(TT):
            for q in range(4):
                o = (c * 4 + q) * 4
                last = (c == TT - 1 and q == 3)
                yield
                k.op("pe", lambda c=c, q=q, o=o: nc.tensor.transpose(out=s.psG[:, o:o + 4], in_=s.g_rows[:, q, c * 128:(c + 1) * 128], identity=s.cst[0:4, 0:4]),
                     reads=[s.g_rows, s.cst], writes=[s.psG], signal=last)
        yield
        k.op("dve", lambda: V.tensor_copy(out=s.g_cols[:, :, :, :].rearrange("p c q h -> p (c q h)"), in_=s.psG[:, 0:TT * 16]), reads=[s.psG], writes=[s.g_cols])

    def mlstm_core(self, b):
        k, s, T, TT = self.k, self, self.T, self.TT
        nc = k.nc
        V, A, P = nc.vector, nc.scalar, nc.tensor
        identb = s.cstb[:, 0:128]
        mask01 = s.cstb[:, 128:256]
        it = 0
        for h in range(4):
            wv_ = s.load_w(s.wv.t[h].rearrange("(c p) j -> p c j", p=128))
            wo_ = s.load_w(s.wo.t[h].rearrange("(c p) j -> p c j", p=128))
            for tt in range(TT):
                for (w, dst, fn) in ((wv_, s.vtok, AF.Copy), (wo_, s.otok, AF.Sigmoid)):
                    ps = s.nextB()
                    for ic in range(4):
                        yield
                        k.op("pe", lambda w=w, ic=ic, tt=tt, ps=ps, h=h: P.matmul(ps[:, 0:512], lhsT=s.xmT[:, 4 * h + ic, tt * 128:(tt + 1) * 128], rhs=w[:, ic, 0:512], start=(ic == 0), stop=(ic == 3)),
                             reads=[w, (s.xmT, 4 * h + ic)], writes=[ps], signal=(ic == 3))
                    yield
                    k.op("act", lambda dst=dst, tt=tt, ps=ps, fn=fn: A.activation(out=dst[:, tt, :], in_=ps[:, 0:512], func=fn), reads=[ps], writes=[(dst, tt)])
            for c in range(TT):
                cs = slice(c * 128, (c + 1) * 128)
                col = lambda q: s.g_cols[:, c, q, h:h + 1]
                scT, kws, hb, hb2, ytok, sm = s.scT[it % 2], s.kws[it % 2], s.hb[it % 2], s.hb2[it % 2], s.ytok[it % 2], s.sm[it % 2]
                it += 1
                for dt in range(2):
                    yield
                    k.op("pe", lambda dt=dt: P.matmul(s.psS[:, 0:128], lhsT=s.kT[:, h, dt, cs], rhs=s.qT[:, h, dt, cs], start=(dt == 0), stop=(dt == 1)),
                         reads=[(s.kT, (h, dt)), (s.qT, (h, dt))], writes=[s.psS], signal=(dt == 1))
                yield
                k.op("dve", lambda: V.scalar_tensor_tensor(out=scT[:, :], in0=s.psS[:, 0:128], scalar=col(0), in1=mask01, op0=ALU.mult, op1=ALU.mult),
                     reads=[s.psS, s.g_cols, s.cstb], writes=[scT])
                psN = s.nextB()
                yield
                k.op("pe", lambda: P.matmul(psN[:, 0:512], lhsT=scT[:, :], rhs=s.vtok[:, c, :], start=True, stop=False), reads=[scT, (s.vtok, c)], writes=[psN], signal=False)
                for dt in range(2):
                    yield
                    k.op("pe", lambda dt=dt: P.matmul(psN[:, 0:512], lhsT=s.qT[:, h, dt, cs], rhs=s.CTb[:, h, dt, :], start=False, stop=(dt == 1)),
                         reads=[(s.qT, (h, dt)), (s.CTb, h)], writes=[psN], signal=(dt == 1))
                yield
                k.op("pe", lambda: P.matmul(s.psG[:, 0:1], lhsT=scT[:, :], rhs=s.onesb[:, 0:1], start=True, stop=False), reads=[scT, s.onesb], writes=[s.psG], signal=False)
                for dt in range(2):
                    yield
                    k.op("pe", lambda dt=dt: P.matmul(s.psG[:, 0:1], lhsT=s.qT[:, h, dt, cs], rhs=s.nstb[:, h, dt:dt + 1], start=False, stop=(dt == 1)),
                         reads=[(s.qT, (h, dt)), (s.nstb, h)], writes=[s.psG], signal=(dt == 1))
                yield
                k.op("dve", lambda: V.tensor_scalar(out=sm[:, 3:4], in0=s.psG[:, 0:1], scalar1=-1.0, scalar2=None, op0=ALU.mult), reads=[s.psG], writes=[sm])
                yield
                k.op("dve", lambda: V.tensor_tensor(out=sm[:, 0:1], in0=s.psG[:, 0:1], in1=sm[:, 3:4], op=ALU.max), reads=[s.psG, sm], writes=[sm])
                yield
                k.op("dve", lambda: V.tensor_scalar(out=sm[:, 0:1], in0=sm[:, 0:1], scalar1=col(2), scalar2=None, op0=ALU.max), reads=[sm, s.g_cols], writes=[sm])
                yield
                k.op("dve", lambda: V.reciprocal(out=sm[:, 1:2], in_=sm[:, 0:1]), reads=[sm], writes=[sm])
                yield
                k.op("act", lambda: A.activation(out=hb[:, :], in_=psN[:, 0:512], func=AF.Copy, scale=sm[:, 1:2]), reads=[psN, sm], writes=[hb])
                yield
                k.op("act", lambda: A.activation(out=hb2[:, :], in_=hb[:, :], func=AF.Square, accum_out=sm[:, 2:3]), reads=[hb], writes=[hb2, sm])
                yield
                s.rstd(sm, sm[:, 2:3], 512)
                yield
                k.op("dve", lambda: V.scalar_tensor_tensor(out=ytok[:, :], in0=hb[:, :], scalar=sm[:, 2:3], in1=s.otok[:, c, :], op0=ALU.mult, op1=ALU.mult),
                     reads=[hb, sm, (s.otok, c)], writes=[ytok])
                i = s.nextT()
                for vt in range(4):
                    yield
                    k.op("pe", lambda vt=vt: P.transpose(out=s.psT[i][:, vt * 128:(vt + 1) * 128], in_=ytok[:, vt * 128:(vt + 1) * 128], identity=identb),
                         reads=[ytok, s.cstb], writes=[s.psT[i]], signal=(vt == 3))
                for vt in range(4):
                    ft = 4 * h + vt
                    yield
                    k.op("dve", lambda vt=vt, ft=ft: V.scalar_tensor_tensor(out=s.xmcT[:, ft, cs], in0=s.psT[i][:, vt * 128:(vt + 1) * 128], scalar=s.mncol[:, ft:ft + 1], in1=s.szm[:, ft, cs], op0=ALU.mult, op1=ALU.mult),
                         reads=[s.psT[i], s.mncol, (s.szm, ft)], writes=[(s.xmcT, ft)])
                i2 = s.nextT()
                for dt in range(2):
                    yield
                    k.op("pe", lambda dt=dt: P.transpose(out=s.psT[i2][:, dt * 128:(dt + 1) * 128], in_=s.kT[:, h, dt, cs], identity=identb),
                         reads=[(s.kT, (h, dt)), s.cstb], writes=[s.psT[i2]], signal=(dt == 1))
                yield
                k.op("dve", lambda: V.tensor_scalar(out=kws[:, :], in0=s.psT[i2][:, 0:256], scalar1=col(1), scalar2=None, op0=ALU.mult), reads=[s.psT[i2], s.g_cols], writes=[kws])
                for dt in range(2):
                    psC = s.nextB()
                    yield
                    k.op("pe", lambda dt=dt, psC=psC: P.matmul(psC[:, 0:512], lhsT=kws[:, dt * 128:(dt + 1) * 128], rhs=s.vtok[:, c, :], start=True, stop=True), reads=[kws, (s.vtok, c)], writes=[psC])
                    yield
                    k.op("dve", lambda dt=dt, psC=psC: V.scalar_tensor_tensor(out=s.CT[:, h, dt, :], in0=s.CT[:, h, dt, :], scalar=col(3), in1=psC[:, 0:512], op0=ALU.mult, op1=ALU.add),
                         reads=[(s.CT, h), s.g_cols, psC], writes=[(s.CT, h)])
                    yield
                    k.op("act", lambda dt=dt: A.copy(out=s.CTb[:, h, dt, :], in_=s.CT[:, h, dt, :]), reads=[(s.CT, h)], writes=[(s.CTb, h)])
                    yield
                    k.op("pe", lambda dt=dt: P.matmul(s.psG[:, 8 + dt:9 + dt], lhsT=kws[:, dt * 128:(dt + 1) * 128], rhs=s.onesb[:, 0:1], start=True, stop=True), reads=[kws, s.onesb], writes=[s.psG])
                    yield
                    k.op("dve", lambda dt=dt: V.scalar_tensor_tensor(out=s.nst[:, h, dt:dt + 1], in0=s.nst[:, h, dt:dt + 1], scalar=col(3), in1=s.psG[:, 8 + dt:9 + dt], op0=ALU.mult, op1=ALU.add),
                         reads=[(s.nst, h), s.g_cols, s.psG], writes=[(s.nst, h)])
                    yield
                    k.op("act", lambda dt=dt: A.copy(out=s.nstb[:, h, dt:dt + 1], in_=s.nst[:, h, dt:dt + 1]), reads=[(s.nst, h)], writes=[(s.nstb, h)])


SSD_IN = 10304
XE_ACT = True


def host_cols_l1(p):
    def col(v):
        return np.ascontiguousarray(v.reshape(-1, 128).T)
    def col4(w):
        return np.ascontiguousarray(w.reshape(4, -1, 128).transpose(2, 1, 0).reshape(128, -1))
    cols = np.concatenate([col(p["o_norm"][0]), col4(p["o_conv_w"][0]), col(p["o_conv_b"][0]), col(p["o_gnorm"][0])], axis=1).astype(np.float32)
    rep = lambda v: np.broadcast_to(v.reshape(1, -1), (128, v.size))
    reps = np.ascontiguousarray(np.concatenate([rep(p["o_dt_bias"][0]), rep(p["o_a_log"][0]), rep(p["o_d_skip"][0])], axis=1)).astype(np.float32)
    return cols, reps


class L1(L0):
    def __init__(self, k, T, xin, xout, out):
        self.k, self.T, self.TT = k, T, T // 128
        self.xin, self.xout, self.out = xin, xout, out
        self.w_in = k.dram("o_w_in", [D, SSD_IN])
        self.w_out = k.dram("o_w_out", [4096, D])
        self.cols_d = k.dram("o_cols", [128, 288])
        self.reps_d = k.dram("o_reps", [128, 192])
        self.frep_d = k.dram("final_rep", [128, D])
        self.const_d = k.dram("consts1", [128, 512])
        self.NSLOT = 64
        self.scr = k.dram("wscr1", [self.NSLOT, 128, 4096], BF16, kind="Internal")

    def alloc(self):
        k, T, TT, s = self.k, self.T, self.TT, self
        s.cols = k.sb("cols1", [128, 288])
        s.colsh = k.sb("colsh1", [128, 288])
        s.mhalf = k.sb("mhalf1", [128, 2])
        s.SW = 256
        s.reps = k.sb("reps1", [128, 192])
        s.arep = k.sb("arep", [128, 64])
        s.frep = k.sb("frep", [128, D])
        s.cst = k.sb("cst1", [128, 512])
        s.cstb = k.sb("cstb1", [128, 128], BF16)
        s.ones = k.sb("ones1", [128, 128])
        s.tail = k.sb("tail1", [128, 48, 3])
        s.S = k.sb("S", [128, 8, 512])
        s.Sb = k.sb("Sb", [128, 8, 512], BF16)
        s.xt = [k.sb("xt1%d" % i, [128, D]) for i in range(2)]
        s.st4 = k.sb("st41", [128, 8])
        s.xnT = k.sb("xnT1", [128, NCH, T], BF16)
        s.yT = k.sb("yT1", [128, 16, T], BF16)
        s.slab = [k.sb("slab1%d" % i, [128, 16, 256], BF16) for i in range(5)]
        s.nslab = 0
        s.xe = [k.sb("xe1%d" % i, [128, T + 3]) for i in range(2)]
        s.tsets = []
        for i in range(2):
            ts = TS()
            ts.xc = k.sb("xc1%d" % i, [128, T])
            ts.th = k.sb("th1%d" % i, [128, T])
            s.tsets.append(ts)
        s.xbcT = k.sb("xbcT", [128, 48, T], BF16)
        s.xtok = k.sb("xtok", [128, TT, 8, 512], BF16)
        s.btok = k.sb("btok", [128, TT, 8, 128], BF16)
        s.dt = k.sb("dt", [128, TT, 64])
        s.t1 = k.sb("t1", [128, 64])
        s.t2 = k.sb("t2", [128, 64])
        s.dA = k.sb("dA", [128, TT, 64])
        s.Acs = k.sb("Acs", [128, TT, 64])
        s.bcol = k.sb("bcol", [128, TT, 64])
        s.dcol = k.sb("dcol", [128, TT, 64])
        s.wcol = k.sb("wcol", [128, TT, 64])
        s.crep = k.sb("crep", [128, TT, 64])
        s.wsets = []
        for i in range(2):
            W = TS()
            W.Z = k.sb("Zb%d" % i, [128, 8, 128])
            W.Lp = k.sb("Lpb%d" % i, [128, 8, 128])
            W.MT = k.sb("MTb%d" % i, [128, 8, 128], BF16)
            W.cbS = k.sb("cbS%d" % i, [128, 128])
            W.ysb = k.sb("ysb%d" % i, [128, 512])
            W.y2 = k.sb("y2%d" % i, [128, 512])
            W.szg = k.sb("szg%d" % i, [128, 512], BF16)
            W.xw = k.sb("xw%d" % i, [128, 512], BF16)
            W.ytok = k.sb("ytokL%d" % i, [128, 512], BF16)
            W.sm = k.sb("smL%d" % i, [128, 8])
            s.wsets.append(W)
        s.psA = [k.ps("qsA%d" % i, [128, 512]) for i in range(2)]
        s.npsA = 0
        s.psT = [k.ps("qsT%d" % i, [128, 1024], BF16) for i in range(2)]
        s.npsT = 0
        s.stage = [(s.xbcT.t[:, 16 * i:16 * i + 16, :].rearrange("p c t -> p (c t)").bitcast(F32), [(s.xbcT, ci) for ci in range(16 * i, 16 * i + 16)]) for i in range(2)]
        s.ipool = ([0, 1, 2], [0])
        s.opool = ([3, 4], [0])
        for i in range(2):
            W = s.wsets[i]
            W.w = s.psA[i]
            W.T = s.psT[i]
            W.L = k.ps("qsL%d" % i, [128, 512])
            W.Y = k.ps("qsY%d" % i, [128, 512])

    def setup(self):
        k, s = self.k, self
        nc = k.nc
        V, A = nc.vector, nc.scalar
        k.dma("sp", s.cols[:, :], s.cols_d[:, :], writes=[s.cols])
        k.dma("sp", s.reps[:, :], s.reps_d[:, :], writes=[s.reps])
        k.dma("sp", s.frep[:, :], s.frep_d[:, :], writes=[s.frep])
        k.dma("sp", s.cst[:, :], s.const_d[:, :], writes=[s.cst])
        k.dma("pool", s.cstb[:, :], s.const_d[:, 0:128], writes=[s.cstb])
        k.op("dve", lambda: V.memset(s.ones[:, :], 1.0), writes=[s.ones])
        k.op("dve", lambda: V.memset(s.mhalf[:, 0:1], -0.5), writes=[s.mhalf])
        k.op("dve", lambda: V.memset(s.mhalf[:, 1:2], 0.5), reads=[s.mhalf], writes=[s.mhalf])
        k.op("dve", lambda: V.tensor_scalar(out=s.colsh[:, :], in0=s.cols[:, :], scalar1=0.5, scalar2=None, op0=ALU.mult), reads=[s.cols], writes=[s.colsh])
        for b in [s.tail, s.S, s.Sb]:
            k.op("dve", (lambda b=b: V.memset(b.t[:], 0.0)), writes=[b])
        k.op("act", lambda: A.activation(out=s.arep[:, :], in_=s.reps[:, 64:128], func=AF.Exp), reads=[s.reps], writes=[s.arep])
        k.op("dve", lambda: V.tensor_scalar(out=s.arep[:, :], in0=s.arep[:, :], scalar1=-1.0, scalar2=None, op0=ALU.mult), reads=[s.arep], writes=[s.arep])

    def ssd_in(self, b):
        for _ in self.ssd_in_g(b):
            pass

    def ssd_in_g(self, b, slabpool=None):
        k, s, T, TT = self.k, self, self.T, self.TT
        nc = k.nc
        V, A, P = nc.vector, nc.scalar, nc.tensor
        wv = s.w_in.t.rearrange("(c p) f -> p c f", p=128)
        def tiles():
            for g in range(24):
                sl = s.load_slab(wv[:, :, 4096 + g * 256: 4096 + (g + 1) * 256], pool=slabpool)
                for j in range(2):
                    ci = g * 2 + j
                    yield s.conv_tile(ci, sl, j, s.tsets[ci % 2], s.tail, 16, 208, s.xbcT, xe_act=XE_ACT)
        yield from pipe_gen(tiles(), depth=2, skew=SIN_SKEW)
        sl = s.load_slab(wv[:, :, 10240:10304], pool=slabpool)
        tri = s.cst[:, 128:256]
        for tt in range(TT):
            ps = s.nextA()
            for c in range(NCH):
                k.op("pe", lambda c=c, tt=tt, ps=ps: P.matmul(ps[:, 0:64], lhsT=s.xnT[:, c, tt * 128:(tt + 1) * 128], rhs=sl[:, c, 0:64], start=(c == 0), stop=(c == NCH - 1)),
                     reads=[sl, (s.xnT, c)], writes=[ps], signal=(c == NCH - 1))
            k.op("dve", lambda ps=ps: V.tensor_tensor(out=s.t1[:, :], in0=ps[:, 0:64], in1=s.reps[:, 0:64], op=ALU.add), reads=[ps, s.reps], writes=[s.t1])
            k.op("dve", lambda: V.tensor_scalar(out=s.t2[:, :], in0=s.t1[:, :], scalar1=-1.0, scalar2=None, op0=ALU.mult), reads=[s.t1], writes=[s.t2])
            k.op("dve", lambda: V.tensor_tensor(out=s.t2[:, :], in0=s.t2[:, :], in1=s.t1[:, :], op=ALU.min), reads=[s.t1, s.t2], writes=[s.t2])
            k.op("act", lambda: A.activation(out=s.t2[:, :], in_=s.t2[:, :], func=AF.Exp), reads=[s.t2], writes=[s.t2])
            k.op("act", lambda: A.activation(out=s.t2[:, :], in_=s.t2[:, :], func=AF.Ln, bias=1.0), reads=[s.t2], writes=[s.t2])
            k.op("dve", lambda: V.tensor_scalar(out=s.t1[:, :], in0=s.t1[:, :], scalar1=0.0, scalar2=None, op0=ALU.max), reads=[s.t1], writes=[s.t1])
            k.op("dve", lambda tt=tt: V.tensor_tensor(out=s.dt[:, tt, :], in0=s.t1[:, :], in1=s.t2[:, :], op=ALU.add), reads=[s.t1, s.t2], writes=[s.dt])
            k.op("dve", lambda tt=tt: V.tensor_tensor(out=s.dA[:, tt, :], in0=s.dt[:, tt, :], in1=s.arep[:, :], op=ALU.mult), reads=[s.dt, s.arep], writes=[s.dA])
            psc = s.nextA()
            k.op("pe", lambda tt=tt, psc=psc: P.matmul(psc[:, 0:64], lhsT=tri, rhs=s.dA[:, tt, :], start=True, stop=True), reads=[s.cst, s.dA], writes=[psc])
            k.op("act", lambda tt=tt, psc=psc: A.copy(out=s.Acs[:, tt, :], in_=psc[:, 0:64]), reads=[psc], writes=[s.Acs])
            pse = s.nextA()
            k.op("pe", lambda tt=tt, pse=pse: P.matmul(pse[:, 0:64], lhsT=s.ones[:, :], rhs=s.dA[:, tt, :], start=True, stop=True), reads=[s.ones, s.dA], writes=[pse])
            k.op("act", lambda tt=tt, pse=pse: A.activation(out=s.crep[:, tt, :], in_=pse[:, 0:64], func=AF.Exp), reads=[pse], writes=[s.crep])
            k.op("act", lambda tt=tt: A.activation(out=s.t1[:, :], in_=s.dt[:, tt, :], func=AF.Ln), reads=[s.dt], writes=[s.t1])
            k.op("dve", lambda tt=tt: V.tensor_tensor(out=s.bcol[:, tt, :], in0=s.t1[:, :], in1=s.Acs[:, tt, :], op=ALU.subtract), reads=[s.t1, s.Acs], writes=[s.bcol])
            k.op("act", lambda tt=tt: A.activation(out=s.dcol[:, tt, :], in_=s.Acs[:, tt, :], func=AF.Exp), reads=[s.Acs], writes=[s.dcol])
            k.op("dve", lambda tt=tt, pse=pse: V.tensor_tensor(out=s.t2[:, :], in0=pse[:, 0:64], in1=s.bcol[:, tt, :], op=ALU.add), reads=[pse, s.bcol], writes=[s.t2])
            k.op("act", lambda tt=tt: A.activation(out=s.wcol[:, tt, :], in_=s.t2[:, :], func=AF.Exp), reads=[s.t2], writes=[s.wcol])
        yield
        identb = s.cstb[:, 0:128]
        for tt in range(TT):
            yield
            for g in range(8):
                i = s.nextT()
                for j in range(4):
                    k.op("pe", lambda tt=tt, g=g, j=j, i=i: P.transpose(out=s.psT[i][:, j * 128:(j + 1) * 128], in_=s.xbcT[:, 4 * g + j, tt * 128:(tt + 1) * 128], identity=identb),
                         reads=[(s.xbcT, 4 * g + j), s.cstb], writes=[s.psT[i]], signal=(j == 3))
                k.op("dve", lambda tt=tt, g=g, i=i: V.tensor_copy(out=s.xtok[:, tt, g, :], in_=s.psT[i][:, 0:512]), reads=[s.psT[i]], writes=[(s.xtok, (tt, g))])
            for g2 in range(2):
                i = s.nextT()
                for j in range(4):
                    k.op("pe", lambda tt=tt, g2=g2, j=j, i=i: P.transpose(out=s.psT[i][:, j * 128:(j + 1) * 128], in_=s.xbcT[:, 32 + 4 * g2 + j, tt * 128:(tt + 1) * 128], identity=identb),
                         reads=[(s.xbcT, 32 + 4 * g2 + j), s.cstb], writes=[s.psT[i]], signal=(j == 3))
                k.op("dve", lambda tt=tt, g2=g2, i=i: V.tensor_copy(out=s.btok[:, tt, 4 * g2:4 * g2 + 4, :].rearrange("p a n -> p (a n)"), in_=s.psT[i][:, 0:512]), reads=[s.psT[i]], writes=[s.btok])

    def ssd_stream(self, g, gl, zs, W):
        k, s, T, TT = self.k, self, self.T, self.TT
        nc = k.nc
        V, A, P, G = nc.vector, nc.scalar, nc.tensor, nc.gpsimd
        identb = s.cstb[:, 0:128]
        ident = s.cst[:, 0:128]
        negmaskT = s.cst[:, 384:512]
        bc3 = lambda ap, shape, ax: ap.unsqueeze(ax).to_broadcast(shape)
        v3 = lambda ap: ap.rearrange("p (e j) -> p e j", j=64)
        g8 = slice(g * 8, g * 8 + 8)
        for c in range(TT):
            cs = slice(c * 128, (c + 1) * 128)
            xg = s.xtok[:, c, g, :]
            k.op("pe", lambda: P.matmul(W.w[:, 0:128], lhsT=s.xbcT[:, 32 + g, cs], rhs=s.xbcT[:, 40 + g, cs], start=True, stop=True),
                 reads=[(s.xbcT, 32 + g), (s.xbcT, 40 + g)], writes=[W.w])
            k.op("dve", lambda: V.tensor_tensor(out=W.Z[:, :, :], in0=bc3(negmaskT, [128, 8, 128], 1), in1=bc3(s.Acs[:, c, g8], [128, 8, 128], 2), op=ALU.add),
                 reads=[s.cst, s.Acs], writes=[W.Z])
            yield
            k.op("dve", lambda: V.tensor_copy(out=W.cbS[:, :], in_=W.w[:, 0:128]), reads=[W.w], writes=[W.cbS])
            for hf in range(2):
                for e4 in range(4):
                    e = hf * 4 + e4
                    r = slice(e4 * 128, (e4 + 1) * 128)
                    k.op("pe", lambda e=e, r=r: P.transpose(out=W.L[:, r], in_=W.Z[:, e, :], identity=ident), reads=[W.Z, s.cst], writes=[W.L], signal=(e4 == 3))
                yield
                for e4 in range(4):
                    e = hf * 4 + e4
                    hh = g * 8 + e
                    r = slice(e4 * 128, (e4 + 1) * 128)
                    k.op("act", lambda e=e, r=r, hh=hh: A.activation(out=W.Lp[:, e, :], in_=W.L[:, r], func=AF.Exp, bias=s.bcol[:, c, hh:hh + 1]), reads=[W.L, s.bcol], writes=[(W.Lp, hf)])
                yield
                k.op("dve", lambda hf=hf: V.tensor_tensor(out=W.MT[:, hf * 4:(hf + 1) * 4, :], in0=W.Lp[:, hf * 4:(hf + 1) * 4, :], in1=bc3(W.cbS[:, :], [128, 4, 128], 1), op=ALU.mult),
                     reads=[(W.Lp, hf), W.cbS], writes=[(W.MT, hf)])
                yield
                for e4 in range(4):
                    e = hf * 4 + e4
                    k.op("pe", lambda e=e, hf=hf: P.matmul(W.Y[:, e * 64:(e + 1) * 64], lhsT=W.MT[:, e, :], rhs=s.xtok[:, c, g, e * 64:(e + 1) * 64], start=True, stop=True),
                         reads=[(W.MT, hf), (s.xtok, (c, g))], writes=[W.Y], signal=(e == 7))
            for hz in range(2):
                for kk in range(NCH):
                    k.op("pe", lambda kk=kk, hz=hz: P.matmul(W.w[:, hz * 256:(hz + 1) * 256], lhsT=s.xnT[:, kk, cs], rhs=zs[hz][:, kk, 0:256], start=(kk == 0), stop=(kk == NCH - 1)),
                         reads=[zs[hz], (s.xnT, kk)], writes=[W.w], signal=(kk == NCH - 1 and hz == 1))
            k.op("pool", lambda: G.tensor_tensor(out=v3(W.y2[:, :]), in0=v3(xg), in1=bc3(s.reps[:, 128 + g * 8:136 + g * 8], [128, 8, 64], 2), op=ALU.mult),
                 reads=[(s.xtok, (c, g)), s.reps], writes=[W.y2])
            yield
            k.op("act", lambda: A.activation(out=W.szg[:, :], in_=W.w[:, 0:512], func=AF.Tanh, scale=0.5), reads=[W.w], writes=[W.szg])
            yield
            k.op("dve", lambda: V.scalar_tensor_tensor(out=W.szg[:, :], in0=W.szg[:, :], scalar=1.0, in1=W.w[:, 0:512], op0=ALU.add, op1=ALU.mult), reads=[W.szg, W.w], writes=[W.szg])
            k.op("pe", lambda: P.matmul(W.w[:, 0:512], lhsT=s.xbcT[:, 40 + g, cs], rhs=s.Sb[:, g, :], start=True, stop=True), reads=[(s.xbcT, 40 + g), (s.Sb, g)], writes=[W.w])
            yield
            k.op("dve", lambda: V.tensor_tensor(out=v3(W.ysb[:, :]), in0=v3(W.w[:, 0:512]), in1=bc3(s.dcol[:, c, g8], [128, 8, 64], 2), op=ALU.mult),
                 reads=[W.w, s.dcol], writes=[W.ysb])
            k.op("dve", lambda: V.tensor_tensor(out=W.ysb[:, :], in0=W.ysb[:, :], in1=W.y2[:, :], op=ALU.add), reads=[W.ysb, W.y2], writes=[W.ysb])
            yield
            k.op("dve", lambda: V.tensor_tensor(out=W.ysb[:, :], in0=W.ysb[:, :], in1=W.Y[:, 0:512], op=ALU.add), reads=[W.ysb, W.Y], writes=[W.ysb])
            k.op("dve", lambda: V.scalar_tensor_tensor(out=W.y2[:, :], in0=W.ysb[:, :], scalar=0.5, in1=W.szg[:, :], op0=ALU.mult, op1=ALU.mult), reads=[W.ysb, W.szg], writes=[W.y2])
            k.op("pool", lambda: G.tensor_tensor(out=v3(W.xw[:, :]), in0=v3(xg), in1=bc3(s.wcol[:, c, g8], [128, 8, 64], 2), op=ALU.mult),
                 reads=[(s.xtok, (c, g)), s.wcol], writes=[W.xw])
            yield
            k.op("act", lambda: A.activation(out=W.ysb[:, :], in_=W.y2[:, :], func=AF.Square, accum_out=W.sm[:, 0:1]), reads=[W.y2], writes=[W.ysb, W.sm])
            k.op("pe", lambda: P.matmul(W.w[:, 0:512], lhsT=s.btok[:, c, g, :], rhs=W.xw[:, :], start=True, stop=True), reads=[s.btok, W.xw], writes=[W.w])
            k.op("pool", lambda: G.tensor_tensor(out=v3(s.S[:, g, :]), in0=v3(s.S[:, g, :]), in1=bc3(s.crep[:, c, g8], [128, 8, 64], 2), op=ALU.mult),
                 reads=[(s.S, g), s.crep], writes=[(s.S, g)])
            yield
            s.rstd(W.sm, W.sm[:, 0:1], 512)
            yield
            k.op("dve", lambda: V.tensor_scalar(out=W.ytok[:, :], in0=W.y2[:, :], scalar1=W.sm[:, 0:1], scalar2=None, op0=ALU.mult), reads=[W.y2, W.sm], writes=[W.ytok])
            k.op("dve", lambda: V.tensor_tensor(out=s.S[:, g, :], in0=s.S[:, g, :], in1=W.w[:, 0:512], op=ALU.add), reads=[(s.S, g), W.w], writes=[(s.S, g)])
            yield
            for vt in range(4):
                k.op("pe", lambda vt=vt: P.transpose(out=W.T[:, vt * 128:(vt + 1) * 128], in_=W.ytok[:, vt * 128:(vt + 1) * 128], identity=identb),
                     reads=[W.ytok, s.cstb], writes=[W.T], signal=(vt == 3))
            k.op("act", lambda: A.copy(out=s.Sb[:, g, :], in_=s.S[:, g, :]), reads=[(s.S, g)], writes=[(s.Sb, g)])
            yield
            for vt in range(4):
                ft = 4 * gl + vt
                gcol = 256 + 4 * g + vt
                k.op("dve", lambda vt=vt, ft=ft, gcol=gcol: V.tensor_scalar(out=s.yT[:, ft, cs], in0=W.T[:, vt * 128:(vt + 1) * 128], scalar1=s.cols[:, gcol:gcol + 1], scalar2=None, op0=ALU.mult),
                     reads=[W.T, s.cols], writes=[(s.yT, ft)])
            yield

    def ssd_core(self, b, half):
        s = self
        wv = s.w_in.t.rearrange("(c p) f -> p c f", p=128)
        for pair in range(2):
            gens = []
            for i in range(2):
                gl = pair * 2 + i
                g = half * 4 + gl
                zs = [s.load_slab(wv[:, :, g * 512 + hz * 256: g * 512 + (hz + 1) * 256]) for hz in range(2)]
                gens.append(s.ssd_stream(g, gl, zs, s.wsets[i]))
            run_pipe(gens, depth=2, skew=SSD_SKEW)

    def final(self, b):
        k, s, T, TT = self.k, self, self.T, self.TT
        nc = k.nc
        V, A = nc.vector, nc.scalar
        for tt in range(TT):
            xt = s.xt[tt]
            r0 = b * T + tt * 128
            k.op("act", lambda xt=xt, tt=tt: A.activation(out=s.xnb_all[tt][:, :], in_=xt[:, :], func=AF.Square, accum_out=s.st4[:, tt:tt + 1]), reads=[xt], writes=[s.st4, s.xnb_all[tt]])
            s.rstd(s.st4, s.st4[:, tt:tt + 1], D)
            k.op("dve", lambda xt=xt, tt=tt: V.scalar_tensor_tensor(out=xt[:, :], in0=xt[:, :], scalar=s.st4[:, tt:tt + 1], in1=s.frep[:, :], op0=ALU.mult, op1=ALU.mult), reads=[xt, s.st4, s.frep], writes=[xt])
            k.dma("pool", s.out[r0:r0 + 128, :], xt[:, :], reads=[xt], writes=[(s.out, r0)])


from concourse.bass_utils import run_bass_kernel_spmd

T_BLK = 256
NMIX = 2
L1_CUT = None


def build(S):
    k = K()
    x = k.dram("x", [S, D])
    x1 = k.dram("x1_scratch", [S, D], kind="Internal")
    x2 = k.dram("x2_scratch", [S, D], kind="Internal")
    out = k.dram("out", [S, D], kind="ExternalOutput")
    nb = S // T_BLK
    k.les = contextlib.ExitStack()
    A0 = L0(k, T_BLK, x, x1)
    A0.alloc()
    A0.setup()
    for b in range(nb):
        A0.rmsnorm_T(b)
        A0.mlstm_in(b)
        A0.lru_mlstm(b)
        A0.outproj(b, 0)
        A0.outproj(b, 1, src=A0.xmcT)
    k.barrier()
    k.les.close()
    k.les = contextlib.ExitStack()
    A1 = L1(k, T_BLK, x1, x2, out)
    A1.alloc()
    A1.setup()
    k.cut = L1_CUT
    def chain(*gs):
        for g in gs:
            yield from g
    obanks = [A1.wsets[0].L, A1.wsets[1].L, A1.wsets[0].Y, A1.wsets[1].Y]
    for _ in A1.rmsnorm_g(0, A1.stage):
        pass
    A1.load_resid(0)
    A1.ssd_in(0)
    for b in range(nb):
        A1.ssd_core(b, 0)
        if b > 0:
            A1.load_resid(b)
        A1.outproj(b, 0)
        A1.ssd_core(b, 1)
        if b + 1 < nb:
            mix(A1.outproj_g(b, 1, store=False, banks=obanks, slabpool=A1.opool),
                chain(A1.rmsnorm_g(b + 1, A1.stage), A1.ssd_in_g(b + 1, slabpool=A1.ipool)), NMIX)
        else:
            A1.outproj(b, 1, store=False)
        A1.final(b)
    k.cut = None
    k.finish([out])
    k.barrier()
    k.les.close()
    return k


def make_maps(p, xs):
    cols1, reps1 = host_cols_l1(p)
    base = {"e_w_in": p["e_w_in"][0], "e_w_out": p["e_w_out"][0], "e_lru_wa": p["e_lru_wa"][0], "e_lru_wx": p["e_lru_wx"][0],
            "e_w_q": p["e_w_q"][0], "e_w_k": p["e_w_k"][0], "e_w_v": p["e_w_v"][0], "e_w_o": p["e_w_o"][0], "e_w_if": p["e_w_if"][0],
            "e_cols": host_cols_l0(p), "e_bif": np.ascontiguousarray(p["e_b_if"][0].reshape(2, 4).T),
            "e_mncol": np.ascontiguousarray(p["e_m_norm"][0].reshape(16, 128).T), "consts": consts_host(), "consts1": consts_host(),
            "o_w_in": p["o_w_in"][0], "o_w_out": p["o_w_out"][0], "o_cols": cols1, "o_reps": reps1,
            "final_rep": np.ascontiguousarray(np.broadcast_to(p["final_norm"].reshape(1, D), (128, D)))}
    base = {kk: np.ascontiguousarray(np.asarray(v, dtype=np.float32)) for kk, v in base.items()}
    return [dict(base, x=np.ascontiguousarray(x_)) for x_ in xs]


def kernel(**inputs):
    p = {kk: np.asarray(v) for kk, v in inputs.items()}
    x = p["x"]
    B, S, _ = x.shape
    k = build(S)
    maps = make_maps(p, [x[c % B] for c in range(8)])
    res = run_bass_kernel_spmd(k.nc, maps, core_ids=list(range(8)))
    return np.stack([res.results[b]["out"] for b in range(B)], axis=0).astype(np.float32)
```
